# Optimizing a Trainium2 kernel written in Bass

```python
import jax, jax.numpy as jnp
from jax import lax
import numpy as np

D_MODEL = 1024
BATCH = 4
SEQ = 8192
DEPTH = 2

GRID_W = 64
CTX_LEN = 256
ROPE_THETA = 10000.0
Q_BLOCK = 128
EPS = 1e-6

A_HEADS = 8
A_NOPE = 64
A_ROPE = 32
A_V = 64
A_Q_RANK = 384
A_KV_RANK = 256
A_WIDTH = A_HEADS * A_V
A_SCALE = (A_NOPE + A_ROPE) ** -0.5

B_HEADS = 8
B_KV_HEADS = 2
B_GROUP = B_HEADS // B_KV_HEADS
B_HD = 64
B_WIDTH = B_HEADS * B_HD
B_SCALE = B_HD ** -0.5

EVEN_WIDTH = A_WIDTH + B_WIDTH
EVEN_SPLITS = [A_Q_RANK, A_KV_RANK, A_ROPE, A_WIDTH,
               B_HEADS * B_HD, B_KV_HEADS * B_HD, B_KV_HEADS * B_HD, B_WIDTH]
EVEN_IN = sum(EVEN_SPLITS)

C_CHUNK = 128
C_GROUPS = 8
C_WIDTH = 1024
C_GROUP_W = C_WIDTH // C_GROUPS
ODD_IN = 3 * C_WIDTH

N_EVEN = (DEPTH + 1) // 2
N_ODD = DEPTH // 2
DEEPNORM_ALPHA = (2 * DEPTH) ** 0.25
DEEPNORM_BETA = (8 * DEPTH) ** -0.25

kernel_name = 'hybrid_mla_gqa_gmlp_prefix_dit'


def rms_norm(x, g):
    xf = x.astype(jnp.float32)
    y = xf * lax.rsqrt(jnp.mean(xf * xf, axis=-1, keepdims=True) + EPS)
    return (y * g.astype(jnp.float32)).astype(x.dtype)


def layer_norm(x, g, b):
    xf = x.astype(jnp.float32)
    mu = jnp.mean(xf, axis=-1, keepdims=True)
    var = jnp.mean(jnp.square(xf - mu), axis=-1, keepdims=True)
    y = (xf - mu) * lax.rsqrt(var + EPS)
    return (y * g.astype(jnp.float32) + b.astype(jnp.float32)).astype(x.dtype)


def modulation(cond, w_mod, b_mod):
    m = jax.nn.silu(cond) @ w_mod + b_mod
    return jnp.split(m, 3, axis=-1)


def modulate(x, shift, scale):
    return x * (1.0 + scale) + shift


def axial_rope_angles(n_tok, d_rot):
    rows = n_tok // GRID_W
    row = jnp.repeat(jnp.arange(rows, dtype=jnp.float32), GRID_W)
    col = jnp.tile(jnp.arange(GRID_W, dtype=jnp.float32), rows)
    d_axis = d_rot // 2
    inv_freq = ROPE_THETA ** (-jnp.arange(0, d_axis, 2, dtype=jnp.float32) / d_axis)
    ang = jnp.stack([row[:, None] * inv_freq, col[:, None] * inv_freq], axis=1)
    return (jnp.cos(ang), jnp.sin(ang))


def apply_axial_rope(x, cos, sin):
    b, s, h, d = x.shape
    xa = x.reshape(b, s, h, 2, d // 2)
    x1, x2 = jnp.split(xa, 2, axis=-1)
    cs = cos[None, :, None].astype(x.dtype)
    sn = sin[None, :, None].astype(x.dtype)
    out = jnp.concatenate([x1 * cs - x2 * sn, x2 * cs + x1 * sn], axis=-1)
    return out.reshape(b, s, h, d)


def block_attention(q, k, v, scale):
    b, s, hk, g, dk = q.shape
    nb = s // Q_BLOCK
    qb = jnp.moveaxis(q.reshape(b, nb, Q_BLOCK, hk, g, dk), 1, 0)

    def one_block(qi):
        sc = jnp.einsum('bqhgd,bthd->bhgqt', qi, k).astype(jnp.float32) * scale
        p = jax.nn.softmax(sc, axis=-1).astype(v.dtype)
        return jnp.einsum('bhgqt,bthe->bqhge', p, v)

    out = lax.map(one_block, qb)
    return jnp.moveaxis(out, 0, 1).reshape(b, s, hk, g, v.shape[-1])


def even_heads(h, a_q_norm, a_kv_norm, a_w_uq, a_w_ukv, b_q_norm, b_k_norm, rope):
    bsz, n, _ = h.shape
    idx = [int(i) for i in np.cumsum(EVEN_SPLITS)[:-1]]
    cq, ckv, kr, g_a, qb, kb, vb, g_b = jnp.split(h, idx, axis=-1)
    q_a = (rms_norm(cq, a_q_norm) @ a_w_uq).reshape(bsz, n, A_HEADS, A_NOPE + A_ROPE)
    kv_a = (rms_norm(ckv, a_kv_norm) @ a_w_ukv).reshape(bsz, n, A_HEADS, A_NOPE + A_V)
    q_nope, q_pe = q_a[..., :A_NOPE], q_a[..., A_NOPE:]
    k_nope, v_a = kv_a[..., :A_NOPE], kv_a[..., A_NOPE:]
    k_pe = kr.reshape(bsz, n, 1, A_ROPE)
    q_b = rms_norm(qb.reshape(bsz, n, B_HEADS, B_HD), b_q_norm)
    k_b = rms_norm(kb.reshape(bsz, n, B_KV_HEADS, B_HD), b_k_norm)
    v_b = vb.reshape(bsz, n, B_KV_HEADS, B_HD)
    if rope is not None:
        cos_a, sin_a, cos_b, sin_b = rope
        q_pe = apply_axial_rope(q_pe, cos_a, sin_a)
        k_pe = apply_axial_rope(k_pe, cos_a, sin_a)
        q_b = apply_axial_rope(q_b, cos_b, sin_b)
        k_b = apply_axial_rope(k_b, cos_b, sin_b)
    q_a = jnp.concatenate([q_nope, q_pe], axis=-1)
    k_a = jnp.concatenate([k_nope, jnp.broadcast_to(k_pe, (bsz, n, A_HEADS, A_ROPE))], axis=-1)
    return (q_a, k_a, v_a, g_a, q_b, k_b, v_b, g_b)


def attend_and_merge(q_a, k_a, v_a, q_b, k_b, v_b, g_a, g_b, w_out):
    bsz, n = q_a.shape[:2]
    o_a = block_attention(q_a[:, :, :, None], k_a, v_a, A_SCALE).reshape(bsz, n, A_WIDTH)
    o_b = block_attention(q_b.reshape(bsz, n, B_KV_HEADS, B_GROUP, B_HD), k_b, v_b,
                          B_SCALE).reshape(bsz, n, B_WIDTH)
    merged = jnp.concatenate([o_a * jax.nn.silu(g_a), o_b * jax.nn.silu(g_b)], axis=-1)
    return merged @ w_out


def even_layer(x, xc, c, c_ctx, w_mod, b_mod, w_in, a_q_norm, a_kv_norm, a_w_uq, a_w_ukv,
               b_q_norm, b_k_norm, w_out, ln_g, ln_b, update_ctx):
    n = x.shape[1]
    rope = axial_rope_angles(n, A_ROPE) + axial_rope_angles(n, B_HD)
    shift, scale, gate = modulation(c[:, None, :], w_mod, b_mod)
    shift_c, scale_c, gate_c = modulation(c_ctx[None, :], w_mod, b_mod)
    q_a, k_a, v_a, g_a, q_b, k_b, v_b, g_b = even_heads(
        modulate(x, shift, scale) @ w_in, a_q_norm, a_kv_norm, a_w_uq, a_w_ukv, b_q_norm, b_k_norm, rope)
    q_ac, k_ac, v_ac, g_ac, q_bc, k_bc, v_bc, g_bc = even_heads(
        modulate(xc, shift_c, scale_c) @ w_in, a_q_norm, a_kv_norm, a_w_uq, a_w_ukv, b_q_norm, b_k_norm, None)
    y = attend_and_merge(q_a, jnp.concatenate([k_ac, k_a], axis=1), jnp.concatenate([v_ac, v_a], axis=1),
                         q_b, jnp.concatenate([k_bc, k_b], axis=1), jnp.concatenate([v_bc, v_b], axis=1),
                         g_a, g_b, w_out)
    x_new = layer_norm(DEEPNORM_ALPHA * x + gate * y, ln_g, ln_b)
    if update_ctx:
        yc = attend_and_merge(q_ac, k_ac, v_ac, q_bc, k_bc, v_bc, g_ac, g_bc, w_out)
        xc = layer_norm(DEEPNORM_ALPHA * xc + gate_c * yc, ln_g, ln_b)
    return x_new, xc


def chunk_gmlp(h, v_ln_g, v_ln_b, w_s, b_s):
    bsz, n, _ = h.shape
    u, v, g = jnp.split(h, 3, axis=-1)
    u = jax.nn.gelu(u)
    v = layer_norm(jax.nn.gelu(v), v_ln_g, v_ln_b)
    nc = n // C_CHUNK
    vv = v.reshape(bsz, nc, C_CHUNK, C_GROUPS, C_GROUP_W)
    mixed = jnp.einsum('gpq,bnqgc->bnpgc', w_s, vv) + b_s.T[None, None, :, :, None]
    return u * mixed.reshape(bsz, n, C_WIDTH) * jax.nn.silu(g)


def odd_layer(x, xc, c, c_ctx, w_mod, b_mod, w_in, v_ln_g, v_ln_b, w_s, b_s, w_out,
              ln_g, ln_b, update_ctx):
    shift, scale, gate = modulation(c[:, None, :], w_mod, b_mod)
    y = chunk_gmlp(modulate(x, shift, scale) @ w_in, v_ln_g, v_ln_b, w_s, b_s) @ w_out
    x_new = layer_norm(DEEPNORM_ALPHA * x + gate * y, ln_g, ln_b)
    if update_ctx:
        shift_c, scale_c, gate_c = modulation(c_ctx[None, :], w_mod, b_mod)
        yc = chunk_gmlp(modulate(xc, shift_c, scale_c) @ w_in, v_ln_g, v_ln_b, w_s, b_s) @ w_out
        xc = layer_norm(DEEPNORM_ALPHA * xc + gate_c * yc, ln_g, ln_b)
    return x_new, xc


def setup_inputs(seed: int = 0) -> dict:
    key = jax.random.key(seed)
    ks = iter(jax.random.split(key, 32))
    D = D_MODEL

    def nrm(shape, scale):
        return jax.random.normal(next(ks), shape, jnp.float32) * scale

    def gain(shape):
        return 1.0 + nrm(shape, 0.05)

    return {
        'x': nrm((BATCH, SEQ, D), 1.0),
        'c': nrm((BATCH, D), 1.0),
        'ctx': nrm((BATCH, CTX_LEN, D), 1.0),
        'c_ctx': nrm((D,), 1.0),
        'e_w_mod': nrm((N_EVEN, D, 3 * D), D ** -0.5),
        'e_b_mod': nrm((N_EVEN, 3 * D), 0.02),
        'e_w_in': nrm((N_EVEN, D, EVEN_IN), D ** -0.5),
        'e_a_q_norm': gain((N_EVEN, A_Q_RANK)),
        'e_a_kv_norm': gain((N_EVEN, A_KV_RANK)),
        'e_a_w_uq': nrm((N_EVEN, A_Q_RANK, A_HEADS * (A_NOPE + A_ROPE)), A_Q_RANK ** -0.5),
        'e_a_w_ukv': nrm((N_EVEN, A_KV_RANK, A_HEADS * (A_NOPE + A_V)), A_KV_RANK ** -0.5),
        'e_b_q_norm': gain((N_EVEN, B_HD)),
        'e_b_k_norm': gain((N_EVEN, B_HD)),
        'e_w_out': nrm((N_EVEN, EVEN_WIDTH, D), EVEN_WIDTH ** -0.5 * DEEPNORM_BETA),
        'e_ln_g': gain((N_EVEN, D)),
        'e_ln_b': nrm((N_EVEN, D), 0.02),
        'o_w_mod': nrm((N_ODD, D, 3 * D), D ** -0.5),
        'o_b_mod': nrm((N_ODD, 3 * D), 0.02),
        'o_w_in': nrm((N_ODD, D, ODD_IN), D ** -0.5),
        'o_v_ln_g': gain((N_ODD, C_WIDTH)),
        'o_v_ln_b': nrm((N_ODD, C_WIDTH), 0.02),
        'o_w_s': nrm((N_ODD, C_GROUPS, C_CHUNK, C_CHUNK), C_CHUNK ** -0.5),
        'o_b_s': 1.0 + nrm((N_ODD, C_GROUPS, C_CHUNK), 0.02),
        'o_w_out': nrm((N_ODD, C_WIDTH, D), C_WIDTH ** -0.5 * DEEPNORM_BETA),
        'o_ln_g': gain((N_ODD, D)),
        'o_ln_b': nrm((N_ODD, D), 0.02),
    }


def reference(x, c, ctx, c_ctx, e_w_mod, e_b_mod, e_w_in, e_a_q_norm, e_a_kv_norm, e_a_w_uq,
              e_a_w_ukv, e_b_q_norm, e_b_k_norm, e_w_out, e_ln_g, e_ln_b, o_w_mod, o_b_mod,
              o_w_in, o_v_ln_g, o_v_ln_b, o_w_s, o_b_s, o_w_out, o_ln_g, o_ln_b):
    xl, xc = x, ctx
    for i in range(DEPTH):
        update_ctx = any(j % 2 == 0 for j in range(i + 1, DEPTH))
        k = i // 2
        if i % 2 == 0:
            xl, xc = even_layer(xl, xc, c, c_ctx, e_w_mod[k], e_b_mod[k], e_w_in[k], e_a_q_norm[k],
                                e_a_kv_norm[k], e_a_w_uq[k], e_a_w_ukv[k], e_b_q_norm[k], e_b_k_norm[k],
                                e_w_out[k], e_ln_g[k], e_ln_b[k], update_ctx)
        else:
            xl, xc = odd_layer(xl, xc, c, c_ctx, o_w_mod[k], o_b_mod[k], o_w_in[k], o_v_ln_g[k],
                               o_v_ln_b[k], o_w_s[k], o_b_s[k], o_w_out[k], o_ln_g[k], o_ln_b[k],
                               update_ctx)
    return xl
```

```python
import os
import numpy as np
from contextlib import ExitStack
import concourse.bass as bass
import concourse.mybir as mybir
from concourse.bass_utils import run_bass_kernel_spmd

F32 = mybir.dt.float32
BF16 = mybir.dt.bfloat16
AF = mybir.ActivationFunctionType
ALU = mybir.AluOpType

S = 8192
SQ = 4096
D = 1024
CTX = 256
NK = S + CTX
NKT = NK // 128
EPS = 1e-6
ALPHA = 4.0 ** 0.25
A_SCALE = 96.0 ** -0.5
B_SCALE = 64.0 ** -0.5

O_CKV, O_KR, O_KRP, O_KB, O_KBP, O_VB, O_CQ, O_GA, O_QB, O_QBP, O_GB = (
    0, 256, 288, 320, 448, 576, 704, 1088, 1600, 2112, 2624)
NC0 = 3136


class Sem:
    def __init__(self, nc, es, name):
        self.h = es.enter_context(nc.semaphore(name))
        self.name = name
        self.count = 0


class Buf:
    def __init__(self):
        self.w = {}
        self.r = {}


def _merge(d, ev):
    s, v = ev
    if s.name not in d or d[s.name][1] < v:
        d[s.name] = (s, v)


class T(Buf):
    def __init__(self, t):
        Buf.__init__(self)
        self.t = t


class Eng:
    def __init__(self, name, sem):
        self.name = name
        self.sem = sem
        self.q = []
        self.seen = {}

    def wait(self, ev):
        s, v = ev
        if v <= 0 or self.seen.get(s.name, 0) >= v:
            return
        self.seen[s.name] = v
        self.q.append(("wait", s, v))


class KB:
    def __init__(self, nc, es):
        self.nc = nc
        self.es = es
        self.nsem = 0
        self.pe = Eng("pe", self.sem("pe"))
        self.act = Eng("act", self.sem("act"))
        self.dve = Eng("dve", self.sem("dve"))
        self.pool = Eng("pool", self.sem("pool"))
        self.sp = Eng("sp", None)
        self.dma_sems = []

    def sem(self, name):
        self.nsem += 1
        return Sem(self.nc, self.es, "%s_%d" % (name, self.nsem))

    def dsem(self, name):
        s = self.sem(name)
        self.dma_sems.append(s)
        return s

    def _deps(self, eng, reads, writes):
        for b in reads:
            for ev in list(b.w.values()):
                eng.wait(ev)
            if isinstance(b, PBank):
                for ev in list(b.r.values()):
                    if eng.sem is None or ev[0].name != eng.sem.name:
                        eng.wait(ev)
        for b in writes:
            for ev in list(b.w.values()) + list(b.r.values()):
                eng.wait(ev)

    def op(self, eng, fn, reads=(), writes=()):
        self._deps(eng, reads, writes)
        eng.sem.count += 1
        ev = (eng.sem, eng.sem.count)
        eng.q.append(("op", fn, eng.sem))
        for b in reads:
            _merge(b.r, ev)
        for b in writes:
            _merge(b.w, ev)
        return ev

    def group(self, eng, fns, reads=(), writes=()):
        self._deps(eng, reads, writes)
        for fn in fns[:-1]:
            eng.q.append(("op", fn, None))
        eng.sem.count += 1
        ev = (eng.sem, eng.sem.count)
        eng.q.append(("op", fns[-1], eng.sem))
        for b in reads:
            _merge(b.r, ev)
        for b in writes:
            _merge(b.w, ev)
        return ev

    def dma(self, sem, items, q=None):
        q = q or self.sp
        for (_, _, reads, writes) in items:
            self._deps(q, reads, writes)
        for (o, i, _, _) in items:
            sem.count += 16
            q.q.append(("dma", o, i, sem))
        ev = (sem, sem.count)
        for (_, _, reads, writes) in items:
            for b in reads:
                _merge(b.r, ev)
            for b in writes:
                _merge(b.w, ev)
        return ev

    def flush(self, final_waits=True):
        nc = self.nc
        if final_waits:
            for s in self.dma_sems:
                self.sp.wait((s, s.count))

        def replay(q, e):
            for it in q:
                if it[0] == "wait":
                    e.wait_ge(it[1].h, it[2])
                elif it[0] == "op":
                    ins = getattr(e, it[1][0])(**it[1][1])
                    if it[2] is not None:
                        ins.then_inc(it[2].h, 1)
                else:
                    e.dma_start(out=it[1], in_=it[2]).then_inc(it[3].h, 16)

        with nc.Block() as blk:
            @blk.sync
            def _(e):
                replay(self.sp.q, e)

            @blk.tensor
            def _(e):
                replay(self.pe.q, e)

            @blk.scalar
            def _(e):
                replay(self.act.q, e)

            @blk.vector
            def _(e):
                replay(self.dve.q, e)

            @blk.gpsimd
            def _(e):
                replay(self.pool.q, e)
        for e in (self.sp, self.pe, self.act, self.dve, self.pool):
            e.q = []


class Rot:
    def __init__(self, items):
        self.items = items
        self.i = 0

    def next(self):
        it = self.items[self.i % len(self.items)]
        self.i += 1
        return it


class PBank(Buf):
    def __init__(self, ap):
        Buf.__init__(self)
        self.ap = ap


def build(stop_after=99, dbg=False):
    nc = bass.Bass("TRN2", target_bir_lowering=False)
    es = ExitStack()
    with es:
        def din(name, shape, dt=F32):
            return nc.dram_tensor(name, list(shape), dt, kind="ExternalInput").ap()

        def dscr(name, shape, dt=BF16):
            kind = "ExternalOutput" if dbg else "Internal"
            return nc.dram_tensor(name, list(shape), dt, kind=kind).ap()

        x_all = din("x_all", [S, D])
        ctx = din("ctx", [CTX, D])
        cc_d = din("cc", [128, 8, 2])
        ident_d = din("ident", [128, 128])
        tabs = din("tabs", [4, 128, S])
        wmod_d = [din("wmod0", [D, 3 * D]), din("wmod1", [D, 3 * D])]
        bmodc_d = [din("bmodc0", [128, 24]), din("bmodc1", [128, 24])]
        bmodg_d = [din("bmodg0", [D]), din("bmodg1", [D])]
        w0_d = din("w0", [D, NC0])
        gv0_d = din("gv0", [NC0])
        wuq_d = din("wuq", [384, 1024])
        aqc_d = din("aqc", [128, 3])
        wukv_d = din("wukv", [256, 1024])
        akvc_d = din("akvc", [128, 2])
        bqk_d = din("bqk", [128, 2])
        wout0_d = din("wout0", [D, D])
        lng0_d = din("lng0", [D]); lnb0_d = din("lnb0", [D])
        w1in_d = din("w1in", [D, 3 * D])
        vlng_d = din("vlng", [D]); vlnb_d = din("vlnb", [D])
        wsT_d = din("wsT", [8, 128, 128])
        bs_d = din("bs", [D])
        wout1_d = din("wout1", [D, D])
        lng1_d = din("lng1", [D]); lnb1_d = din("lnb1", [D])
        out_d = nc.dram_tensor("out", [SQ, D], F32, kind="ExternalOutput").ap()

        KnTd = dscr("KnTd", [512, NK])
        KpTd = dscr("KpTd", [32, NK])
        KbTd = dscr("KbTd", [128, NK])
        Vd = dscr("Vd", [10, 128, NKT, 128])
        QnTd = dscr("QnTd", [512, SQ])
        QpTd = dscr("QpTd", [256, SQ])
        QbTd = dscr("QbTd", [512, SQ])
        GTd = dscr("GTd", [1024, SQ])
        MTd = dscr("MTd", [1024, SQ])

        kb = KB(nc, es)

        def sb(name, shape, dt, stack=es):
            return T(stack.enter_context(nc.sbuf_tensor("s_" + name, list(shape), dt)))

        PS = es.enter_context(nc.psum_tensor("ps", [128, 4096], F32))
        banks = [PBank(PS[:, k * 512:(k + 1) * 512]) for k in range(8)]

        ident = sb("ident", [128, 128], F32)
        modc = [sb("modc0", [128, 16, 2], F32), sb("modc1", [128, 16, 2], F32)]
        gate_bc = [sb("gate0", [128, D], F32), sb("gate1", [128, D], F32)]
        epst = sb("epst", [128, 1], F32)
        ones_bf = sb("ones_bf", [128, 128], BF16)
        bdiag = sb("bdiag", [128, 128], BF16)
        sel = sb("sel", [128, 64], F32)

        sem_misc = kb.dsem("misc")

        with ExitStack() as p0:
            wm = sb("wm", [128, 8, 3 * D], F32, p0)
            cc = sb("cc", [128, 8, 2], F32, p0)
            sc = sb("sc", [128, 8, 2], F32, p0)
            screp = sb("screp", [128, 8, 128], F32, p0)
            bmc = [sb("bmc0", [128, 24], F32, p0), sb("bmc1", [128, 24], F32, p0)]
            bmg = [sb("bmg0", [128, D], F32, p0), sb("bmg1", [128, D], F32, p0)]
            sem_wm = kb.dsem("wm")

            kb.dma(sem_misc, [
                (ident.t[:], ident_d[:, :], [], [ident]),
                (cc.t[:], cc_d[:, :, :], [], [cc]),
                (bmc[0].t[:], bmodc_d[0][:, :], [], [bmc[0]]),
                (bmc[1].t[:], bmodc_d[1][:, :], [], [bmc[1]]),
                (bmg[0].t[:], bmodg_d[0].partition_broadcast(128), [], [bmg[0]]),
                (bmg[1].t[:], bmodg_d[1].partition_broadcast(128), [], [bmg[1]]),
            ])
            kb.op(kb.dve, ("memset", dict(ap=epst.t[:], constant=EPS)), writes=[epst])
            kb.op(kb.dve, ("memset", dict(ap=ones_bf.t[:], constant=1.0)), writes=[ones_bf])
            kb.op(kb.dve, ("memset", dict(ap=bdiag.t[:], constant=0.0)), writes=[bdiag])
            kb.op(kb.dve, ("memset", dict(ap=bdiag.t[0:64, 0:64], constant=1.0)), writes=[bdiag])
            kb.op(kb.dve, ("memset", dict(ap=bdiag.t[64:128, 64:128], constant=1.0)), writes=[bdiag])
            kb.op(kb.dve, ("memset", dict(ap=sel.t[:], constant=0.0)), writes=[sel])
            kb.op(kb.dve, ("memset", dict(ap=sel.t[64:65, :], constant=1.0)), writes=[sel])
            kb.op(kb.act, ("activation", dict(out=sc.t[:], in_=cc.t[:], func=AF.Silu)),
                  reads=[cc], writes=[sc])
            for j in range(8):
                kb.op(kb.dve, ("tensor_copy", dict(
                    out=screp.t[:, j, :], in_=sc.t[:, j, 0:1].to_broadcast([128, 128]))),
                    reads=[sc], writes=[screp])
            for L in range(2):
                wv = wmod_d[L].rearrange("(j p) n -> p j n", p=128)
                kb.dma(sem_wm, [(wm.t[:, j, :], wv[:, j, :], [], [wm]) for j in range(8)])
                fns = []
                for k in range(16):
                    for j in range(8):
                        fns.append(("matmul", dict(out=PS[:, 2 * k:2 * k + 2], lhsT=wm.t[:, j, k * 128:(k + 1) * 128],
                            rhs=sc.t[:, j, 0:2], start=(j == 0), stop=(j == 7))))
                kb.group(kb.pe, fns, reads=[wm, sc], writes=[banks[0]])
                kb.op(kb.dve, ("tensor_tensor", dict(
                    out=modc[L].t[:], in0=PS[:, 0:32].rearrange("p (k t) -> p k t", t=2),
                    in1=bmc[L].t[:, 0:16].unsqueeze(2).to_broadcast([128, 16, 2]), op=ALU.add)),
                    reads=[banks[0], bmc[L]], writes=[modc[L]])
                kb.op(kb.dve, ("tensor_scalar", dict(
                    out=modc[L].t[:, 8:16, :], in0=modc[L].t[:, 8:16, :], scalar1=1.0, scalar2=None,
                    op0=ALU.add)), reads=[modc[L]], writes=[modc[L]])
                fns = []
                for blk in range(2):
                    for j in range(8):
                        fns.append(("matmul", dict(out=PS[:, 512 + blk * 512:1024 + blk * 512], lhsT=screp.t[:, j, :],
                            rhs=wm.t[:, j, 2048 + blk * 512:2560 + blk * 512],
                            start=(j == 0), stop=(j == 7))))
                kb.group(kb.pe, fns, reads=[wm, screp], writes=[banks[1], banks[2]])
                kb.op(kb.dve, ("tensor_tensor", dict(
                    out=gate_bc[L].t[:], in0=PS[:, 512:1536], in1=bmg[L].t[:], op=ALU.add)),
                    reads=[banks[1], banks[2], bmg[L]], writes=[gate_bc[L]])
            kb.flush()
        if stop_after == 0:
            return nc

        PH = dict(kb=kb, nc=nc, sb=sb, PS=PS, banks=banks, ident=ident, modc=modc, gate_bc=gate_bc,
                  epst=epst, ones_bf=ones_bf, bdiag=bdiag, sel=sel)
        PH.update(x_all=x_all, ctx=ctx, tabs=tabs, w0_d=w0_d, gv0_d=gv0_d, wuq_d=wuq_d, aqc_d=aqc_d,
                  wukv_d=wukv_d, akvc_d=akvc_d, bqk_d=bqk_d, KnTd=KnTd, KpTd=KpTd, KbTd=KbTd, Vd=Vd,
                  QnTd=QnTd, QpTd=QpTd, QbTd=QbTd, GTd=GTd, MTd=MTd, wout0_d=wout0_d, lng0_d=lng0_d,
                  lnb0_d=lnb0_d, w1in_d=w1in_d, vlng_d=vlng_d, vlnb_d=vlnb_d, wsT_d=wsT_d, bs_d=bs_d,
                  wout1_d=wout1_d, lng1_d=lng1_d, lnb1_d=lnb1_d, out_d=out_d)
        phase1(PH)
        if stop_after == 1:
            return nc
        phase2(PH)
        if stop_after == 2:
            return nc
        phase3(PH)
        return nc


def phase1(P):
    kb, nc, sb, PS, banks = P["kb"], P["nc"], P["sb"], P["PS"], P["banks"]
    ident, modc, epst, ones_bf, bdiag = P["ident"], P["modc"][0], P["epst"], P["ones_bf"], P["bdiag"]
    pe, act, dve, pool = kb.pe, kb.act, kb.dve, kb.pool
    with ExitStack() as p1:
        W0 = sb("W0", [128, 8, NC0], BF16, p1)
        wuq = sb("wuq", [128, 3, 1024], BF16, p1)
        wukv = sb("wukv", [128, 2, 1024], BF16, p1)
        aqc = sb("aqc", [128, 3], F32, p1)
        akvc = sb("akvc", [128, 2], F32, p1)
        bqk = sb("bqk", [128, 2], F32, p1)
        ginv = sb("ginv", [128, 2], F32, p1)
        p1a = ExitStack()
        gv = sb("gv", [128, NC0], F32, p1a)
        wst = [sb("wst%d" % i, [128, NC0], F32, p1a) for i in range(2)]
        wst_sem = [kb.dsem("wst%d" % i) for i in range(2)]
        sem_c = kb.dsem("p1c")
        kb.dma(sem_c, [
            (gv.t[:], P["gv0_d"].partition_broadcast(128), [], [gv]),
            (aqc.t[:], P["aqc_d"][:, :], [], [aqc]),
            (akvc.t[:], P["akvc_d"][:, :], [], [akvc]),
            (bqk.t[:], P["bqk_d"][:, :], [], [bqk]),
        ])
        kb.op(dve, ("reciprocal", dict(out=ginv.t[:], in_=bqk.t[:])), reads=[bqk], writes=[ginv])
        for j in range(8):
            st = wst[j % 2]
            kb.dma(wst_sem[j % 2], [(st.t[:], P["w0_d"][j * 128:(j + 1) * 128, :], [], [st])])
            eng = dve if j % 2 == 0 else pool
            kb.op(eng, ("tensor_tensor", dict(
                out=W0.t[:, j, :], in0=st.t[:], in1=gv.t[:], op=ALU.mult)), reads=[st, gv], writes=[W0])
        for r in range(3):
            st = wst[r % 2]
            kb.dma(wst_sem[r % 2], [(st.t[:, 0:1024], P["wuq_d"][r * 128:(r + 1) * 128, :], [], [st])])
            kb.op(act, ("activation", dict(
                out=wuq.t[:, r, :], in_=st.t[:, 0:1024], func=AF.Copy, scale=aqc.t[:, r:r + 1])),
                reads=[st, aqc], writes=[wuq])
        for r in range(2):
            st = wst[(r + 1) % 2]
            kb.dma(wst_sem[(r + 1) % 2], [(st.t[:, 0:1024], P["wukv_d"][r * 128:(r + 1) * 128, :], [], [st])])
            kb.op(act, ("activation", dict(
                out=wukv.t[:, r, :], in_=st.t[:, 0:1024], func=AF.Copy, scale=akvc.t[:, r:r + 1])),
                reads=[st, akvc], writes=[wukv])

        kb.flush()
        p1a.close()

        def rot(name, n, shape, dt):
            return Rot([sb("%s%d" % (name, i), shape, dt, p1) for i in range(n)])

        XIN = rot("xin", 2, [128, 4, D], F32)
        TBL = rot("tbl", 2, [128, 4, 512], F32)
        ld_sem = Rot([kb.dsem("ld%d" % i) for i in range(2)])
        XMT = rot("xmT", 2, [128, 8, 512], BF16)
        CKVT = rot("ckvT", 2, [128, 2, 512], BF16)
        SQ2 = rot("sq2", 2, [128, 3, 512], BF16)
        RBC = rot("rbc", 2, [128, 512], F32)
        RCOL = rot("rcol", 2, [128, 4], F32)
        RQ = rot("rq", 1, [128, 512], F32)
        TMP = rot("tmp", 3, [128, 512], F32)
        KNT = rot("knT", 1, [128, 4, 512], BF16)
        KPE = rot("kpe", 2, [32, 512], BF16)
        KBR = rot("kbr", 2, [128, 512], BF16)
        VST = rot("vst", 2, [128, 10, 4, 128], BF16)
        CQT = rot("cqT", 1, [128, 3, 512], BF16)
        GST = rot("gst", 1, [128, 8, 512], BF16)
        QBR = rot("qbr", 1, [128, 4, 512], BF16)
        QNT = rot("qnT", 1, [128, 4, 512], BF16)
        QPT = rot("qpT", 1, [128, 2, 512], BF16)
        st_sems = {}

        def stsem(name, slot):
            key = (name, slot)
            if key not in st_sems:
                st_sems[key] = kb.dsem("st_%s%d" % (name, slot))
            return st_sems[key]

        for v in VST.items:
            kb.op(pool, ("memset", dict(ap=v.t[:], constant=1.0)), writes=[v])
        PSB = Rot(banks)

        nblk = S // 512

        def do_block(bi):
            is_ctx = (bi == nblk)
            nt = 256 if is_ctx else 512
            ntt = nt // 128
            do_q = (not is_ctx) and bi < SQ // 512
            k0 = bi * 512
            mc = 1 if is_ctx else 0
            slot = bi % 2
            xin, tbl, lsem = XIN.next(), TBL.next(), ld_sem.next()
            if is_ctx:
                src = P["ctx"].rearrange("(t p) f -> p t f", p=128)
                kb.dma(lsem, [(xin.t[:, 0:2, :], src[:, :, :], [], [xin])])
            else:
                src = P["x_all"].rearrange("(t p) f -> p t f", p=128)
                tsrc = P["tabs"].rearrange("k p t -> p k t")
                kb.dma(lsem, [(xin.t[:], src[:, bi * 4:bi * 4 + 4, :], [], [xin]),
                              (tbl.t[:], tsrc[:, :, k0:k0 + 512], [], [tbl])])
            TBc, TBs, TAc, TAs = (tbl.t[:, i, :] for i in range(4))
            xmT = XMT.next()
            for j in range(8):
                pb = PSB.next()
                kb.group(pe, [("transpose", dict(
                    out=pb.ap[:, tt * 128:(tt + 1) * 128], in_=xin.t[:, tt, j * 128:(j + 1) * 128],
                    identity=ident.t[:])) for tt in range(ntt)], reads=[xin, ident], writes=[pb])
                kb.op(act, ("activation", dict(
                    out=xmT.t[:, j, 0:nt], in_=pb.ap[:, 0:nt], func=AF.Identity,
                    scale=modc.t[:, 8 + j, mc:mc + 1], bias=modc.t[:, j, mc:mc + 1])),
                    reads=[pb, modc], writes=[xmT])

            def proj(off, M):
                pb = PSB.next()
                kb.group(pe, [("matmul", dict(out=pb.ap[0:M, 0:nt], lhsT=W0.t[:, j, off:off + M], rhs=xmT.t[:, j, 0:nt],
                    start=(j == 0), stop=(j == 7))) for j in range(8)], reads=[W0, xmT], writes=[pb])
                return pb

            def rstd_from(pb, npart, ncol, scale, dst_ap, dst):
                kb.op(act, ("activation", dict(out=dst_ap, in_=pb.ap[0:npart, 0:ncol], func=AF.Ln,
                                                  scale=scale, bias=epst.t[0:npart, 0:1])),
                      reads=[pb, epst], writes=[dst])
                kb.op(act, ("activation", dict(out=dst_ap, in_=dst_ap, func=AF.Exp, scale=-0.5)),
                      reads=[dst], writes=[dst])

            ckvT = CKVT.next()
            sq = SQ2.next()
            for r in range(2):
                pb = proj(O_CKV + r * 128, 128)
                kb.op(dve, ("tensor_copy", dict(out=ckvT.t[:, r, 0:nt], in_=pb.ap[:, 0:nt])),
                      reads=[pb], writes=[ckvT])
                kb.op(act, ("activation", dict(out=sq.t[:, r, 0:nt], in_=pb.ap[:, 0:nt],
                                                              func=AF.Square)), reads=[pb], writes=[sq])
            pbR = PSB.next()
            kb.group(pe, [("matmul", dict(out=pbR.ap[:, 0:nt], lhsT=ones_bf.t[:], rhs=sq.t[:, r, 0:nt], start=(r == 0), stop=(r == 1)))
                for r in range(2)], reads=[ones_bf, sq], writes=[pbR])
            rkv = RBC.next()
            rstd_from(pbR, 128, nt, 1.0 / 256, rkv.t[:, 0:nt], rkv)
            pbc = PSB.next()
            fns = []
            for tt in range(ntt):
                for r in range(2):
                    fns.append(("matmul", dict(out=pbc.ap[:, tt:tt + 1], lhsT=sq.t[:, r, tt * 128:(tt + 1) * 128],
                        rhs=ones_bf.t[:, 0:1], start=(r == 0), stop=(r == 1))))
            kb.group(pe, fns, reads=[sq, ones_bf], writes=[pbc])
            rcol = RCOL.next()
            rstd_from(pbc, 128, ntt, 1.0 / 256, rcol.t[:, 0:ntt], rcol)
            knT = KNT.next()
            for c in range(4):
                pb = PSB.next()
                kb.group(pe, [("matmul", dict(out=pb.ap[:, 0:nt], lhsT=wukv.t[:, r, c * 128:(c + 1) * 128], rhs=ckvT.t[:, r, 0:nt],
                    start=(r == 0), stop=(r == 1))) for r in range(2)], reads=[wukv, ckvT], writes=[pb])
                kb.op(dve, ("tensor_tensor", dict(
                    out=knT.t[:, c, 0:nt], in0=pb.ap[:, 0:nt], in1=rkv.t[:, 0:nt], op=ALU.mult)),
                    reads=[pb, rkv], writes=[knT])
            kb.dma(stsem("kn", slot), [(P["KnTd"].rearrange("(c p) t -> p c t", p=128)[:, :, k0:k0 + nt],
                                        knT.t[:, :, 0:nt], [knT], [])])
            vst = VST.next()
            for tt in range(ntt):
                pb = PSB.next()
                kb.group(pe, [("matmul", dict(out=pb.ap[:, 0:512], lhsT=ckvT.t[:, r, tt * 128:(tt + 1) * 128], rhs=wukv.t[:, r, 512:1024],
                    start=(r == 0), stop=(r == 1))) for r in range(2)], reads=[wukv, ckvT], writes=[pb])
                kb.op(dve, ("tensor_scalar", dict(
                    out=vst.t[:, 0:8, tt, 0:64], in0=pb.ap[:, 0:512].rearrange("p (h e) -> p h e", e=64),
                    scalar1=rcol.t[:, tt:tt + 1], scalar2=None, op0=ALU.mult)),
                    reads=[pb, rcol], writes=[vst])
            for tt in range(ntt):
                pb = PSB.next()
                kb.group(pe, [("matmul", dict(out=pb.ap[:, 0:128], lhsT=xmT.t[:, j, tt * 128:(tt + 1) * 128],
                    rhs=W0.t[:, j, O_VB:O_VB + 128], start=(j == 0), stop=(j == 7))) for j in range(8)],
                    reads=[W0, xmT], writes=[pb])
                kb.op(act, ("activation", dict(
                    out=vst.t[:, 8:10, tt, 0:64], in_=pb.ap[:, 0:128].rearrange("p (h e) -> p h e", e=64),
                    func=AF.Copy)), reads=[pb], writes=[vst])
            kt0 = k0 // 128
            kb.dma(stsem("v", slot), [(P["Vd"].rearrange("h p k e -> p h k e")[:, :, kt0:kt0 + ntt, :],
                                       vst.t[:, :, 0:ntt, :], [vst], [])])
            kpe = KPE.next()
            pb1 = proj(O_KR, 32)
            if is_ctx:
                kb.op(dve, ("tensor_copy", dict(out=kpe.t[:, 0:nt], in_=pb1.ap[0:32, 0:nt])),
                      reads=[pb1], writes=[kpe])
            else:
                pb2 = proj(O_KRP, 32)
                t1, t2 = TMP.next(), TMP.next()
                kb.op(dve, ("tensor_tensor", dict(out=t1.t[0:32, :], in0=pb1.ap[0:32, :], in1=TAc[0:32, :],
                                                     op=ALU.mult)), reads=[pb1, tbl], writes=[t1])
                kb.op(dve, ("tensor_tensor", dict(out=t2.t[0:32, :], in0=pb2.ap[0:32, :], in1=TAs[0:32, :],
                                                     op=ALU.mult)), reads=[pb2, tbl], writes=[t2])
                kb.op(pool, ("tensor_tensor", dict(out=kpe.t[:, :], in0=t1.t[0:32, :], in1=t2.t[0:32, :],
                                                      op=ALU.add)), reads=[t1, t2], writes=[kpe])
            kb.dma(stsem("kp", slot), [(P["KpTd"][:, k0:k0 + nt], kpe.t[:, 0:nt], [kpe], [])])

            def normrope(off, offp, gcol, dst_ap, dst, rope):
                pb1 = proj(off, 128)
                sqk = SQ2.next()
                kb.op(act, ("activation", dict(out=sqk.t[:, 0, 0:nt], in_=pb1.ap[:, 0:nt], func=AF.Square,
                                                  scale=ginv.t[:, gcol:gcol + 1])), reads=[pb1, ginv], writes=[sqk])
                pbR = PSB.next()
                kb.group(pe, [("matmul", dict(out=pbR.ap[:, 0:nt], lhsT=bdiag.t[:], rhs=sqk.t[:, 0, 0:nt],
                                                 start=True, stop=True))], reads=[bdiag, sqk], writes=[pbR])
                rr = RBC.next()
                rstd_from(pbR, 128, nt, 1.0 / 64, rr.t[:, 0:nt], rr)
                if not rope:
                    kb.op(dve, ("tensor_tensor", dict(out=dst_ap, in0=pb1.ap[:, 0:nt], in1=rr.t[:, 0:nt],
                                                         op=ALU.mult)), reads=[pb1, rr], writes=[dst])
                    return
                pb2 = proj(offp, 128)
                t1, t2 = TMP.next(), TMP.next()
                kb.op(dve, ("tensor_tensor", dict(out=t1.t[:], in0=pb1.ap[:, :], in1=TBc, op=ALU.mult)),
                      reads=[pb1, tbl], writes=[t1])
                kb.op(dve, ("tensor_tensor", dict(out=t2.t[:], in0=pb2.ap[:, :], in1=TBs, op=ALU.mult)),
                      reads=[pb2, tbl], writes=[t2])
                kb.op(pool, ("tensor_tensor", dict(out=t1.t[:], in0=t1.t[:], in1=t2.t[:], op=ALU.add)),
                      reads=[t1, t2], writes=[t1])
                kb.op(dve, ("tensor_tensor", dict(out=dst_ap, in0=t1.t[:], in1=rr.t[:], op=ALU.mult)),
                      reads=[t1, rr], writes=[dst])

            kbr = KBR.next()
            normrope(O_KB, O_KBP, 1, kbr.t[:, 0:nt], kbr, not is_ctx)
            kb.dma(stsem("kb", slot), [(P["KbTd"][:, k0:k0 + nt], kbr.t[:, 0:nt], [kbr], [])])

            if not do_q:
                return
            cqT = CQT.next()
            sq3 = SQ2.next()
            for r in range(3):
                pb = proj(O_CQ + r * 128, 128)
                kb.op(dve, ("tensor_copy", dict(out=cqT.t[:, r, :], in_=pb.ap[:, :])),
                      reads=[pb], writes=[cqT])
                kb.op(act, ("activation", dict(out=sq3.t[:, r, :], in_=pb.ap[:, :],
                                                              func=AF.Square)), reads=[pb], writes=[sq3])
            pbR = PSB.next()
            kb.group(pe, [("matmul", dict(out=pbR.ap[:, :], lhsT=ones_bf.t[:], rhs=sq3.t[:, r, :], start=(r == 0), stop=(r == 2)))
                for r in range(3)], reads=[ones_bf, sq3], writes=[pbR])
            rq = RQ.next()
            rstd_from(pbR, 128, 512, 1.0 / 384, rq.t[:], rq)
            gst = GST.next()
            for gi, off in enumerate((O_GA, O_GB)):
                for c in range(4):
                    pb = proj(off + c * 128, 128)
                    kb.op(act, ("activation", dict(
                        out=gst.t[:, gi * 4 + c, :], in_=pb.ap[:, :], func=AF.Silu)), reads=[pb], writes=[gst])
            kb.dma(stsem("g", slot), [(P["GTd"].rearrange("(c p) t -> p c t", p=128)[:, :, k0:k0 + 512],
                                       gst.t[:], [gst], [])])
            qbr = QBR.next()
            for c in range(4):
                normrope(O_QB + c * 128, O_QBP + c * 128, 0, qbr.t[:, c, :], qbr, True)
            kb.dma(stsem("qb", slot), [(P["QbTd"].rearrange("(c p) t -> p c t", p=128)[:, :, k0:k0 + 512],
                                        qbr.t[:], [qbr], [])])
            qnT = QNT.next()
            for c in range(4):
                pb = PSB.next()
                kb.group(pe, [("matmul", dict(out=pb.ap[:, :], lhsT=wuq.t[:, r, c * 128:(c + 1) * 128], rhs=cqT.t[:, r, :],
                    start=(r == 0), stop=(r == 2))) for r in range(3)], reads=[wuq, cqT], writes=[pb])
                kb.op(dve, ("tensor_tensor", dict(
                    out=qnT.t[:, c, :], in0=pb.ap[:, :], in1=rq.t[:], op=ALU.mult)),
                    reads=[pb, rq], writes=[qnT])
            kb.dma(stsem("qn", slot), [(P["QnTd"].rearrange("(c p) t -> p c t", p=128)[:, :, k0:k0 + 512],
                                        qnT.t[:], [qnT], [])])
            qpT = QPT.next()
            for c in range(2):
                pbs = []
                for base in (512, 768):
                    pb = PSB.next()
                    kb.group(pe, [("matmul", dict(out=pb.ap[:, :], lhsT=wuq.t[:, r, base + c * 128:base + (c + 1) * 128], rhs=cqT.t[:, r, :],
                        start=(r == 0), stop=(r == 2))) for r in range(3)], reads=[wuq, cqT], writes=[pb])
                    pbs.append(pb)
                t1, t2 = TMP.next(), TMP.next()
                kb.op(dve, ("tensor_tensor", dict(out=t1.t[:], in0=pbs[0].ap[:, :], in1=TAc,
                                                                       op=ALU.mult)), reads=[pbs[0], tbl], writes=[t1])
                kb.op(dve, ("tensor_tensor", dict(out=t2.t[:], in0=pbs[1].ap[:, :], in1=TAs,
                                                                       op=ALU.mult)), reads=[pbs[1], tbl], writes=[t2])
                kb.op(pool, ("tensor_tensor", dict(out=t1.t[:], in0=t1.t[:], in1=t2.t[:],
                                                                    op=ALU.add)), reads=[t1, t2], writes=[t1])
                kb.op(dve, ("tensor_tensor", dict(out=qpT.t[:, c, :], in0=t1.t[:], in1=rq.t[:],
                                                                 op=ALU.mult)), reads=[t1, rq], writes=[qpT])
            kb.dma(stsem("qp", slot), [(P["QpTd"].rearrange("(c p) t -> p c t", p=128)[:, :, k0:k0 + 512],
                                        qpT.t[:], [qpT], [])])

        blist = list(range(nblk + 1))
        if os.environ.get("P1_BLOCKS"):
            blist = [int(v) for v in os.environ["P1_BLOCKS"].split(",") if v != "x"]
        for bi in blist:
            do_block(bi)
        kb.flush()


def phase2(P):
    kb, nc, sb, PS = P["kb"], P["nc"], P["sb"], P["PS"]
    sel = P["sel"]
    pe, act, dve, pool = kb.pe, kb.act, kb.dve, kb.pool
    with ExitStack() as p2:
        KT = [sb("KT%d" % i, [128, NK], BF16, p2) for i in range(2)]
        VV = [sb("VV%d" % i, [128, NKT, 128], BF16, p2) for i in range(2)]
        QT = [sb("QT%d" % i, [128, SQ], BF16, p2) for i in range(2)]
        GT = [sb("GT%d" % i, [64, SQ], BF16, p2) for i in range(2)]
        hsem = [kb.dsem("hd%d" % i) for i in range(2)]
        PT = [sb("PT%d" % i, [128, 1024], BF16, p2) for i in range(2)]
        OCP = [sb("ocp%d" % i, [128, 1024], F32, p2) for i in range(2)]
        RC = sb("rc", [64, 1024], F32, p2)
        M1 = sb("m1", [64, 1024], F32, p2)
        MST = [sb("mst%d" % i, [64, 1024], BF16, p2) for i in range(2)]
        msem = [kb.dsem("mst%d" % i) for i in range(2)]
        psS = [PBank(PS[:, 0:1024]), PBank(PS[:, 1024:2048])]
        psO = PBank(PS[:, 2048:3072])
        psB = PBank(PS[:, 3072:4096])

        def load_head(h16):
            sl = h16 % 2
            kt, vv, qt, gt = KT[sl], VV[sl], QT[sl], GT[sl]
            items = []
            if h16 < 8:
                h = h16
                items.append((kt.t[0:64, :], P["KnTd"][h * 64:(h + 1) * 64, :], [], [kt]))
                items.append((kt.t[64:96, :], P["KpTd"][:, :], [], [kt]))
                items.append((vv.t[:], P["Vd"][h], [], [vv]))
                items.append((qt.t[0:64, :], P["QnTd"][h * 64:(h + 1) * 64, :], [], [qt]))
                items.append((qt.t[64:96, :], P["QpTd"][h * 32:(h + 1) * 32, :], [], [qt]))
            else:
                hb = h16 - 8
                kvh = hb // 4
                items.append((kt.t[0:64, :], P["KbTd"][kvh * 64:(kvh + 1) * 64, :], [], [kt]))
                items.append((vv.t[:], P["Vd"][8 + kvh], [], [vv]))
                items.append((qt.t[0:64, :], P["QbTd"][hb * 64:(hb + 1) * 64, :], [], [qt]))
            items.append((gt.t[:], P["GTd"][h16 * 64:(h16 + 1) * 64, :], [], [gt]))
            kb.dma(hsem[sl], items)

        iters = [(h16, sbk, kt) for h16 in range(16) for sbk in range(SQ // 1024) for kt in range(NKT)]
        N = len(iters)

        def emit_S(n):
            h16, sbk, kt = iters[n]
            sl = h16 % 2
            kd = 96 if h16 < 8 else 64
            ps = psS[n % 2]
            kb.group(pe, [("matmul", dict(out=ps.ap[:, hf * 512:(hf + 1) * 512],
                                          lhsT=KT[sl].t[0:kd, kt * 128:(kt + 1) * 128],
                                          rhs=QT[sl].t[0:kd, sbk * 1024 + hf * 512:sbk * 1024 + (hf + 1) * 512],
                                          start=True, stop=True)) for hf in range(2)],
                     reads=[KT[sl], QT[sl]], writes=[ps])

        def emit_exp(n):
            h16, sbk, kt = iters[n]
            sc = A_SCALE if h16 < 8 else B_SCALE
            kb.op(act, ("activation", dict(out=PT[n % 2].t[:], in_=psS[n % 2].ap[:, :], func=AF.Exp, scale=sc)),
                  reads=[psS[n % 2]], writes=[PT[n % 2]])

        def emit_PV(n):
            h16, sbk, kt = iters[n]
            sl = h16 % 2
            kb.group(pe, [("matmul", dict(out=psO.ap[:, hf * 512:(hf + 1) * 512], lhsT=VV[sl].t[:, kt, :],
                                          rhs=PT[n % 2].t[:, hf * 512:(hf + 1) * 512],
                                          start=(kt == 0), stop=(kt == NKT - 1))) for hf in range(2)],
                     reads=[VV[sl], PT[n % 2]], writes=[psO])

        ep = [0]

        def emit_epi1(n):
            e = ep[0]
            kb.op(dve, ("tensor_copy", dict(out=OCP[e % 2].t[:], in_=psO.ap[:, :])), reads=[psO], writes=[OCP[e % 2]])

        def emit_epi2(n):
            h16, sbk, kt = iters[n]
            sl = h16 % 2
            e = ep[0]
            ep[0] += 1
            ocp, mst = OCP[e % 2], MST[e % 2]
            kb.group(pe, [("matmul", dict(out=psB.ap[0:64, hf * 512:(hf + 1) * 512], lhsT=sel.t[:],
                                          rhs=ocp.t[:, hf * 512:(hf + 1) * 512], start=True, stop=True))
                          for hf in range(2)], reads=[sel, ocp], writes=[psB])
            kb.op(dve, ("reciprocal", dict(out=RC.t[:], in_=psB.ap[0:64, :])), reads=[psB], writes=[RC])
            kb.op(dve, ("tensor_tensor", dict(out=M1.t[:], in0=ocp.t[0:64, :], in1=RC.t[:], op=ALU.mult)),
                  reads=[ocp, RC], writes=[M1])
            kb.op(dve, ("tensor_tensor", dict(out=mst.t[:], in0=M1.t[:],
                                              in1=GT[sl].t[:, sbk * 1024:(sbk + 1) * 1024], op=ALU.mult)),
                  reads=[M1, GT[sl]], writes=[mst])
            kb.dma(msem[e % 2], [(P["MTd"][h16 * 64:(h16 + 1) * 64, sbk * 1024:(sbk + 1) * 1024], mst.t[:],
                                  [mst], [])])
            if sbk == SQ // 1024 - 1 and h16 + 2 < 16:
                load_head(h16 + 2)

        load_head(0)
        load_head(1)
        pending = []
        emit_S(0)
        for n in range(N):
            h16, sbk, kt = iters[n]
            if n + 1 < N:
                emit_S(n + 1)
            emit_exp(n)
            emit_PV(n)
            if kt == NKT - 1:
                emit_epi1(n)
                pending.append((n + 2, n))
            if pending and (pending[0][0] <= n or n == N - 1):
                emit_epi2(pending.pop(0)[1])
        while pending:
            emit_epi2(pending.pop(0)[1])
        kb.flush()


def phase3(P):
    kb, nc, sb, PS = P["kb"], P["nc"], P["sb"], P["PS"]
    ident, epst = P["ident"], P["epst"]
    modc1 = P["modc"][1]
    gate0, gate1 = P["gate_bc"]
    pe, act, dve, pool = kb.pe, kb.act, kb.dve, kb.pool
    with ExitStack() as p3:
        w0o = sb("w0o", [128, 8, D], BF16, p3)
        w1i = sb("w1i", [128, 8, 3 * D], BF16, p3)
        w1o = sb("w1o", [128, 8, D], BF16, p3)
        wsT = sb("wsT", [128, 8, 128], BF16, p3)
        bct = {}
        for nm in ("lng0", "lnb0", "lng1", "lnb1", "vlng", "vlnb"):
            bct[nm] = sb("bc_" + nm, [128, D], F32, p3)
        bsb = sb("bsb", [128, 8, 128], F32, p3)
        sem_c = kb.dsem("p3c")
        kb.dma(sem_c, [(bct[nm].t[:], P[nm + "_d"].partition_broadcast(128), [], [bct[nm]]) for nm in bct] +
               [(bsb.t[:], P["bs_d"].partition_broadcast(128), [], [bsb])])
        p3a = ExitStack()
        wst = [sb("wst3_%d" % i, [128, 3 * D], F32, p3a) for i in range(2)]
        wsem = [kb.dsem("wst3_%d" % i) for i in range(2)]
        k = 0
        for j in range(8):
            st = wst[k % 2]
            kb.dma(wsem[k % 2], [(st.t[:], P["w1in_d"][j * 128:(j + 1) * 128, :], [], [st])])
            kb.op(act if j % 2 == 0 else dve, ("activation" if j % 2 == 0 else "tensor_copy",
                                               dict(out=w1i.t[:, j, :], in_=st.t[:], func=AF.Copy) if j % 2 == 0
                                               else dict(out=w1i.t[:, j, :], in_=st.t[:])),
                  reads=[st], writes=[w1i])
            k += 1
        for (dst, src, gt) in ((w0o, "wout0_d", gate0), (w1o, "wout1_d", gate1)):
            for j2 in range(4):
                st = wst[k % 2]
                kb.dma(wsem[k % 2], [(st.t[:, 0:2048].rearrange("p (a n) -> p a n", a=2),
                                      P[src][j2 * 256:(j2 + 1) * 256, :].rearrange("(a p) n -> p a n", p=128),
                                      [], [st])])
                for a in range(2):
                    kb.op(dve if a == 0 else pool, ("tensor_tensor", dict(
                        out=dst.t[:, j2 * 2 + a, :], in0=st.t[:, a * 1024:(a + 1) * 1024], in1=gt.t[:], op=ALU.mult)),
                        reads=[st, gt], writes=[dst])
                k += 1
        st = wst[k % 2]
        kb.dma(wsem[k % 2], [(st.t[:, 0:1024].rearrange("p (g q) -> p g q", g=8),
                              P["wsT_d"].rearrange("g q p -> q g p"), [], [st])])
        kb.op(dve, ("tensor_copy", dict(out=wsT.t[:], in_=st.t[:, 0:1024].rearrange("p (g q) -> p g q", g=8))),
              reads=[st], writes=[wsT])
        kb.flush()
        p3a.close()

        def rot(name, n, shape, dt):
            return Rot([sb("%s%d" % (name, i), shape, dt, p3) for i in range(n)])

        MTB = rot("mtb", 1, [128, 8, 512], BF16)
        mt_sem = Rot([kb.dsem("mtb%d" % i) for i in range(2)])
        XR = rot("xr", 2, [128, D], F32)
        xr_sem = Rot([kb.dsem("xr%d" % i) for i in range(2)])
        X1 = rot("x1", 1, [128, 4, D], F32)
        XMT = rot("x1mT", 1, [128, 8, 512], BF16)
        UG = rot("ug", 1, [128, 8, 512], F32)
        WK = rot("wk", 3, [128, D], F32)
        TS = rot("tsil", 2, [128, 512], F32)
        STT = rot("stt", 2, [128, 2, 6], F32)
        MV = rot("mv", 2, [128, 2], F32)
        RS = rot("rs", 2, [128, 1], F32)
        VLN = rot("vln", 2, [128, D], BF16)
        ZT = rot("zT", 2, [128, 8, 128], BF16)
        OUT = rot("outt", 2, [128, D], F32)
        out_sem = Rot([kb.dsem("out%d" % i) for i in range(2)])
        PS2 = Rot([PBank(PS[:, 0:1024]), PBank(PS[:, 1024:2048]), PBank(PS[:, 2048:3072])])
        PS1 = Rot([PBank(PS[:, 3072:3584]), PBank(PS[:, 3584:4096])])

        def layer_norm(src, dst_ap, dst, g, b):
            stt, mv, rs = STT.next(), MV.next(), RS.next()
            kb.group(dve, [("bn_stats", dict(out=stt.t[:, hh, :], in_=src.t[:, hh * 512:(hh + 1) * 512]))
                           for hh in range(2)], reads=[src], writes=[stt])
            kb.op(dve, ("bn_aggr", dict(out=mv.t[:], in_=stt.t[:])), reads=[stt], writes=[mv])
            kb.op(act, ("activation", dict(out=rs.t[:], in_=mv.t[:, 1:2], func=AF.Ln, bias=epst.t[:, 0:1])),
                  reads=[mv, epst], writes=[rs])
            kb.op(act, ("activation", dict(out=rs.t[:], in_=rs.t[:], func=AF.Exp, scale=-0.5)),
                  reads=[rs], writes=[rs])
            kb.op(dve, ("tensor_scalar", dict(out=src.t[:], in0=src.t[:], scalar1=mv.t[:, 0:1], scalar2=rs.t[:, 0:1],
                                              op0=ALU.subtract, op1=ALU.mult)), reads=[src, mv, rs], writes=[src])
            kb.op(pool, ("tensor_tensor", dict(out=src.t[:], in0=src.t[:], in1=g.t[:], op=ALU.mult)),
                  reads=[src, g], writes=[src])
            kb.op(pool, ("tensor_tensor", dict(out=dst_ap, in0=src.t[:], in1=b.t[:], op=ALU.add)),
                  reads=[src, b], writes=[dst])

        for bi in range(SQ // 512):
            k0 = bi * 512
            mt, msem = MTB.next(), mt_sem.next()
            kb.dma(msem, [(mt.t[:], P["MTd"].rearrange("(c p) t -> p c t", p=128)[:, :, k0:k0 + 512], [], [mt])])
            x1 = X1.next()
            for tt in range(4):
                xr, xsem = XR.next(), xr_sem.next()
                r0 = k0 + tt * 128
                kb.dma(xsem, [(xr.t[:], P["x_all"][r0:r0 + 128, :], [], [xr])])
                psY = PS2.next()
                kb.group(pe, [("matmul", dict(out=psY.ap[:, cb * 512:(cb + 1) * 512],
                                              lhsT=mt.t[:, c, tt * 128:(tt + 1) * 128],
                                              rhs=w0o.t[:, c, cb * 512:(cb + 1) * 512], start=(c == 0), stop=(c == 7)))
                              for cb in range(2) for c in range(8)], reads=[mt, w0o], writes=[psY])
                r = WK.next()
                kb.op(dve, ("scalar_tensor_tensor", dict(out=r.t[:], in0=xr.t[:], scalar=ALPHA, in1=psY.ap[:, :],
                                                         op0=ALU.mult, op1=ALU.add)), reads=[xr, psY], writes=[r])
                layer_norm(r, x1.t[:, tt, :], x1, bct["lng0"], bct["lnb0"])
            xmT = XMT.next()
            for j in range(8):
                pb = PS1.next()
                kb.group(pe, [("transpose", dict(out=pb.ap[:, tt * 128:(tt + 1) * 128],
                                                 in_=x1.t[:, tt, j * 128:(j + 1) * 128], identity=ident.t[:]))
                              for tt in range(4)], reads=[x1, ident], writes=[pb])
                kb.op(act, ("activation", dict(out=xmT.t[:, j, :], in_=pb.ap[:, :], func=AF.Identity,
                                               scale=modc1.t[:, 8 + j, 0:1], bias=modc1.t[:, j, 0:1])),
                      reads=[pb, modc1], writes=[xmT])
            ug = UG.next()
            for c in range(8):
                pb = PS1.next()
                kb.group(pe, [("matmul", dict(out=pb.ap[:, :], lhsT=w1i.t[:, j, c * 128:(c + 1) * 128],
                                              rhs=xmT.t[:, j, :], start=(j == 0), stop=(j == 7))) for j in range(8)],
                         reads=[w1i, xmT], writes=[pb])
                kb.op(act, ("activation", dict(out=ug.t[:, c, :], in_=pb.ap[:, :], func=AF.Gelu_apprx_tanh)),
                      reads=[pb], writes=[ug])
            for c in range(8):
                pb = PS1.next()
                kb.group(pe, [("matmul", dict(out=pb.ap[:, :], lhsT=w1i.t[:, j, 2048 + c * 128:2048 + (c + 1) * 128],
                                              rhs=xmT.t[:, j, :], start=(j == 0), stop=(j == 7))) for j in range(8)],
                         reads=[w1i, xmT], writes=[pb])
                ts = TS.next()
                kb.op(act, ("activation", dict(out=ts.t[:], in_=pb.ap[:, :], func=AF.Silu)), reads=[pb], writes=[ts])
                kb.op(pool, ("tensor_tensor", dict(out=ug.t[:, c, :], in0=ug.t[:, c, :], in1=ts.t[:], op=ALU.mult)),
                      reads=[ug, ts], writes=[ug])
            for tt in range(4):
                psV = PS2.next()
                kb.group(pe, [("matmul", dict(out=psV.ap[:, cb * 512:(cb + 1) * 512],
                                              lhsT=xmT.t[:, j, tt * 128:(tt + 1) * 128],
                                              rhs=w1i.t[:, j, 1024 + cb * 512:1536 + cb * 512],
                                              start=(j == 0), stop=(j == 7))) for cb in range(2) for j in range(8)],
                         reads=[xmT, w1i], writes=[psV])
                vg = WK.next()
                kb.op(act, ("activation", dict(out=vg.t[:], in_=psV.ap[:, :], func=AF.Gelu_apprx_tanh)),
                      reads=[psV], writes=[vg])
                vln = VLN.next()
                layer_norm(vg, vln.t[:], vln, bct["vlng"], bct["vlnb"])
                psM = PS2.next()
                kb.group(pe, [("matmul", dict(out=psM.ap[:, g * 128:(g + 1) * 128], lhsT=vln.t[:, g * 128:(g + 1) * 128],
                                              rhs=wsT.t[:, g, :], start=True, stop=True)) for g in range(8)],
                         reads=[vln, wsT], writes=[psM])
                tz = WK.next()
                kb.op(dve, ("tensor_tensor", dict(out=tz.t[:].rearrange("p (g q) -> p g q", g=8),
                                                  in0=psM.ap[:, :].rearrange("p (g q) -> p g q", g=8),
                                                  in1=bsb.t[:], op=ALU.add)), reads=[psM, bsb], writes=[tz])
                zT = ZT.next()
                kb.op(dve, ("tensor_tensor", dict(out=zT.t[:], in0=tz.t[:].rearrange("p (g q) -> p g q", g=8),
                                                  in1=ug.t[:, :, tt * 128:(tt + 1) * 128], op=ALU.mult)),
                      reads=[tz, ug], writes=[zT])
                psY1 = PS2.next()
                kb.group(pe, [("matmul", dict(out=psY1.ap[:, cb * 512:(cb + 1) * 512], lhsT=zT.t[:, g, :],
                                              rhs=w1o.t[:, g, cb * 512:(cb + 1) * 512], start=(g == 0), stop=(g == 7)))
                              for cb in range(2) for g in range(8)], reads=[zT, w1o], writes=[psY1])
                r1 = WK.next()
                kb.op(dve, ("scalar_tensor_tensor", dict(out=r1.t[:], in0=x1.t[:, tt, :], scalar=ALPHA,
                                                         in1=psY1.ap[:, :], op0=ALU.mult, op1=ALU.add)),
                      reads=[x1, psY1], writes=[r1])
                ot, osem = OUT.next(), out_sem.next()
                layer_norm(r1, ot.t[:], ot, bct["lng1"], bct["lnb1"])
                r0 = k0 + tt * 128
                kb.dma(osem, [(P["out_d"][r0:r0 + 128, :], ot.t[:], [ot], [])])
        kb.flush()


def _partner(d):
    q = d // 4
    i = np.arange(d)
    r = i % (d // 2)
    partner = np.where(r < q, i + q, i - q)
    sign = np.where(r < q, -1.0, 1.0).astype(np.float32)
    return partner, sign


def _rope_tables(pos):
    out = []
    row = (pos // 64).astype(np.float32)
    col = (pos % 64).astype(np.float32)
    for d, rep in ((64, 2), (32, 4)):
        d_axis = d // 2
        inv = (np.float32(10000.0) ** (-np.arange(0, d_axis, 2, dtype=np.float32) / np.float32(d_axis))).astype(np.float32)
        _, sign = _partner(d)
        i = np.arange(d)
        axis = i // d_axis
        j = i % (d // 4)
        p = np.where(axis[:, None] == 0, row[None, :], col[None, :]).astype(np.float32)
        ang = (p * inv[j][:, None]).astype(np.float32)
        c = np.cos(ang).astype(np.float32)
        s = (np.sin(ang).astype(np.float32) * sign[:, None]).astype(np.float32)
        out.append(np.tile(c, (rep, 1)))
        out.append(np.tile(s, (rep, 1)))
    return np.ascontiguousarray(np.stack(out, 0))


def make_in_maps(inp):
    f = lambda a: np.ascontiguousarray(np.asarray(a, dtype=np.float32))
    x = f(inp["x"]); c = f(inp["c"]); ctx = f(inp["ctx"]); c_ctx = f(inp["c_ctx"])
    e_w_in = f(inp["e_w_in"])[0]
    p64, _ = _partner(64)
    p32, _ = _partner(32)
    h8 = np.arange(8)[:, None]
    cols = np.concatenate([
        384 + np.arange(256), 640 + np.arange(32), 640 + p32,
        1696 + np.arange(128), 1696 + (np.arange(2)[:, None] * 64 + p64[None, :]).reshape(-1),
        1824 + np.arange(128), np.arange(384), 672 + np.arange(512),
        1184 + np.arange(512), 1184 + (h8 * 64 + p64[None, :]).reshape(-1), 1952 + np.arange(512)])
    assert cols.shape[0] == NC0
    w0 = np.ascontiguousarray(e_w_in[:, cols])
    bq = f(inp["e_b_q_norm"])[0]; bk = f(inp["e_b_k_norm"])[0]
    one = lambda n: np.ones(n, np.float32)
    gv0 = np.concatenate([one(256), one(32), one(32), np.tile(bk, 2), np.tile(bk[p64], 2), one(128), one(384),
                          one(512), np.tile(bq, 8), np.tile(bq[p64], 8), one(512)]).astype(np.float32)
    w_uq = f(inp["e_a_w_uq"])[0]
    cu = np.concatenate([(h8 * 96 + np.arange(64)[None, :]).reshape(-1),
                         (h8 * 96 + 64 + np.arange(32)[None, :]).reshape(-1),
                         (h8 * 96 + 64 + p32[None, :]).reshape(-1)])
    wuq = np.ascontiguousarray(w_uq[:, cu])
    w_ukv = f(inp["e_a_w_ukv"])[0]
    ck = np.concatenate([(h8 * 128 + np.arange(64)[None, :]).reshape(-1),
                         (h8 * 128 + 64 + np.arange(64)[None, :]).reshape(-1)])
    wukv = np.ascontiguousarray(w_ukv[:, ck])
    aqc = np.ascontiguousarray(f(inp["e_a_q_norm"])[0].reshape(3, 128).T)
    akvc = np.ascontiguousarray(f(inp["e_a_kv_norm"])[0].reshape(2, 128).T)
    bqk = np.ascontiguousarray(np.stack([np.tile(bq, 2), np.tile(bk, 2)], 1))
    shared = dict(
        ident=np.eye(128, dtype=np.float32),
        wmod0=f(inp["e_w_mod"])[0], wmod1=f(inp["o_w_mod"])[0],
        bmodc0=np.ascontiguousarray(f(inp["e_b_mod"])[0].reshape(24, 128).T),
        bmodc1=np.ascontiguousarray(f(inp["o_b_mod"])[0].reshape(24, 128).T),
        bmodg0=np.ascontiguousarray(f(inp["e_b_mod"])[0][2048:]), bmodg1=np.ascontiguousarray(f(inp["o_b_mod"])[0][2048:]),
        w0=w0, gv0=gv0, wuq=wuq, aqc=aqc, wukv=wukv, akvc=akvc, bqk=bqk,
        wout0=f(inp["e_w_out"])[0], lng0=f(inp["e_ln_g"])[0], lnb0=f(inp["e_ln_b"])[0],
        w1in=f(inp["o_w_in"])[0], vlng=f(inp["o_v_ln_g"])[0], vlnb=f(inp["o_v_ln_b"])[0],
        wsT=np.ascontiguousarray(np.transpose(f(inp["o_w_s"])[0], (0, 2, 1))),
        bs=np.ascontiguousarray(f(inp["o_b_s"])[0].reshape(-1)),
        wout1=f(inp["o_w_out"])[0], lng1=f(inp["o_ln_g"])[0], lnb1=f(inp["o_ln_b"])[0],
    )
    maps = []
    for core in range(8):
        b, half = core // 2, core % 2
        order = np.concatenate([np.arange(half * SQ, (half + 1) * SQ), np.arange((1 - half) * SQ, (2 - half) * SQ)])
        m = dict(shared)
        m["x_all"] = np.ascontiguousarray(x[b][order])
        m["ctx"] = ctx[b]
        m["cc"] = np.ascontiguousarray(np.stack([c[b].reshape(8, 128).T, c_ctx.reshape(8, 128).T], 2))
        m["tabs"] = _rope_tables(order)
        maps.append(m)
    return maps


_CACHE = {}


def kernel(**inputs):
    if "nc" not in _CACHE:
        _CACHE["nc"] = build()
    nc = _CACHE["nc"]
    maps = make_in_maps(inputs)
    res = run_bass_kernel_spmd(nc, maps, core_ids=list(range(8)))
    out = np.empty((4, S, D), np.float32)
    for core in range(8):
        b, half = core // 2, core % 2
        out[b, half * SQ:(half + 1) * SQ] = res.results[core]["out"]
    return out
```

```python
import os
import numpy as np
from contextlib import ExitStack
import concourse.bass as bass
import concourse.mybir as mybir
from concourse.bass_utils import run_bass_kernel_spmd

F32 = mybir.dt.float32
BF16 = mybir.dt.bfloat16
AF = mybir.ActivationFunctionType
ALU = mybir.AluOpType

S = 8192
SQ = 4096
D = 1024
CTX = 256
NK = S + CTX
NKT = NK // 128
EPS = 1e-6
ALPHA = 4.0 ** 0.25
A_SCALE = 96.0 ** -0.5
B_SCALE = 64.0 ** -0.5

O_CKV, O_KR, O_KRP, O_KB, O_KBP, O_VB, O_CQ, O_GA, O_QB, O_QBP, O_GB = (
    0, 256, 288, 320, 448, 576, 704, 1088, 1600, 2112, 2624)
NC0 = 3136


class Sem:
    def __init__(self, nc, es, name):
        self.h = es.enter_context(nc.semaphore(name))
        self.name = name
        self.count = 0


class Buf:
    def __init__(self):
        self.w = {}
        self.r = {}


def _merge(d, ev):
    s, v = ev
    if s.name not in d or d[s.name][1] < v:
        d[s.name] = (s, v)


class T(Buf):
    def __init__(self, t):
        Buf.__init__(self)
        self.t = t


class Eng:
    def __init__(self, name, sem):
        self.name = name
        self.sem = sem
        self.q = []
        self.seen = {}

    def wait(self, ev):
        s, v = ev
        if v <= 0 or self.seen.get(s.name, 0) >= v:
            return
        self.seen[s.name] = v
        self.q.append(("wait", s, v))


class KB:
    def __init__(self, nc, es):
        self.nc = nc
        self.es = es
        self.nsem = 0
        self.pe = Eng("pe", self.sem("pe"))
        self.act = Eng("act", self.sem("act"))
        self.dve = Eng("dve", self.sem("dve"))
        self.pool = Eng("pool", self.sem("pool"))
        self.sp = Eng("sp", None)
        self.dma_sems = []

    def sem(self, name):
        self.nsem += 1
        return Sem(self.nc, self.es, "%s_%d" % (name, self.nsem))

    def dsem(self, name):
        s = self.sem(name)
        self.dma_sems.append(s)
        return s

    def _deps(self, eng, reads, writes):
        for b in reads:
            for ev in list(b.w.values()):
                eng.wait(ev)
            if isinstance(b, PBank):
                for ev in list(b.r.values()):
                    if eng.sem is None or ev[0].name != eng.sem.name:
                        eng.wait(ev)
        for b in writes:
            for ev in list(b.w.values()) + list(b.r.values()):
                eng.wait(ev)

    def op(self, eng, fn, reads=(), writes=()):
        self._deps(eng, reads, writes)
        eng.sem.count += 1
        ev = (eng.sem, eng.sem.count)
        eng.q.append(("op", fn, eng.sem))
        for b in reads:
            _merge(b.r, ev)
        for b in writes:
            _merge(b.w, ev)
        return ev

    def group(self, eng, fns, reads=(), writes=()):
        self._deps(eng, reads, writes)
        for fn in fns[:-1]:
            eng.q.append(("op", fn, None))
        eng.sem.count += 1
        ev = (eng.sem, eng.sem.count)
        eng.q.append(("op", fns[-1], eng.sem))
        for b in reads:
            _merge(b.r, ev)
        for b in writes:
            _merge(b.w, ev)
        return ev

    def dma(self, sem, items, q=None):
        q = q or self.sp
        for (_, _, reads, writes) in items:
            self._deps(q, reads, writes)
        for (o, i, _, _) in items:
            sem.count += 16
            q.q.append(("dma", o, i, sem))
        ev = (sem, sem.count)
        for (_, _, reads, writes) in items:
            for b in reads:
                _merge(b.r, ev)
            for b in writes:
                _merge(b.w, ev)
        return ev

    def flush(self, final_waits=True):
        nc = self.nc
        if final_waits:
            for s in self.dma_sems:
                self.sp.wait((s, s.count))

        def replay(q, e):
            for it in q:
                if it[0] == "wait":
                    e.wait_ge(it[1].h, it[2])
                elif it[0] == "op":
                    ins = getattr(e, it[1][0])(**it[1][1])
                    if it[2] is not None:
                        ins.then_inc(it[2].h, 1)
                else:
                    e.dma_start(out=it[1], in_=it[2]).then_inc(it[3].h, 16)

        with nc.Block() as blk:
            @blk.sync
            def _(e):
                replay(self.sp.q, e)

            @blk.tensor
            def _(e):
                replay(self.pe.q, e)

            @blk.scalar
            def _(e):
                replay(self.act.q, e)

            @blk.vector
            def _(e):
                replay(self.dve.q, e)

            @blk.gpsimd
            def _(e):
                replay(self.pool.q, e)
        for e in (self.sp, self.pe, self.act, self.dve, self.pool):
            e.q = []


class Rot:
    def __init__(self, items):
        self.items = items
        self.i = 0

    def next(self):
        it = self.items[self.i % len(self.items)]
        self.i += 1
        return it


class PBank(Buf):
    def __init__(self, ap):
        Buf.__init__(self)
        self.ap = ap


def build(stop_after=99, dbg=False):
    nc = bass.Bass("TRN2", target_bir_lowering=False)
    es = ExitStack()
    with es:
        def din(name, shape, dt=F32):
            return nc.dram_tensor(name, list(shape), dt, kind="ExternalInput").ap()

        def dscr(name, shape, dt=BF16):
            kind = "ExternalOutput" if dbg else "Internal"
            return nc.dram_tensor(name, list(shape), dt, kind=kind).ap()

        x_all = din("x_all", [S, D])
        ctx = din("ctx", [CTX, D])
        cc_d = din("cc", [128, 8, 2])
        ident_d = din("ident", [128, 128])
        tabs = din("tabs", [4, 128, S])
        wmod_d = [din("wmod0", [D, 3 * D]), din("wmod1", [D, 3 * D])]
        bmodc_d = [din("bmodc0", [128, 24]), din("bmodc1", [128, 24])]
        bmodg_d = [din("bmodg0", [D]), din("bmodg1", [D])]
        w0_d = din("w0", [D, NC0])
        gv0_d = din("gv0", [NC0])
        wuq_d = din("wuq", [384, 1024])
        aqc_d = din("aqc", [128, 3])
        wukv_d = din("wukv", [256, 1024])
        akvc_d = din("akvc", [128, 2])
        bqk_d = din("bqk", [128, 2])
        wout0_d = din("wout0", [D, D])
        lng0_d = din("lng0", [D]); lnb0_d = din("lnb0", [D])
        w1in_d = din("w1in", [D, 3 * D])
        vlng_d = din("vlng", [D]); vlnb_d = din("vlnb", [D])
        wsT_d = din("wsT", [8, 128, 128])
        bs_d = din("bs", [D])
        wout1_d = din("wout1", [D, D])
        lng1_d = din("lng1", [D]); lnb1_d = din("lnb1", [D])
        out_d = nc.dram_tensor("out", [SQ, D], F32, kind="ExternalOutput").ap()

        KnTd = dscr("KnTd", [512, NK])
        KpTd = dscr("KpTd", [32, NK])
        KbTd = dscr("KbTd", [128, NK])
        Vd = dscr("Vd", [10, 128, NKT, 128])
        QnTd = dscr("QnTd", [512, SQ])
        QpTd = dscr("QpTd", [256, SQ])
        QbTd = dscr("QbTd", [512, SQ])
        GTd = dscr("GTd", [1024, SQ])
        MTd = dscr("MTd", [1024, SQ])
        sums_d = nc.dram_tensor("sums_d", [2, 1024], F32, kind="Internal").ap()

        kb = KB(nc, es)

        def sb(name, shape, dt, stack=es):
            return T(stack.enter_context(nc.sbuf_tensor("s_" + name, list(shape), dt)))

        PS = es.enter_context(nc.psum_tensor("ps", [128, 4096], F32))
        banks = [PBank(PS[:, k * 512:(k + 1) * 512]) for k in range(8)]

        ident = sb("ident", [128, 128], F32)
        modc = [sb("modc0", [128, 16, 2], F32), sb("modc1", [128, 16, 2], F32)]
        gate_bc = [sb("gate0", [128, D], F32), sb("gate1", [128, D], F32)]
        epst = sb("epst", [128, 1], F32)
        ones_bf = sb("ones_bf", [128, 128], BF16)
        bdiag = sb("bdiag", [128, 128], BF16)
        sel = sb("sel", [128, 64], F32)

        sem_misc = kb.dsem("misc")

        with ExitStack() as p0:
            wm = sb("wm", [128, 8, 3 * D], F32, p0)
            cc = sb("cc", [128, 8, 2], F32, p0)
            sc = sb("sc", [128, 8, 2], F32, p0)
            screp = sb("screp", [128, 8, 128], F32, p0)
            bmc = [sb("bmc0", [128, 24], F32, p0), sb("bmc1", [128, 24], F32, p0)]
            bmg = [sb("bmg0", [128, D], F32, p0), sb("bmg1", [128, D], F32, p0)]
            sem_wm = kb.dsem("wm")

            kb.dma(sem_misc, [
                (ident.t[:], ident_d[:, :], [], [ident]),
                (cc.t[:], cc_d[:, :, :], [], [cc]),
                (bmc[0].t[:], bmodc_d[0][:, :], [], [bmc[0]]),
                (bmc[1].t[:], bmodc_d[1][:, :], [], [bmc[1]]),
                (bmg[0].t[:], bmodg_d[0].partition_broadcast(128), [], [bmg[0]]),
                (bmg[1].t[:], bmodg_d[1].partition_broadcast(128), [], [bmg[1]]),
            ])
            kb.op(kb.dve, ("memset", dict(ap=epst.t[:], constant=EPS)), writes=[epst])
            kb.op(kb.dve, ("memset", dict(ap=ones_bf.t[:], constant=1.0)), writes=[ones_bf])
            kb.op(kb.dve, ("memset", dict(ap=bdiag.t[:], constant=0.0)), writes=[bdiag])
            kb.op(kb.dve, ("memset", dict(ap=bdiag.t[0:64, 0:64], constant=1.0)), writes=[bdiag])
            kb.op(kb.dve, ("memset", dict(ap=bdiag.t[64:128, 64:128], constant=1.0)), writes=[bdiag])
            kb.op(kb.dve, ("memset", dict(ap=sel.t[:], constant=0.0)), writes=[sel])
            kb.op(kb.dve, ("memset", dict(ap=sel.t[64:65, :], constant=1.0)), writes=[sel])
            kb.op(kb.act, ("activation", dict(out=sc.t[:], in_=cc.t[:], func=AF.Silu)),
                  reads=[cc], writes=[sc])
            for j in range(8):
                kb.op(kb.dve, ("tensor_copy", dict(
                    out=screp.t[:, j, :], in_=sc.t[:, j, 0:1].to_broadcast([128, 128]))),
                    reads=[sc], writes=[screp])
            for L in range(2):
                wv = wmod_d[L].rearrange("(j p) n -> p j n", p=128)
                kb.dma(sem_wm, [(wm.t[:, j, :], wv[:, j, :], [], [wm]) for j in range(8)])
                fns = []
                for k in range(16):
                    for j in range(8):
                        fns.append(("matmul", dict(out=PS[:, 2 * k:2 * k + 2], lhsT=wm.t[:, j, k * 128:(k + 1) * 128],
                            rhs=sc.t[:, j, 0:2], start=(j == 0), stop=(j == 7))))
                kb.group(kb.pe, fns, reads=[wm, sc], writes=[banks[0]])
                kb.op(kb.dve, ("tensor_tensor", dict(
                    out=modc[L].t[:], in0=PS[:, 0:32].rearrange("p (k t) -> p k t", t=2),
                    in1=bmc[L].t[:, 0:16].unsqueeze(2).to_broadcast([128, 16, 2]), op=ALU.add)),
                    reads=[banks[0], bmc[L]], writes=[modc[L]])
                kb.op(kb.dve, ("tensor_scalar", dict(
                    out=modc[L].t[:, 8:16, :], in0=modc[L].t[:, 8:16, :], scalar1=1.0, scalar2=None,
                    op0=ALU.add)), reads=[modc[L]], writes=[modc[L]])
                fns = []
                for blk in range(2):
                    for j in range(8):
                        fns.append(("matmul", dict(out=PS[:, 512 + blk * 512:1024 + blk * 512], lhsT=screp.t[:, j, :],
                            rhs=wm.t[:, j, 2048 + blk * 512:2560 + blk * 512],
                            start=(j == 0), stop=(j == 7))))
                kb.group(kb.pe, fns, reads=[wm, screp], writes=[banks[1], banks[2]])
                kb.op(kb.dve, ("tensor_tensor", dict(
                    out=gate_bc[L].t[:], in0=PS[:, 512:1536], in1=bmg[L].t[:], op=ALU.add)),
                    reads=[banks[1], banks[2], bmg[L]], writes=[gate_bc[L]])
            kb.flush()
        if stop_after == 0:
            return nc

        PH = dict(kb=kb, nc=nc, sb=sb, PS=PS, banks=banks, ident=ident, modc=modc, gate_bc=gate_bc,
                  epst=epst, ones_bf=ones_bf, bdiag=bdiag, sel=sel)
        PH.update(x_all=x_all, ctx=ctx, tabs=tabs, w0_d=w0_d, gv0_d=gv0_d, wuq_d=wuq_d, aqc_d=aqc_d,
                  wukv_d=wukv_d, akvc_d=akvc_d, bqk_d=bqk_d, KnTd=KnTd, KpTd=KpTd, KbTd=KbTd, Vd=Vd,
                  QnTd=QnTd, QpTd=QpTd, QbTd=QbTd, GTd=GTd, MTd=MTd, sums_d=sums_d, wout0_d=wout0_d, lng0_d=lng0_d,
                  lnb0_d=lnb0_d, w1in_d=w1in_d, vlng_d=vlng_d, vlnb_d=vlnb_d, wsT_d=wsT_d, bs_d=bs_d,
                  wout1_d=wout1_d, lng1_d=lng1_d, lnb1_d=lnb1_d, out_d=out_d)
        phase1(PH)
        if stop_after == 1:
            return nc
        phase2(PH)
        if stop_after == 2:
            return nc
        phase3(PH)
        return nc


def phase1(P):
    kb, nc, sb, PS, banks = P["kb"], P["nc"], P["sb"], P["PS"], P["banks"]
    ident, modc, epst, ones_bf, bdiag = P["ident"], P["modc"][0], P["epst"], P["ones_bf"], P["bdiag"]
    pe, act, dve, pool = kb.pe, kb.act, kb.dve, kb.pool
    with ExitStack() as p1:
        W0 = sb("W0", [128, 8, NC0], BF16, p1)
        wuq = sb("wuq", [128, 3, 1024], BF16, p1)
        wukv = sb("wukv", [128, 2, 1024], BF16, p1)
        aqc = sb("aqc", [128, 3], F32, p1)
        akvc = sb("akvc", [128, 2], F32, p1)
        bqk = sb("bqk", [128, 2], F32, p1)
        ginv = sb("ginv", [128, 2], F32, p1)
        p1a = ExitStack()
        gv = sb("gv", [128, NC0], F32, p1a)
        wst = [sb("wst%d" % i, [128, NC0], F32, p1a) for i in range(2)]
        wst_sem = [kb.dsem("wst%d" % i) for i in range(2)]
        sem_c = kb.dsem("p1c")
        kb.dma(sem_c, [
            (gv.t[:], P["gv0_d"].partition_broadcast(128), [], [gv]),
            (aqc.t[:], P["aqc_d"][:, :], [], [aqc]),
            (akvc.t[:], P["akvc_d"][:, :], [], [akvc]),
            (bqk.t[:], P["bqk_d"][:, :], [], [bqk]),
        ])
        kb.op(dve, ("reciprocal", dict(out=ginv.t[:], in_=bqk.t[:])), reads=[bqk], writes=[ginv])
        for j in range(8):
            st = wst[j % 2]
            kb.dma(wst_sem[j % 2], [(st.t[:], P["w0_d"][j * 128:(j + 1) * 128, :], [], [st])])
            eng = dve if j % 2 == 0 else pool
            kb.op(eng, ("tensor_tensor", dict(
                out=W0.t[:, j, :], in0=st.t[:], in1=gv.t[:], op=ALU.mult)), reads=[st, gv], writes=[W0])
        for r in range(3):
            st = wst[r % 2]
            kb.dma(wst_sem[r % 2], [(st.t[:, 0:1024], P["wuq_d"][r * 128:(r + 1) * 128, :], [], [st])])
            kb.op(act, ("activation", dict(
                out=wuq.t[:, r, :], in_=st.t[:, 0:1024], func=AF.Copy, scale=aqc.t[:, r:r + 1])),
                reads=[st, aqc], writes=[wuq])
        for r in range(2):
            st = wst[(r + 1) % 2]
            kb.dma(wst_sem[(r + 1) % 2], [(st.t[:, 0:1024], P["wukv_d"][r * 128:(r + 1) * 128, :], [], [st])])
            kb.op(act, ("activation", dict(
                out=wukv.t[:, r, :], in_=st.t[:, 0:1024], func=AF.Copy, scale=akvc.t[:, r:r + 1])),
                reads=[st, akvc], writes=[wukv])

        kb.flush()
        p1a.close()

        def rot(name, n, shape, dt):
            return Rot([sb("%s%d" % (name, i), shape, dt, p1) for i in range(n)])

        XIN = rot("xin", 2, [128, 4, D], F32)
        TBL = rot("tbl", 2, [128, 4, 512], F32)
        ld_sem = Rot([kb.dsem("ld%d" % i) for i in range(2)])
        XMT = rot("xmT", 2, [128, 8, 512], BF16)
        CKVT = rot("ckvT", 2, [128, 2, 512], BF16)
        SQ2 = rot("sq2", 2, [128, 3, 512], BF16)
        RBC = rot("rbc", 2, [128, 512], F32)
        RCOL = rot("rcol", 2, [128, 4], F32)
        RQ = rot("rq", 1, [128, 512], F32)
        TMP = rot("tmp", 3, [128, 512], F32)
        KNT = rot("knT", 1, [128, 4, 512], BF16)
        KPE = rot("kpe", 2, [32, 512], BF16)
        KBR = rot("kbr", 2, [128, 512], BF16)
        VST = rot("vst", 2, [128, 10, 4, 128], BF16)
        CQT = rot("cqT", 1, [128, 3, 512], BF16)
        GST = rot("gst", 1, [128, 8, 512], BF16)
        QBR = rot("qbr", 1, [128, 4, 512], BF16)
        QNT = rot("qnT", 1, [128, 4, 512], BF16)
        QPT = rot("qpT", 1, [128, 2, 512], BF16)
        st_sems = {}

        def stsem(name, slot):
            key = (name, slot)
            if key not in st_sems:
                st_sems[key] = kb.dsem("st_%s%d" % (name, slot))
            return st_sems[key]

        for v in VST.items:
            kb.op(pool, ("memset", dict(ap=v.t[:], constant=1.0)), writes=[v])
        PSB = Rot(banks)

        nblk = S // 512
        blist = list(range(nblk + 1))
        if os.environ.get("P1_BLOCKS"):
            blist = [int(v) for v in os.environ["P1_BLOCKS"].split(",") if v != "x"]

        loaded = {}

        def issue_load(bi):
            xin, tbl, lsem = XIN.next(), TBL.next(), ld_sem.next()
            if bi == nblk:
                src = P["ctx"].rearrange("(t p) f -> p t f", p=128)
                kb.dma(lsem, [(xin.t[:, 0:2, :], src[:, :, :], [], [xin])])
            else:
                src = P["x_all"].rearrange("(t p) f -> p t f", p=128)
                tsrc = P["tabs"].rearrange("k p t -> p k t")
                kb.dma(lsem, [(xin.t[:], src[:, bi * 4:bi * 4 + 4, :], [], [xin]),
                              (tbl.t[:], tsrc[:, :, bi * 512:bi * 512 + 512], [], [tbl])])
            loaded[bi] = (xin, tbl)

        def do_block(bi):
            is_ctx = (bi == nblk)
            nt = 256 if is_ctx else 512
            ntt = nt // 128
            do_q = (not is_ctx) and bi < SQ // 512
            k0 = bi * 512
            mc = 1 if is_ctx else 0
            slot = bi % 2
            if bi not in loaded:
                issue_load(bi)
            xin, tbl = loaded.pop(bi)
            TBc, TBs, TAc, TAs = (tbl.t[:, i, :] for i in range(4))
            xmT = XMT.next()
            for j in range(8):
                pb = PSB.next()
                kb.group(pe, [("transpose", dict(
                    out=pb.ap[:, tt * 128:(tt + 1) * 128], in_=xin.t[:, tt, j * 128:(j + 1) * 128],
                    identity=ident.t[:])) for tt in range(ntt)], reads=[xin, ident], writes=[pb])
                kb.op(act, ("activation", dict(
                    out=xmT.t[:, j, 0:nt], in_=pb.ap[:, 0:nt], func=AF.Identity,
                    scale=modc.t[:, 8 + j, mc:mc + 1], bias=modc.t[:, j, mc:mc + 1])),
                    reads=[pb, modc], writes=[xmT])

            nxt = blist[blist.index(bi) + 1] if blist.index(bi) + 1 < len(blist) else None
            if nxt is not None:
                issue_load(nxt)

            def proj(off, M):
                pb = PSB.next()
                kb.group(pe, [("matmul", dict(out=pb.ap[0:M, 0:nt], lhsT=W0.t[:, j, off:off + M], rhs=xmT.t[:, j, 0:nt],
                    start=(j == 0), stop=(j == 7))) for j in range(8)], reads=[W0, xmT], writes=[pb])
                return pb

            def rstd_from(pb, npart, ncol, scale, dst_ap, dst):
                kb.op(act, ("activation", dict(out=dst_ap, in_=pb.ap[0:npart, 0:ncol], func=AF.Ln,
                                                  scale=scale, bias=epst.t[0:npart, 0:1])),
                      reads=[pb, epst], writes=[dst])
                kb.op(act, ("activation", dict(out=dst_ap, in_=dst_ap, func=AF.Exp, scale=-0.5)),
                      reads=[dst], writes=[dst])

            ckvT = CKVT.next()
            sq = SQ2.next()
            for r in range(2):
                pb = proj(O_CKV + r * 128, 128)
                kb.op(dve, ("tensor_copy", dict(out=ckvT.t[:, r, 0:nt], in_=pb.ap[:, 0:nt])),
                      reads=[pb], writes=[ckvT])
                kb.op(act, ("activation", dict(out=sq.t[:, r, 0:nt], in_=pb.ap[:, 0:nt],
                                                              func=AF.Square)), reads=[pb], writes=[sq])
            pbR = PSB.next()
            kb.group(pe, [("matmul", dict(out=pbR.ap[:, 0:nt], lhsT=ones_bf.t[:], rhs=sq.t[:, r, 0:nt], start=(r == 0), stop=(r == 1)))
                for r in range(2)], reads=[ones_bf, sq], writes=[pbR])
            rkv = RBC.next()
            rstd_from(pbR, 128, nt, 1.0 / 256, rkv.t[:, 0:nt], rkv)
            pbc = PSB.next()
            fns = []
            for tt in range(ntt):
                for r in range(2):
                    fns.append(("matmul", dict(out=pbc.ap[:, tt:tt + 1], lhsT=sq.t[:, r, tt * 128:(tt + 1) * 128],
                        rhs=ones_bf.t[:, 0:1], start=(r == 0), stop=(r == 1))))
            kb.group(pe, fns, reads=[sq, ones_bf], writes=[pbc])
            rcol = RCOL.next()
            rstd_from(pbc, 128, ntt, 1.0 / 256, rcol.t[:, 0:ntt], rcol)
            knT = KNT.next()
            for c in range(4):
                pb = PSB.next()
                kb.group(pe, [("matmul", dict(out=pb.ap[:, 0:nt], lhsT=wukv.t[:, r, c * 128:(c + 1) * 128], rhs=ckvT.t[:, r, 0:nt],
                    start=(r == 0), stop=(r == 1))) for r in range(2)], reads=[wukv, ckvT], writes=[pb])
                kb.op(dve, ("tensor_tensor", dict(
                    out=knT.t[:, c, 0:nt], in0=pb.ap[:, 0:nt], in1=rkv.t[:, 0:nt], op=ALU.mult)),
                    reads=[pb, rkv], writes=[knT])
            kb.dma(stsem("kn", slot), [(P["KnTd"].rearrange("(c p) t -> p c t", p=128)[:, :, k0:k0 + nt],
                                        knT.t[:, :, 0:nt], [knT], [])])
            vst = VST.next()
            for tt in range(ntt):
                pb = PSB.next()
                kb.group(pe, [("matmul", dict(out=pb.ap[:, 0:512], lhsT=ckvT.t[:, r, tt * 128:(tt + 1) * 128], rhs=wukv.t[:, r, 512:1024],
                    start=(r == 0), stop=(r == 1))) for r in range(2)], reads=[wukv, ckvT], writes=[pb])
                kb.op(dve, ("tensor_scalar", dict(
                    out=vst.t[:, 0:8, tt, 0:64], in0=pb.ap[:, 0:512].rearrange("p (h e) -> p h e", e=64),
                    scalar1=rcol.t[:, tt:tt + 1], scalar2=None, op0=ALU.mult)),
                    reads=[pb, rcol], writes=[vst])
            for tt in range(ntt):
                pb = PSB.next()
                kb.group(pe, [("matmul", dict(out=pb.ap[:, 0:128], lhsT=xmT.t[:, j, tt * 128:(tt + 1) * 128],
                    rhs=W0.t[:, j, O_VB:O_VB + 128], start=(j == 0), stop=(j == 7))) for j in range(8)],
                    reads=[W0, xmT], writes=[pb])
                kb.op(act, ("activation", dict(
                    out=vst.t[:, 8:10, tt, 0:64], in_=pb.ap[:, 0:128].rearrange("p (h e) -> p h e", e=64),
                    func=AF.Copy)), reads=[pb], writes=[vst])
            kt0 = k0 // 128
            kb.dma(stsem("v", slot), [(P["Vd"].rearrange("h p k e -> p h k e")[:, :, kt0:kt0 + ntt, :],
                                       vst.t[:, :, 0:ntt, :], [vst], [])])
            kpe = KPE.next()
            pb1 = proj(O_KR, 32)
            if is_ctx:
                kb.op(dve, ("tensor_copy", dict(out=kpe.t[:, 0:nt], in_=pb1.ap[0:32, 0:nt])),
                      reads=[pb1], writes=[kpe])
            else:
                pb2 = proj(O_KRP, 32)
                t1, t2 = TMP.next(), TMP.next()
                kb.op(dve, ("tensor_tensor", dict(out=t1.t[0:32, :], in0=pb1.ap[0:32, :], in1=TAc[0:32, :],
                                                     op=ALU.mult)), reads=[pb1, tbl], writes=[t1])
                kb.op(dve, ("tensor_tensor", dict(out=t2.t[0:32, :], in0=pb2.ap[0:32, :], in1=TAs[0:32, :],
                                                     op=ALU.mult)), reads=[pb2, tbl], writes=[t2])
                kb.op(pool, ("tensor_tensor", dict(out=kpe.t[:, :], in0=t1.t[0:32, :], in1=t2.t[0:32, :],
                                                      op=ALU.add)), reads=[t1, t2], writes=[kpe])
            kb.dma(stsem("kp", slot), [(P["KpTd"][:, k0:k0 + nt], kpe.t[:, 0:nt], [kpe], [])])

            def normrope(off, offp, gcol, dst_ap, dst, rope):
                pb1 = proj(off, 128)
                sqk = SQ2.next()
                kb.op(act, ("activation", dict(out=sqk.t[:, 0, 0:nt], in_=pb1.ap[:, 0:nt], func=AF.Square,
                                                  scale=ginv.t[:, gcol:gcol + 1])), reads=[pb1, ginv], writes=[sqk])
                pbR = PSB.next()
                kb.group(pe, [("matmul", dict(out=pbR.ap[:, 0:nt], lhsT=bdiag.t[:], rhs=sqk.t[:, 0, 0:nt],
                                                 start=True, stop=True))], reads=[bdiag, sqk], writes=[pbR])
                rr = RBC.next()
                rstd_from(pbR, 128, nt, 1.0 / 64, rr.t[:, 0:nt], rr)
                if not rope:
                    kb.op(dve, ("tensor_tensor", dict(out=dst_ap, in0=pb1.ap[:, 0:nt], in1=rr.t[:, 0:nt],
                                                         op=ALU.mult)), reads=[pb1, rr], writes=[dst])
                    return
                pb2 = proj(offp, 128)
                t1, t2 = TMP.next(), TMP.next()
                kb.op(dve, ("tensor_tensor", dict(out=t1.t[:], in0=pb1.ap[:, :], in1=TBc, op=ALU.mult)),
                      reads=[pb1, tbl], writes=[t1])
                kb.op(dve, ("tensor_tensor", dict(out=t2.t[:], in0=pb2.ap[:, :], in1=TBs, op=ALU.mult)),
                      reads=[pb2, tbl], writes=[t2])
                kb.op(pool, ("tensor_tensor", dict(out=t1.t[:], in0=t1.t[:], in1=t2.t[:], op=ALU.add)),
                      reads=[t1, t2], writes=[t1])
                kb.op(dve, ("tensor_tensor", dict(out=dst_ap, in0=t1.t[:], in1=rr.t[:], op=ALU.mult)),
                      reads=[t1, rr], writes=[dst])

            kbr = KBR.next()
            normrope(O_KB, O_KBP, 1, kbr.t[:, 0:nt], kbr, not is_ctx)
            kb.dma(stsem("kb", slot), [(P["KbTd"][:, k0:k0 + nt], kbr.t[:, 0:nt], [kbr], [])])

            if not do_q:
                return
            cqT = CQT.next()
            sq3 = SQ2.next()
            for r in range(3):
                pb = proj(O_CQ + r * 128, 128)
                kb.op(dve, ("tensor_copy", dict(out=cqT.t[:, r, :], in_=pb.ap[:, :])),
                      reads=[pb], writes=[cqT])
                kb.op(act, ("activation", dict(out=sq3.t[:, r, :], in_=pb.ap[:, :],
                                                              func=AF.Square)), reads=[pb], writes=[sq3])
            pbR = PSB.next()
            kb.group(pe, [("matmul", dict(out=pbR.ap[:, :], lhsT=ones_bf.t[:], rhs=sq3.t[:, r, :], start=(r == 0), stop=(r == 2)))
                for r in range(3)], reads=[ones_bf, sq3], writes=[pbR])
            rq = RQ.next()
            rstd_from(pbR, 128, 512, 1.0 / 384, rq.t[:], rq)
            gst = GST.next()
            for gi, off in enumerate((O_GA, O_GB)):
                for c in range(4):
                    pb = proj(off + c * 128, 128)
                    kb.op(act, ("activation", dict(
                        out=gst.t[:, gi * 4 + c, :], in_=pb.ap[:, :], func=AF.Silu)), reads=[pb], writes=[gst])
            kb.dma(stsem("g", slot), [(P["GTd"].rearrange("(c p) t -> p c t", p=128)[:, :, k0:k0 + 512],
                                       gst.t[:], [gst], [])])
            qbr = QBR.next()
            for c in range(4):
                normrope(O_QB + c * 128, O_QBP + c * 128, 0, qbr.t[:, c, :], qbr, True)
            kb.dma(stsem("qb", slot), [(P["QbTd"].rearrange("(c p) t -> p c t", p=128)[:, :, k0:k0 + 512],
                                        qbr.t[:], [qbr], [])])
            qnT = QNT.next()
            for c in range(4):
                pb = PSB.next()
                kb.group(pe, [("matmul", dict(out=pb.ap[:, :], lhsT=wuq.t[:, r, c * 128:(c + 1) * 128], rhs=cqT.t[:, r, :],
                    start=(r == 0), stop=(r == 2))) for r in range(3)], reads=[wuq, cqT], writes=[pb])
                kb.op(dve, ("tensor_tensor", dict(
                    out=qnT.t[:, c, :], in0=pb.ap[:, :], in1=rq.t[:], op=ALU.mult)),
                    reads=[pb, rq], writes=[qnT])
            kb.dma(stsem("qn", slot), [(P["QnTd"].rearrange("(c p) t -> p c t", p=128)[:, :, k0:k0 + 512],
                                        qnT.t[:], [qnT], [])])
            qpT = QPT.next()
            for c in range(2):
                pbs = []
                for base in (512, 768):
                    pb = PSB.next()
                    kb.group(pe, [("matmul", dict(out=pb.ap[:, :], lhsT=wuq.t[:, r, base + c * 128:base + (c + 1) * 128], rhs=cqT.t[:, r, :],
                        start=(r == 0), stop=(r == 2))) for r in range(3)], reads=[wuq, cqT], writes=[pb])
                    pbs.append(pb)
                t1, t2 = TMP.next(), TMP.next()
                kb.op(dve, ("tensor_tensor", dict(out=t1.t[:], in0=pbs[0].ap[:, :], in1=TAc,
                                                                       op=ALU.mult)), reads=[pbs[0], tbl], writes=[t1])
                kb.op(dve, ("tensor_tensor", dict(out=t2.t[:], in0=pbs[1].ap[:, :], in1=TAs,
                                                                       op=ALU.mult)), reads=[pbs[1], tbl], writes=[t2])
                kb.op(pool, ("tensor_tensor", dict(out=t1.t[:], in0=t1.t[:], in1=t2.t[:],
                                                                    op=ALU.add)), reads=[t1, t2], writes=[t1])
                kb.op(dve, ("tensor_tensor", dict(out=qpT.t[:, c, :], in0=t1.t[:], in1=rq.t[:],
                                                                 op=ALU.mult)), reads=[t1, rq], writes=[qpT])
            kb.dma(stsem("qp", slot), [(P["QpTd"].rearrange("(c p) t -> p c t", p=128)[:, :, k0:k0 + 512],
                                        qpT.t[:], [qpT], [])])

        for bi in blist:
            do_block(bi)
        kb.flush()


def phase2(P):
    kb, nc, sb, PS = P["kb"], P["nc"], P["sb"], P["PS"]
    pe, act, dve, pool = kb.pe, kb.act, kb.dve, kb.pool
    NB = 3
    with ExitStack() as p2:
        KT = [sb("KT%d" % i, [128, NK], BF16, p2) for i in range(2)]
        VV = [sb("VV%d" % i, [128, NKT, 128], BF16, p2) for i in range(2)]
        QT = [sb("QT%d" % i, [128, SQ], BF16, p2) for i in range(2)]
        GT = [sb("GT%d" % i, [64, SQ], BF16, p2) for i in range(2)]
        hsem = [kb.dsem("hd%d" % i) for i in range(2)]
        PT = [sb("PT%d" % i, [128, 1024], BF16, p2) for i in range(NB)]
        OCP = [sb("ocp%d" % i, [128, 1024], F32, p2) for i in range(2)]
        BC = [sb("bc%d" % i, [64, 1024], F32, p2) for i in range(2)]
        bsem = [kb.dsem("bcs%d" % i) for i in range(2)]
        bsem2 = [kb.dsem("bcl%d" % i) for i in range(2)]
        SUMd = [Buf(), Buf()]
        M1 = sb("m1", [64, 1024], F32, p2)
        MST = [sb("mst%d" % i, [64, 1024], BF16, p2) for i in range(2)]
        msem = [kb.dsem("mst%d" % i) for i in range(2)]
        psS = [PBank(PS[:, i * 1024:(i + 1) * 1024]) for i in range(NB)]
        psO = PBank(PS[:, 3072:4096])
        sums_d = P["sums_d"]

        def load_head(h16):
            sl = h16 % 2
            kt, vv, qt, gt = KT[sl], VV[sl], QT[sl], GT[sl]
            items = []
            if h16 < 8:
                h = h16
                items.append((kt.t[0:64, :], P["KnTd"][h * 64:(h + 1) * 64, :], [], [kt]))
                items.append((kt.t[64:96, :], P["KpTd"][:, :], [], [kt]))
                items.append((vv.t[:], P["Vd"][h], [], [vv]))
                items.append((qt.t[0:64, :], P["QnTd"][h * 64:(h + 1) * 64, :], [], [qt]))
                items.append((qt.t[64:96, :], P["QpTd"][h * 32:(h + 1) * 32, :], [], [qt]))
            else:
                hb = h16 - 8
                kvh = hb // 4
                items.append((kt.t[0:64, :], P["KbTd"][kvh * 64:(kvh + 1) * 64, :], [], [kt]))
                items.append((vv.t[:], P["Vd"][8 + kvh], [], [vv]))
                items.append((qt.t[0:64, :], P["QbTd"][hb * 64:(hb + 1) * 64, :], [], [qt]))
            items.append((gt.t[:], P["GTd"][h16 * 64:(h16 + 1) * 64, :], [], [gt]))
            kb.dma(hsem[sl], items)

        iters = [(h16, sbk, kt) for h16 in range(16) for sbk in range(SQ // 1024) for kt in range(NKT)]
        N = len(iters)

        def emit_S(n):
            h16, sbk, kt = iters[n]
            sl = h16 % 2
            kd = 96 if h16 < 8 else 64
            ps = psS[n % NB]
            kb.group(pe, [("matmul", dict(out=ps.ap[:, hf * 512:(hf + 1) * 512],
                                          lhsT=KT[sl].t[0:kd, kt * 128:(kt + 1) * 128],
                                          rhs=QT[sl].t[0:kd, sbk * 1024 + hf * 512:sbk * 1024 + (hf + 1) * 512],
                                          start=True, stop=True)) for hf in range(2)],
                     reads=[KT[sl], QT[sl]], writes=[ps])

        def emit_exp(n):
            h16, sbk, kt = iters[n]
            sc = A_SCALE if h16 < 8 else B_SCALE
            kb.op(act, ("activation", dict(out=PT[n % NB].t[:], in_=psS[n % NB].ap[:, :], func=AF.Exp, scale=sc)),
                  reads=[psS[n % NB]], writes=[PT[n % NB]])

        def emit_PV(n):
            h16, sbk, kt = iters[n]
            sl = h16 % 2
            kb.group(pe, [("matmul", dict(out=psO.ap[:, hf * 512:(hf + 1) * 512], lhsT=VV[sl].t[:, kt, :],
                                          rhs=PT[n % NB].t[:, hf * 512:(hf + 1) * 512],
                                          start=(kt == 0), stop=(kt == NKT - 1))) for hf in range(2)],
                     reads=[VV[sl], PT[n % NB]], writes=[psO])

        ep = [0]

        def emit_epi1(n):
            e = ep[0]
            ocp = OCP[e % 2]
            kb.op(dve, ("tensor_copy", dict(out=ocp.t[:], in_=psO.ap[:, :])), reads=[psO], writes=[ocp])
            kb.dma(bsem[e % 2], [(sums_d[e % 2:e % 2 + 1, :], ocp.t[64:65, :], [ocp], [SUMd[e % 2]])])
            kb.dma(bsem2[e % 2], [(BC[e % 2].t[:], sums_d[e % 2, :].partition_broadcast(64), [SUMd[e % 2]], [BC[e % 2]])])

        def emit_epi2(n):
            h16, sbk, kt = iters[n]
            sl = h16 % 2
            e = ep[0]
            ep[0] += 1
            ocp, mst, bc = OCP[e % 2], MST[e % 2], BC[e % 2]
            kb.op(dve, ("reciprocal", dict(out=bc.t[:], in_=bc.t[:])), reads=[bc], writes=[bc])
            kb.op(dve, ("tensor_tensor", dict(out=M1.t[:], in0=ocp.t[0:64, :], in1=bc.t[:], op=ALU.mult)),
                  reads=[ocp, bc], writes=[M1])
            kb.op(dve, ("tensor_tensor", dict(out=mst.t[:], in0=M1.t[:],
                                              in1=GT[sl].t[:, sbk * 1024:(sbk + 1) * 1024], op=ALU.mult)),
                  reads=[M1, GT[sl]], writes=[mst])
            kb.dma(msem[e % 2], [(P["MTd"][h16 * 64:(h16 + 1) * 64, sbk * 1024:(sbk + 1) * 1024], mst.t[:],
                                  [mst], [])])
            if sbk == SQ // 1024 - 1 and h16 + 2 < 16:
                load_head(h16 + 2)

        load_head(0)
        load_head(1)
        pending = []
        emit_S(0)
        emit_S(1)
        for n in range(N):
            h16, sbk, kt = iters[n]
            if n + 2 < N:
                emit_S(n + 2)
            emit_exp(n)
            emit_PV(n)
            if kt == NKT - 1:
                emit_epi1(n)
                pending.append((n + 6, n))
            if pending and (pending[0][0] <= n or n == N - 1):
                emit_epi2(pending.pop(0)[1])
        while pending:
            emit_epi2(pending.pop(0)[1])
        kb.flush()


def phase3(P):
    kb, nc, sb, PS = P["kb"], P["nc"], P["sb"], P["PS"]
    ident, epst = P["ident"], P["epst"]
    modc1 = P["modc"][1]
    gate0, gate1 = P["gate_bc"]
    pe, act, dve, pool = kb.pe, kb.act, kb.dve, kb.pool
    with ExitStack() as p3:
        w0o = sb("w0o", [128, 8, D], BF16, p3)
        w1i = sb("w1i", [128, 8, 3 * D], BF16, p3)
        w1o = sb("w1o", [128, 8, D], BF16, p3)
        wsT = sb("wsT", [128, 8, 128], BF16, p3)
        bct = {}
        for nm in ("lng0", "lnb0", "lng1", "lnb1", "vlng", "vlnb"):
            bct[nm] = sb("bc_" + nm, [128, D], F32, p3)
        bsb = sb("bsb", [128, 8, 128], F32, p3)
        sem_c = kb.dsem("p3c")
        kb.dma(sem_c, [(bct[nm].t[:], P[nm + "_d"].partition_broadcast(128), [], [bct[nm]]) for nm in bct] +
               [(bsb.t[:], P["bs_d"].partition_broadcast(128), [], [bsb])])
        p3a = ExitStack()
        wst = [sb("wst3_%d" % i, [128, 3 * D], F32, p3a) for i in range(2)]
        wsem = [kb.dsem("wst3_%d" % i) for i in range(2)]
        k = 0
        for j in range(8):
            st = wst[k % 2]
            kb.dma(wsem[k % 2], [(st.t[:], P["w1in_d"][j * 128:(j + 1) * 128, :], [], [st])])
            kb.op(act if j % 2 == 0 else dve, ("activation" if j % 2 == 0 else "tensor_copy",
                                               dict(out=w1i.t[:, j, :], in_=st.t[:], func=AF.Copy) if j % 2 == 0
                                               else dict(out=w1i.t[:, j, :], in_=st.t[:])),
                  reads=[st], writes=[w1i])
            k += 1
        for (dst, src, gt) in ((w0o, "wout0_d", gate0), (w1o, "wout1_d", gate1)):
            for j2 in range(4):
                st = wst[k % 2]
                kb.dma(wsem[k % 2], [(st.t[:, 0:2048].rearrange("p (a n) -> p a n", a=2),
                                      P[src][j2 * 256:(j2 + 1) * 256, :].rearrange("(a p) n -> p a n", p=128),
                                      [], [st])])
                for a in range(2):
                    kb.op(dve if a == 0 else pool, ("tensor_tensor", dict(
                        out=dst.t[:, j2 * 2 + a, :], in0=st.t[:, a * 1024:(a + 1) * 1024], in1=gt.t[:], op=ALU.mult)),
                        reads=[st, gt], writes=[dst])
                k += 1
        st = wst[k % 2]
        kb.dma(wsem[k % 2], [(st.t[:, 0:1024].rearrange("p (g q) -> p g q", g=8),
                              P["wsT_d"].rearrange("g q p -> q g p"), [], [st])])
        kb.op(dve, ("tensor_copy", dict(out=wsT.t[:], in_=st.t[:, 0:1024].rearrange("p (g q) -> p g q", g=8))),
              reads=[st], writes=[wsT])
        kb.flush()
        p3a.close()

        def rot(name, n, shape, dt):
            return Rot([sb("%s%d" % (name, i), shape, dt, p3) for i in range(n)])

        MTB = rot("mtb", 1, [128, 8, 512], BF16)
        mt_sem = Rot([kb.dsem("mtb%d" % i) for i in range(2)])
        XR = rot("xr", 2, [128, D], F32)
        xr_sem = Rot([kb.dsem("xr%d" % i) for i in range(2)])
        X1 = rot("x1", 1, [128, 4, D], F32)
        XMT = rot("x1mT", 1, [128, 8, 512], BF16)
        UG = rot("ug", 1, [128, 8, 512], F32)
        WK = rot("wk", 3, [128, D], F32)
        TS = rot("tsil", 2, [128, 512], F32)
        STT = rot("stt", 2, [128, 2, 6], F32)
        MV = rot("mv", 2, [128, 2], F32)
        RS = rot("rs", 2, [128, 1], F32)
        VLN = rot("vln", 2, [128, D], BF16)
        ZT = rot("zT", 2, [128, 8, 128], BF16)
        OUT = rot("outt", 2, [128, D], F32)
        out_sem = Rot([kb.dsem("out%d" % i) for i in range(2)])
        PS2 = Rot([PBank(PS[:, 0:1024]), PBank(PS[:, 1024:2048]), PBank(PS[:, 2048:3072])])
        PS1 = Rot([PBank(PS[:, 3072:3584]), PBank(PS[:, 3584:4096])])

        def layer_norm(src, dst_ap, dst, g, b):
            stt, mv, rs = STT.next(), MV.next(), RS.next()
            kb.group(dve, [("bn_stats", dict(out=stt.t[:, hh, :], in_=src.t[:, hh * 512:(hh + 1) * 512]))
                           for hh in range(2)], reads=[src], writes=[stt])
            kb.op(dve, ("bn_aggr", dict(out=mv.t[:], in_=stt.t[:])), reads=[stt], writes=[mv])
            kb.op(act, ("activation", dict(out=rs.t[:], in_=mv.t[:, 1:2], func=AF.Ln, bias=epst.t[:, 0:1])),
                  reads=[mv, epst], writes=[rs])
            kb.op(act, ("activation", dict(out=rs.t[:], in_=rs.t[:], func=AF.Exp, scale=-0.5)),
                  reads=[rs], writes=[rs])
            kb.op(dve, ("tensor_scalar", dict(out=src.t[:], in0=src.t[:], scalar1=mv.t[:, 0:1], scalar2=rs.t[:, 0:1],
                                              op0=ALU.subtract, op1=ALU.mult)), reads=[src, mv, rs], writes=[src])
            kb.op(pool, ("tensor_tensor", dict(out=src.t[:], in0=src.t[:], in1=g.t[:], op=ALU.mult)),
                  reads=[src, g], writes=[src])
            kb.op(pool, ("tensor_tensor", dict(out=dst_ap, in0=src.t[:], in1=b.t[:], op=ALU.add)),
                  reads=[src, b], writes=[dst])

        for bi in range(SQ // 512):
            k0 = bi * 512
            mt, msem = MTB.next(), mt_sem.next()
            kb.dma(msem, [(mt.t[:], P["MTd"].rearrange("(c p) t -> p c t", p=128)[:, :, k0:k0 + 512], [], [mt])])
            x1 = X1.next()
            for tt in range(4):
                xr, xsem = XR.next(), xr_sem.next()
                r0 = k0 + tt * 128
                kb.dma(xsem, [(xr.t[:], P["x_all"][r0:r0 + 128, :], [], [xr])])
                psY = PS2.next()
                kb.group(pe, [("matmul", dict(out=psY.ap[:, cb * 512:(cb + 1) * 512],
                                              lhsT=mt.t[:, c, tt * 128:(tt + 1) * 128],
                                              rhs=w0o.t[:, c, cb * 512:(cb + 1) * 512], start=(c == 0), stop=(c == 7)))
                              for cb in range(2) for c in range(8)], reads=[mt, w0o], writes=[psY])
                r = WK.next()
                kb.op(dve, ("scalar_tensor_tensor", dict(out=r.t[:], in0=xr.t[:], scalar=ALPHA, in1=psY.ap[:, :],
                                                         op0=ALU.mult, op1=ALU.add)), reads=[xr, psY], writes=[r])
                layer_norm(r, x1.t[:, tt, :], x1, bct["lng0"], bct["lnb0"])
            xmT = XMT.next()
            for j in range(8):
                pb = PS1.next()
                kb.group(pe, [("transpose", dict(out=pb.ap[:, tt * 128:(tt + 1) * 128],
                                                 in_=x1.t[:, tt, j * 128:(j + 1) * 128], identity=ident.t[:]))
                              for tt in range(4)], reads=[x1, ident], writes=[pb])
                kb.op(act, ("activation", dict(out=xmT.t[:, j, :], in_=pb.ap[:, :], func=AF.Identity,
                                               scale=modc1.t[:, 8 + j, 0:1], bias=modc1.t[:, j, 0:1])),
                      reads=[pb, modc1], writes=[xmT])
            ug = UG.next()
            for c in range(8):
                pb = PS1.next()
                kb.group(pe, [("matmul", dict(out=pb.ap[:, :], lhsT=w1i.t[:, j, c * 128:(c + 1) * 128],
                                              rhs=xmT.t[:, j, :], start=(j == 0), stop=(j == 7))) for j in range(8)],
                         reads=[w1i, xmT], writes=[pb])
                kb.op(act, ("activation", dict(out=ug.t[:, c, :], in_=pb.ap[:, :], func=AF.Gelu_apprx_tanh)),
                      reads=[pb], writes=[ug])
            for c in range(8):
                pb = PS1.next()
                kb.group(pe, [("matmul", dict(out=pb.ap[:, :], lhsT=w1i.t[:, j, 2048 + c * 128:2048 + (c + 1) * 128],
                                              rhs=xmT.t[:, j, :], start=(j == 0), stop=(j == 7))) for j in range(8)],
                         reads=[w1i, xmT], writes=[pb])
                ts = TS.next()
                kb.op(act, ("activation", dict(out=ts.t[:], in_=pb.ap[:, :], func=AF.Silu)), reads=[pb], writes=[ts])
                kb.op(pool, ("tensor_tensor", dict(out=ug.t[:, c, :], in0=ug.t[:, c, :], in1=ts.t[:], op=ALU.mult)),
                      reads=[ug, ts], writes=[ug])
            for tt in range(4):
                psV = PS2.next()
                kb.group(pe, [("matmul", dict(out=psV.ap[:, cb * 512:(cb + 1) * 512],
                                              lhsT=xmT.t[:, j, tt * 128:(tt + 1) * 128],
                                              rhs=w1i.t[:, j, 1024 + cb * 512:1536 + cb * 512],
                                              start=(j == 0), stop=(j == 7))) for cb in range(2) for j in range(8)],
                         reads=[xmT, w1i], writes=[psV])
                vg = WK.next()
                kb.op(act, ("activation", dict(out=vg.t[:], in_=psV.ap[:, :], func=AF.Gelu_apprx_tanh)),
                      reads=[psV], writes=[vg])
                vln = VLN.next()
                layer_norm(vg, vln.t[:], vln, bct["vlng"], bct["vlnb"])
                psM = PS2.next()
                kb.group(pe, [("matmul", dict(out=psM.ap[:, g * 128:(g + 1) * 128], lhsT=vln.t[:, g * 128:(g + 1) * 128],
                                              rhs=wsT.t[:, g, :], start=True, stop=True)) for g in range(8)],
                         reads=[vln, wsT], writes=[psM])
                tz = WK.next()
                kb.op(dve, ("tensor_tensor", dict(out=tz.t[:].rearrange("p (g q) -> p g q", g=8),
                                                  in0=psM.ap[:, :].rearrange("p (g q) -> p g q", g=8),
                                                  in1=bsb.t[:], op=ALU.add)), reads=[psM, bsb], writes=[tz])
                zT = ZT.next()
                kb.op(dve, ("tensor_tensor", dict(out=zT.t[:], in0=tz.t[:].rearrange("p (g q) -> p g q", g=8),
                                                  in1=ug.t[:, :, tt * 128:(tt + 1) * 128], op=ALU.mult)),
                      reads=[tz, ug], writes=[zT])
                psY1 = PS2.next()
                kb.group(pe, [("matmul", dict(out=psY1.ap[:, cb * 512:(cb + 1) * 512], lhsT=zT.t[:, g, :],
                                              rhs=w1o.t[:, g, cb * 512:(cb + 1) * 512], start=(g == 0), stop=(g == 7)))
                              for cb in range(2) for g in range(8)], reads=[zT, w1o], writes=[psY1])
                r1 = WK.next()
                kb.op(dve, ("scalar_tensor_tensor", dict(out=r1.t[:], in0=x1.t[:, tt, :], scalar=ALPHA,
                                                         in1=psY1.ap[:, :], op0=ALU.mult, op1=ALU.add)),
                      reads=[x1, psY1], writes=[r1])
                ot, osem = OUT.next(), out_sem.next()
                layer_norm(r1, ot.t[:], ot, bct["lng1"], bct["lnb1"])
                r0 = k0 + tt * 128
                kb.dma(osem, [(P["out_d"][r0:r0 + 128, :], ot.t[:], [ot], [])])
        kb.flush()


def _partner(d):
    q = d // 4
    i = np.arange(d)
    r = i % (d // 2)
    partner = np.where(r < q, i + q, i - q)
    sign = np.where(r < q, -1.0, 1.0).astype(np.float32)
    return partner, sign


def _rope_tables(pos):
    out = []
    row = (pos // 64).astype(np.float32)
    col = (pos % 64).astype(np.float32)
    for d, rep in ((64, 2), (32, 4)):
        d_axis = d // 2
        inv = (np.float32(10000.0) ** (-np.arange(0, d_axis, 2, dtype=np.float32) / np.float32(d_axis))).astype(np.float32)
        _, sign = _partner(d)
        i = np.arange(d)
        axis = i // d_axis
        j = i % (d // 4)
        p = np.where(axis[:, None] == 0, row[None, :], col[None, :]).astype(np.float32)
        ang = (p * inv[j][:, None]).astype(np.float32)
        c = np.cos(ang).astype(np.float32)
        s = (np.sin(ang).astype(np.float32) * sign[:, None]).astype(np.float32)
        out.append(np.tile(c, (rep, 1)))
        out.append(np.tile(s, (rep, 1)))
    return np.ascontiguousarray(np.stack(out, 0))


def make_in_maps(inp):
    f = lambda a: np.ascontiguousarray(np.asarray(a, dtype=np.float32))
    x = f(inp["x"]); c = f(inp["c"]); ctx = f(inp["ctx"]); c_ctx = f(inp["c_ctx"])
    e_w_in = f(inp["e_w_in"])[0]
    p64, _ = _partner(64)
    p32, _ = _partner(32)
    h8 = np.arange(8)[:, None]
    cols = np.concatenate([
        384 + np.arange(256), 640 + np.arange(32), 640 + p32,
        1696 + np.arange(128), 1696 + (np.arange(2)[:, None] * 64 + p64[None, :]).reshape(-1),
        1824 + np.arange(128), np.arange(384), 672 + np.arange(512),
        1184 + np.arange(512), 1184 + (h8 * 64 + p64[None, :]).reshape(-1), 1952 + np.arange(512)])
    assert cols.shape[0] == NC0
    w0 = np.ascontiguousarray(e_w_in[:, cols])
    bq = f(inp["e_b_q_norm"])[0]; bk = f(inp["e_b_k_norm"])[0]
    one = lambda n: np.ones(n, np.float32)
    gv0 = np.concatenate([one(256), one(32), one(32), np.tile(bk, 2), np.tile(bk[p64], 2), one(128), one(384),
                          one(512), np.tile(bq, 8), np.tile(bq[p64], 8), one(512)]).astype(np.float32)
    w_uq = f(inp["e_a_w_uq"])[0]
    cu = np.concatenate([(h8 * 96 + np.arange(64)[None, :]).reshape(-1),
                         (h8 * 96 + 64 + np.arange(32)[None, :]).reshape(-1),
                         (h8 * 96 + 64 + p32[None, :]).reshape(-1)])
    wuq = np.ascontiguousarray(w_uq[:, cu])
    w_ukv = f(inp["e_a_w_ukv"])[0]
    ck = np.concatenate([(h8 * 128 + np.arange(64)[None, :]).reshape(-1),
                         (h8 * 128 + 64 + np.arange(64)[None, :]).reshape(-1)])
    wukv = np.ascontiguousarray(w_ukv[:, ck])
    aqc = np.ascontiguousarray(f(inp["e_a_q_norm"])[0].reshape(3, 128).T)
    akvc = np.ascontiguousarray(f(inp["e_a_kv_norm"])[0].reshape(2, 128).T)
    bqk = np.ascontiguousarray(np.stack([np.tile(bq, 2), np.tile(bk, 2)], 1))
    shared = dict(
        ident=np.eye(128, dtype=np.float32),
        wmod0=f(inp["e_w_mod"])[0], wmod1=f(inp["o_w_mod"])[0],
        bmodc0=np.ascontiguousarray(f(inp["e_b_mod"])[0].reshape(24, 128).T),
        bmodc1=np.ascontiguousarray(f(inp["o_b_mod"])[0].reshape(24, 128).T),
        bmodg0=np.ascontiguousarray(f(inp["e_b_mod"])[0][2048:]), bmodg1=np.ascontiguousarray(f(inp["o_b_mod"])[0][2048:]),
        w0=w0, gv0=gv0, wuq=wuq, aqc=aqc, wukv=wukv, akvc=akvc, bqk=bqk,
        wout0=f(inp["e_w_out"])[0], lng0=f(inp["e_ln_g"])[0], lnb0=f(inp["e_ln_b"])[0],
        w1in=f(inp["o_w_in"])[0], vlng=f(inp["o_v_ln_g"])[0], vlnb=f(inp["o_v_ln_b"])[0],
        wsT=np.ascontiguousarray(np.transpose(f(inp["o_w_s"])[0], (0, 2, 1))),
        bs=np.ascontiguousarray(f(inp["o_b_s"])[0].reshape(-1)),
        wout1=f(inp["o_w_out"])[0], lng1=f(inp["o_ln_g"])[0], lnb1=f(inp["o_ln_b"])[0],
    )
    maps = []
    for core in range(8):
        b, half = core // 2, core % 2
        order = np.concatenate([np.arange(half * SQ, (half + 1) * SQ), np.arange((1 - half) * SQ, (2 - half) * SQ)])
        m = dict(shared)
        m["x_all"] = np.ascontiguousarray(x[b][order])
        m["ctx"] = ctx[b]
        m["cc"] = np.ascontiguousarray(np.stack([c[b].reshape(8, 128).T, c_ctx.reshape(8, 128).T], 2))
        m["tabs"] = _rope_tables(order)
        maps.append(m)
    return maps


_CACHE = {}


def kernel(**inputs):
    if "nc" not in _CACHE:
        _CACHE["nc"] = build()
    nc = _CACHE["nc"]
    maps = make_in_maps(inputs)
    res = run_bass_kernel_spmd(nc, maps, core_ids=list(range(8)))
    out = np.empty((4, S, D), np.float32)
    for core in range(8):
        b, half = core // 2, core % 2
        out[b, half * SQ:(half + 1) * SQ] = res.results[core]["out"]
    return out
```

```python
import os
import numpy as np
from contextlib import ExitStack
import concourse.bass as bass
import concourse.mybir as mybir
from concourse.bass_utils import run_bass_kernel_spmd

F32 = mybir.dt.float32
BF16 = mybir.dt.bfloat16
AF = mybir.ActivationFunctionType
ALU = mybir.AluOpType

S = 8192
SQ = 4096
D = 1024
CTX = 256
NK = S + CTX
NKT = NK // 128
EPS = 1e-6
ALPHA = 4.0 ** 0.25
A_SCALE = 96.0 ** -0.5
B_SCALE = 64.0 ** -0.5

O_CKV, O_KR, O_KRP, O_KB, O_KBP, O_VB, O_CQ, O_GA, O_QB, O_QBP, O_GB = (
    0, 256, 288, 320, 448, 576, 704, 1088, 1600, 2112, 2624)
NC0 = 3136


class Sem:
    def __init__(self, nc, es, name):
        self.h = es.enter_context(nc.semaphore(name))
        self.name = name
        self.count = 0


class Buf:
    def __init__(self):
        self.w = {}
        self.r = {}


def _merge(d, ev):
    s, v = ev
    if s.name not in d or d[s.name][1] < v:
        d[s.name] = (s, v)


class T(Buf):
    def __init__(self, t):
        Buf.__init__(self)
        self.t = t


class Eng:
    def __init__(self, name, sem):
        self.name = name
        self.sem = sem
        self.q = []
        self.seen = {}

    def wait(self, ev):
        s, v = ev
        if v <= 0 or self.seen.get(s.name, 0) >= v:
            return
        self.seen[s.name] = v
        self.q.append(("wait", s, v))


class KB:
    def __init__(self, nc, es):
        self.nc = nc
        self.es = es
        self.nsem = 0
        self.pe = Eng("pe", self.sem("pe"))
        self.act = Eng("act", self.sem("act"))
        self.dve = Eng("dve", self.sem("dve"))
        self.pool = Eng("pool", self.sem("pool"))
        self.sp = Eng("sp", None)
        self.dma_sems = []

    def sem(self, name):
        self.nsem += 1
        return Sem(self.nc, self.es, "%s_%d" % (name, self.nsem))

    def dsem(self, name):
        s = self.sem(name)
        self.dma_sems.append(s)
        return s

    def _deps(self, eng, reads, writes):
        for b in reads:
            for ev in list(b.w.values()):
                eng.wait(ev)
            if isinstance(b, PBank):
                for ev in list(b.r.values()):
                    if eng.sem is None or ev[0].name != eng.sem.name:
                        eng.wait(ev)
        for b in writes:
            for ev in list(b.w.values()) + list(b.r.values()):
                eng.wait(ev)

    def op(self, eng, fn, reads=(), writes=()):
        self._deps(eng, reads, writes)
        eng.sem.count += 1
        ev = (eng.sem, eng.sem.count)
        eng.q.append(("op", fn, eng.sem))
        for b in reads:
            _merge(b.r, ev)
        for b in writes:
            _merge(b.w, ev)
        return ev

    def group(self, eng, fns, reads=(), writes=()):
        self._deps(eng, reads, writes)
        for fn in fns[:-1]:
            eng.q.append(("op", fn, None))
        eng.sem.count += 1
        ev = (eng.sem, eng.sem.count)
        eng.q.append(("op", fns[-1], eng.sem))
        for b in reads:
            _merge(b.r, ev)
        for b in writes:
            _merge(b.w, ev)
        return ev

    def dma(self, sem, items, q=None):
        q = q or self.sp
        for (_, _, reads, writes) in items:
            self._deps(q, reads, writes)
        for (o, i, _, _) in items:
            sem.count += 16
            q.q.append(("dma", o, i, sem))
        ev = (sem, sem.count)
        for (_, _, reads, writes) in items:
            for b in reads:
                _merge(b.r, ev)
            for b in writes:
                _merge(b.w, ev)
        return ev

    def flush(self, final_waits=True):
        nc = self.nc
        if final_waits:
            for s in self.dma_sems:
                self.sp.wait((s, s.count))

        def replay(q, e):
            for it in q:
                if it[0] == "wait":
                    e.wait_ge(it[1].h, it[2])
                elif it[0] == "op":
                    ins = getattr(e, it[1][0])(**it[1][1])
                    if it[2] is not None:
                        ins.then_inc(it[2].h, 1)
                else:
                    e.dma_start(out=it[1], in_=it[2]).then_inc(it[3].h, 16)

        with nc.Block() as blk:
            @blk.sync
            def _(e):
                replay(self.sp.q, e)

            @blk.tensor
            def _(e):
                replay(self.pe.q, e)

            @blk.scalar
            def _(e):
                replay(self.act.q, e)

            @blk.vector
            def _(e):
                replay(self.dve.q, e)

            @blk.gpsimd
            def _(e):
                replay(self.pool.q, e)
        for e in (self.sp, self.pe, self.act, self.dve, self.pool):
            e.q = []


class Rot:
    def __init__(self, items):
        self.items = items
        self.i = 0

    def next(self):
        it = self.items[self.i % len(self.items)]
        self.i += 1
        return it


class PBank(Buf):
    def __init__(self, ap):
        Buf.__init__(self)
        self.ap = ap


def build(stop_after=99, dbg=False):
    nc = bass.Bass("TRN2", target_bir_lowering=False)
    es = ExitStack()
    with es:
        def din(name, shape, dt=F32):
            return nc.dram_tensor(name, list(shape), dt, kind="ExternalInput").ap()

        def dscr(name, shape, dt=BF16):
            kind = "ExternalOutput" if dbg else "Internal"
            return nc.dram_tensor(name, list(shape), dt, kind=kind).ap()

        x_all = din("x_all", [S, D])
        ctx = din("ctx", [CTX, D])
        cc_d = din("cc", [128, 8, 2])
        ident_d = din("ident", [128, 128])
        tabs = din("tabs", [4, 128, S])
        wmod_d = [din("wmod0", [D, 3 * D]), din("wmod1", [D, 3 * D])]
        bmodc_d = [din("bmodc0", [128, 24]), din("bmodc1", [128, 24])]
        bmodg_d = [din("bmodg0", [D]), din("bmodg1", [D])]
        w0_d = din("w0", [D, NC0])
        gv0_d = din("gv0", [NC0])
        wuq_d = din("wuq", [384, 1024])
        aqc_d = din("aqc", [128, 3])
        wukv_d = din("wukv", [256, 1024])
        akvc_d = din("akvc", [128, 2])
        bqk_d = din("bqk", [128, 2])
        wout0_d = din("wout0", [D, D])
        lng0_d = din("lng0", [D]); lnb0_d = din("lnb0", [D])
        w1in_d = din("w1in", [D, 3 * D])
        vlng_d = din("vlng", [D]); vlnb_d = din("vlnb", [D])
        wsT_d = din("wsT", [8, 128, 128])
        bs_d = din("bs", [D])
        wout1_d = din("wout1", [D, D])
        lng1_d = din("lng1", [D]); lnb1_d = din("lnb1", [D])
        out_d = nc.dram_tensor("out", [SQ, D], F32, kind="ExternalOutput").ap()

        KnTd = dscr("KnTd", [512, NK])
        KpTd = dscr("KpTd", [32, NK])
        KbTd = dscr("KbTd", [128, NK])
        Vd = dscr("Vd", [10, 128, NKT, 128])
        QnTd = dscr("QnTd", [512, SQ])
        QpTd = dscr("QpTd", [256, SQ])
        QbTd = dscr("QbTd", [512, SQ])
        GTd = dscr("GTd", [1024, SQ])
        MTd = dscr("MTd", [1024, SQ])
        sums_d = nc.dram_tensor("sums_d", [2, 1024], F32, kind="Internal").ap()

        kb = KB(nc, es)

        def sb(name, shape, dt, stack=es):
            return T(stack.enter_context(nc.sbuf_tensor("s_" + name, list(shape), dt)))

        PS = es.enter_context(nc.psum_tensor("ps", [128, 4096], F32))
        banks = [PBank(PS[:, k * 512:(k + 1) * 512]) for k in range(8)]

        ident = sb("ident", [128, 128], F32)
        modc = [sb("modc0", [128, 16, 2], F32), sb("modc1", [128, 16, 2], F32)]
        gate_bc = [sb("gate0", [128, D], F32), sb("gate1", [128, D], F32)]
        epst = sb("epst", [128, 1], F32)
        ones_bf = sb("ones_bf", [128, 128], BF16)
        bdiag = sb("bdiag", [128, 128], BF16)
        sel = sb("sel", [128, 64], F32)

        sem_misc = kb.dsem("misc")

        with ExitStack() as p0:
            wm = sb("wm", [128, 8, 3 * D], F32, p0)
            cc = sb("cc", [128, 8, 2], F32, p0)
            sc = sb("sc", [128, 8, 2], F32, p0)
            screp = sb("screp", [128, 8, 128], F32, p0)
            bmc = [sb("bmc0", [128, 24], F32, p0), sb("bmc1", [128, 24], F32, p0)]
            bmg = [sb("bmg0", [128, D], F32, p0), sb("bmg1", [128, D], F32, p0)]
            sem_wm = kb.dsem("wm")

            kb.dma(sem_misc, [
                (ident.t[:], ident_d[:, :], [], [ident]),
                (cc.t[:], cc_d[:, :, :], [], [cc]),
                (bmc[0].t[:], bmodc_d[0][:, :], [], [bmc[0]]),
                (bmc[1].t[:], bmodc_d[1][:, :], [], [bmc[1]]),
                (bmg[0].t[:], bmodg_d[0].partition_broadcast(128), [], [bmg[0]]),
                (bmg[1].t[:], bmodg_d[1].partition_broadcast(128), [], [bmg[1]]),
            ])
            kb.op(kb.dve, ("memset", dict(ap=epst.t[:], constant=EPS)), writes=[epst])
            kb.op(kb.dve, ("memset", dict(ap=ones_bf.t[:], constant=1.0)), writes=[ones_bf])
            kb.op(kb.dve, ("memset", dict(ap=bdiag.t[:], constant=0.0)), writes=[bdiag])
            kb.op(kb.dve, ("memset", dict(ap=bdiag.t[0:64, 0:64], constant=1.0)), writes=[bdiag])
            kb.op(kb.dve, ("memset", dict(ap=bdiag.t[64:128, 64:128], constant=1.0)), writes=[bdiag])
            kb.op(kb.dve, ("memset", dict(ap=sel.t[:], constant=0.0)), writes=[sel])
            kb.op(kb.dve, ("memset", dict(ap=sel.t[64:65, :], constant=1.0)), writes=[sel])
            kb.op(kb.act, ("activation", dict(out=sc.t[:], in_=cc.t[:], func=AF.Silu)),
                  reads=[cc], writes=[sc])
            for j in range(8):
                kb.op(kb.dve, ("tensor_copy", dict(
                    out=screp.t[:, j, :], in_=sc.t[:, j, 0:1].to_broadcast([128, 128]))),
                    reads=[sc], writes=[screp])
            for L in range(2):
                wv = wmod_d[L].rearrange("(j p) n -> p j n", p=128)
                kb.dma(sem_wm, [(wm.t[:, j, :], wv[:, j, :], [], [wm]) for j in range(8)])
                fns = []
                for k in range(16):
                    for j in range(8):
                        fns.append(("matmul", dict(out=PS[:, 2 * k:2 * k + 2], lhsT=wm.t[:, j, k * 128:(k + 1) * 128],
                            rhs=sc.t[:, j, 0:2], start=(j == 0), stop=(j == 7))))
                kb.group(kb.pe, fns, reads=[wm, sc], writes=[banks[0]])
                kb.op(kb.dve, ("tensor_tensor", dict(
                    out=modc[L].t[:], in0=PS[:, 0:32].rearrange("p (k t) -> p k t", t=2),
                    in1=bmc[L].t[:, 0:16].unsqueeze(2).to_broadcast([128, 16, 2]), op=ALU.add)),
                    reads=[banks[0], bmc[L]], writes=[modc[L]])
                kb.op(kb.dve, ("tensor_scalar", dict(
                    out=modc[L].t[:, 8:16, :], in0=modc[L].t[:, 8:16, :], scalar1=1.0, scalar2=None,
                    op0=ALU.add)), reads=[modc[L]], writes=[modc[L]])
                fns = []
                for blk in range(2):
                    for j in range(8):
                        fns.append(("matmul", dict(out=PS[:, 512 + blk * 512:1024 + blk * 512], lhsT=screp.t[:, j, :],
                            rhs=wm.t[:, j, 2048 + blk * 512:2560 + blk * 512],
                            start=(j == 0), stop=(j == 7))))
                kb.group(kb.pe, fns, reads=[wm, screp], writes=[banks[1], banks[2]])
                kb.op(kb.dve, ("tensor_tensor", dict(
                    out=gate_bc[L].t[:], in0=PS[:, 512:1536], in1=bmg[L].t[:], op=ALU.add)),
                    reads=[banks[1], banks[2], bmg[L]], writes=[gate_bc[L]])
            kb.flush()
        if stop_after == 0:
            return nc

        PH = dict(kb=kb, nc=nc, sb=sb, PS=PS, banks=banks, ident=ident, modc=modc, gate_bc=gate_bc,
                  epst=epst, ones_bf=ones_bf, bdiag=bdiag, sel=sel)
        PH.update(x_all=x_all, ctx=ctx, tabs=tabs, w0_d=w0_d, gv0_d=gv0_d, wuq_d=wuq_d, aqc_d=aqc_d,
                  wukv_d=wukv_d, akvc_d=akvc_d, bqk_d=bqk_d, KnTd=KnTd, KpTd=KpTd, KbTd=KbTd, Vd=Vd,
                  QnTd=QnTd, QpTd=QpTd, QbTd=QbTd, GTd=GTd, MTd=MTd, sums_d=sums_d, wout0_d=wout0_d, lng0_d=lng0_d,
                  lnb0_d=lnb0_d, w1in_d=w1in_d, vlng_d=vlng_d, vlnb_d=vlnb_d, wsT_d=wsT_d, bs_d=bs_d,
                  wout1_d=wout1_d, lng1_d=lng1_d, lnb1_d=lnb1_d, out_d=out_d)
        phase1(PH)
        if stop_after == 1:
            return nc
        phase2(PH)
        if stop_after == 2:
            return nc
        phase3(PH)
        return nc


def phase1(P):
    kb, nc, sb, PS, banks = P["kb"], P["nc"], P["sb"], P["PS"], P["banks"]
    ident, modc, epst, ones_bf, bdiag = P["ident"], P["modc"][0], P["epst"], P["ones_bf"], P["bdiag"]
    pe, act, dve, pool = kb.pe, kb.act, kb.dve, kb.pool
    with ExitStack() as p1:
        W0 = sb("W0", [128, 8, NC0], BF16, p1)
        wuq = sb("wuq", [128, 3, 1024], BF16, p1)
        wukv = sb("wukv", [128, 2, 1024], BF16, p1)
        aqc = sb("aqc", [128, 3], F32, p1)
        akvc = sb("akvc", [128, 2], F32, p1)
        bqk = sb("bqk", [128, 2], F32, p1)
        ginv = sb("ginv", [128, 2], F32, p1)
        p1a = ExitStack()
        gv = sb("gv", [128, NC0], F32, p1a)
        wst = [sb("wst%d" % i, [128, NC0], F32, p1a) for i in range(2)]
        wst_sem = [kb.dsem("wst%d" % i) for i in range(2)]
        sem_c = kb.dsem("p1c")
        kb.dma(sem_c, [
            (gv.t[:], P["gv0_d"].partition_broadcast(128), [], [gv]),
            (aqc.t[:], P["aqc_d"][:, :], [], [aqc]),
            (akvc.t[:], P["akvc_d"][:, :], [], [akvc]),
            (bqk.t[:], P["bqk_d"][:, :], [], [bqk]),
        ])
        kb.op(dve, ("reciprocal", dict(out=ginv.t[:], in_=bqk.t[:])), reads=[bqk], writes=[ginv])
        for j in range(8):
            st = wst[j % 2]
            kb.dma(wst_sem[j % 2], [(st.t[:], P["w0_d"][j * 128:(j + 1) * 128, :], [], [st])])
            eng = dve if j % 2 == 0 else pool
            kb.op(eng, ("tensor_tensor", dict(
                out=W0.t[:, j, :], in0=st.t[:], in1=gv.t[:], op=ALU.mult)), reads=[st, gv], writes=[W0])
        for r in range(3):
            st = wst[r % 2]
            kb.dma(wst_sem[r % 2], [(st.t[:, 0:1024], P["wuq_d"][r * 128:(r + 1) * 128, :], [], [st])])
            kb.op(act, ("activation", dict(
                out=wuq.t[:, r, :], in_=st.t[:, 0:1024], func=AF.Copy, scale=aqc.t[:, r:r + 1])),
                reads=[st, aqc], writes=[wuq])
        for r in range(2):
            st = wst[(r + 1) % 2]
            kb.dma(wst_sem[(r + 1) % 2], [(st.t[:, 0:1024], P["wukv_d"][r * 128:(r + 1) * 128, :], [], [st])])
            kb.op(act, ("activation", dict(
                out=wukv.t[:, r, :], in_=st.t[:, 0:1024], func=AF.Copy, scale=akvc.t[:, r:r + 1])),
                reads=[st, akvc], writes=[wukv])

        kb.flush()
        p1a.close()

        def rot(name, n, shape, dt):
            return Rot([sb("%s%d" % (name, i), shape, dt, p1) for i in range(n)])

        XIN = rot("xin", 2, [128, 4, D], F32)
        TBL = rot("tbl", 2, [128, 4, 512], F32)
        ld_sem = Rot([kb.dsem("ld%d" % i) for i in range(2)])
        XMT = rot("xmT", 2, [128, 8, 512], BF16)
        CKVT = rot("ckvT", 2, [128, 2, 512], BF16)
        SQ2 = rot("sq2", 2, [128, 3, 512], BF16)
        RBC = rot("rbc", 2, [128, 512], F32)
        RCOL = rot("rcol", 2, [128, 4], F32)
        RQ = rot("rq", 1, [128, 512], F32)
        TMP = rot("tmp", 3, [128, 512], F32)
        KNT = rot("knT", 1, [128, 4, 512], BF16)
        KPE = rot("kpe", 2, [32, 512], BF16)
        KBR = rot("kbr", 2, [128, 512], BF16)
        VST = rot("vst", 2, [128, 10, 4, 128], BF16)
        CQT = rot("cqT", 1, [128, 3, 512], BF16)
        GST = rot("gst", 1, [128, 8, 512], BF16)
        QBR = rot("qbr", 1, [128, 4, 512], BF16)
        QNT = rot("qnT", 1, [128, 4, 512], BF16)
        QPT = rot("qpT", 1, [128, 2, 512], BF16)
        st_sems = {}

        def stsem(name, slot):
            key = (name, slot)
            if key not in st_sems:
                st_sems[key] = kb.dsem("st_%s%d" % (name, slot))
            return st_sems[key]

        for v in VST.items:
            kb.op(pool, ("memset", dict(ap=v.t[:], constant=1.0)), writes=[v])
        PSB = Rot(banks)

        nblk = S // 512
        blist = list(range(nblk + 1))
        if os.environ.get("P1_BLOCKS"):
            blist = [int(v) for v in os.environ["P1_BLOCKS"].split(",") if v != "x"]

        loaded = {}

        def issue_load(bi):
            xin, tbl, lsem = XIN.next(), TBL.next(), ld_sem.next()
            if bi == nblk:
                src = P["ctx"].rearrange("(t p) f -> p t f", p=128)
                kb.dma(lsem, [(xin.t[:, 0:2, :], src[:, :, :], [], [xin])])
            else:
                src = P["x_all"].rearrange("(t p) f -> p t f", p=128)
                tsrc = P["tabs"].rearrange("k p t -> p k t")
                kb.dma(lsem, [(xin.t[:], src[:, bi * 4:bi * 4 + 4, :], [], [xin]),
                              (tbl.t[:], tsrc[:, :, bi * 512:bi * 512 + 512], [], [tbl])])
            loaded[bi] = (xin, tbl)

        def do_block(bi):
            is_ctx = (bi == nblk)
            nt = 256 if is_ctx else 512
            ntt = nt // 128
            do_q = (not is_ctx) and bi < SQ // 512
            k0 = bi * 512
            mc = 1 if is_ctx else 0
            slot = bi % 2
            if bi not in loaded:
                issue_load(bi)
            xin, tbl = loaded.pop(bi)
            TBc, TBs, TAc, TAs = (tbl.t[:, i, :] for i in range(4))
            xmT = XMT.next()
            for j in range(8):
                pb = PSB.next()
                kb.group(pe, [("transpose", dict(
                    out=pb.ap[:, tt * 128:(tt + 1) * 128], in_=xin.t[:, tt, j * 128:(j + 1) * 128],
                    identity=ident.t[:])) for tt in range(ntt)], reads=[xin, ident], writes=[pb])
                kb.op(act, ("activation", dict(
                    out=xmT.t[:, j, 0:nt], in_=pb.ap[:, 0:nt], func=AF.Identity,
                    scale=modc.t[:, 8 + j, mc:mc + 1], bias=modc.t[:, j, mc:mc + 1])),
                    reads=[pb, modc], writes=[xmT])

            nxt = blist[blist.index(bi) + 1] if blist.index(bi) + 1 < len(blist) else None
            if nxt is not None:
                issue_load(nxt)

            def proj(off, M):
                pb = PSB.next()
                kb.group(pe, [("matmul", dict(out=pb.ap[0:M, 0:nt], lhsT=W0.t[:, j, off:off + M], rhs=xmT.t[:, j, 0:nt],
                    start=(j == 0), stop=(j == 7))) for j in range(8)], reads=[W0, xmT], writes=[pb])
                return pb

            def rstd_from(pb, npart, ncol, scale, dst_ap, dst):
                kb.op(act, ("activation", dict(out=dst_ap, in_=pb.ap[0:npart, 0:ncol], func=AF.Ln,
                                                  scale=scale, bias=epst.t[0:npart, 0:1])),
                      reads=[pb, epst], writes=[dst])
                kb.op(act, ("activation", dict(out=dst_ap, in_=dst_ap, func=AF.Exp, scale=-0.5)),
                      reads=[dst], writes=[dst])

            ckvT = CKVT.next()
            sq = SQ2.next()
            for r in range(2):
                pb = proj(O_CKV + r * 128, 128)
                kb.op(dve, ("tensor_copy", dict(out=ckvT.t[:, r, 0:nt], in_=pb.ap[:, 0:nt])),
                      reads=[pb], writes=[ckvT])
                kb.op(act, ("activation", dict(out=sq.t[:, r, 0:nt], in_=pb.ap[:, 0:nt],
                                                              func=AF.Square)), reads=[pb], writes=[sq])
            pbR = PSB.next()
            kb.group(pe, [("matmul", dict(out=pbR.ap[:, 0:nt], lhsT=ones_bf.t[:], rhs=sq.t[:, r, 0:nt], start=(r == 0), stop=(r == 1)))
                for r in range(2)], reads=[ones_bf, sq], writes=[pbR])
            rkv = RBC.next()
            rstd_from(pbR, 128, nt, 1.0 / 256, rkv.t[:, 0:nt], rkv)
            pbc = PSB.next()
            fns = []
            for tt in range(ntt):
                for r in range(2):
                    fns.append(("matmul", dict(out=pbc.ap[:, tt:tt + 1], lhsT=sq.t[:, r, tt * 128:(tt + 1) * 128],
                        rhs=ones_bf.t[:, 0:1], start=(r == 0), stop=(r == 1))))
            kb.group(pe, fns, reads=[sq, ones_bf], writes=[pbc])
            rcol = RCOL.next()
            rstd_from(pbc, 128, ntt, 1.0 / 256, rcol.t[:, 0:ntt], rcol)
            knT = KNT.next()
            for c in range(4):
                pb = PSB.next()
                kb.group(pe, [("matmul", dict(out=pb.ap[:, 0:nt], lhsT=wukv.t[:, r, c * 128:(c + 1) * 128], rhs=ckvT.t[:, r, 0:nt],
                    start=(r == 0), stop=(r == 1))) for r in range(2)], reads=[wukv, ckvT], writes=[pb])
                kb.op(dve, ("tensor_tensor", dict(
                    out=knT.t[:, c, 0:nt], in0=pb.ap[:, 0:nt], in1=rkv.t[:, 0:nt], op=ALU.mult)),
                    reads=[pb, rkv], writes=[knT])
            kb.dma(stsem("kn", slot), [(P["KnTd"].rearrange("(c p) t -> p c t", p=128)[:, :, k0:k0 + nt],
                                        knT.t[:, :, 0:nt], [knT], [])])
            vst = VST.next()
            for tt in range(ntt):
                pb = PSB.next()
                kb.group(pe, [("matmul", dict(out=pb.ap[:, 0:512], lhsT=ckvT.t[:, r, tt * 128:(tt + 1) * 128], rhs=wukv.t[:, r, 512:1024],
                    start=(r == 0), stop=(r == 1))) for r in range(2)], reads=[wukv, ckvT], writes=[pb])
                kb.op(dve, ("tensor_scalar", dict(
                    out=vst.t[:, 0:8, tt, 0:64], in0=pb.ap[:, 0:512].rearrange("p (h e) -> p h e", e=64),
                    scalar1=rcol.t[:, tt:tt + 1], scalar2=None, op0=ALU.mult)),
                    reads=[pb, rcol], writes=[vst])
            for tt in range(ntt):
                pb = PSB.next()
                kb.group(pe, [("matmul", dict(out=pb.ap[:, 0:128], lhsT=xmT.t[:, j, tt * 128:(tt + 1) * 128],
                    rhs=W0.t[:, j, O_VB:O_VB + 128], start=(j == 0), stop=(j == 7))) for j in range(8)],
                    reads=[W0, xmT], writes=[pb])
                kb.op(act, ("activation", dict(
                    out=vst.t[:, 8:10, tt, 0:64], in_=pb.ap[:, 0:128].rearrange("p (h e) -> p h e", e=64),
                    func=AF.Copy)), reads=[pb], writes=[vst])
            kt0 = k0 // 128
            kb.dma(stsem("v", slot), [(P["Vd"].rearrange("h p k e -> p h k e")[:, :, kt0:kt0 + ntt, :],
                                       vst.t[:, :, 0:ntt, :], [vst], [])])
            kpe = KPE.next()
            pb1 = proj(O_KR, 32)
            if is_ctx:
                kb.op(dve, ("tensor_copy", dict(out=kpe.t[:, 0:nt], in_=pb1.ap[0:32, 0:nt])),
                      reads=[pb1], writes=[kpe])
            else:
                pb2 = proj(O_KRP, 32)
                t1, t2 = TMP.next(), TMP.next()
                kb.op(dve, ("tensor_tensor", dict(out=t1.t[0:32, :], in0=pb1.ap[0:32, :], in1=TAc[0:32, :],
                                                     op=ALU.mult)), reads=[pb1, tbl], writes=[t1])
                kb.op(dve, ("tensor_tensor", dict(out=t2.t[0:32, :], in0=pb2.ap[0:32, :], in1=TAs[0:32, :],
                                                     op=ALU.mult)), reads=[pb2, tbl], writes=[t2])
                kb.op(pool, ("tensor_tensor", dict(out=kpe.t[:, :], in0=t1.t[0:32, :], in1=t2.t[0:32, :],
                                                      op=ALU.add)), reads=[t1, t2], writes=[kpe])
            kb.dma(stsem("kp", slot), [(P["KpTd"][:, k0:k0 + nt], kpe.t[:, 0:nt], [kpe], [])])

            def normrope(off, offp, gcol, dst_ap, dst, rope):
                pb1 = proj(off, 128)
                sqk = SQ2.next()
                kb.op(act, ("activation", dict(out=sqk.t[:, 0, 0:nt], in_=pb1.ap[:, 0:nt], func=AF.Square,
                                                  scale=ginv.t[:, gcol:gcol + 1])), reads=[pb1, ginv], writes=[sqk])
                pbR = PSB.next()
                kb.group(pe, [("matmul", dict(out=pbR.ap[:, 0:nt], lhsT=bdiag.t[:], rhs=sqk.t[:, 0, 0:nt],
                                                 start=True, stop=True))], reads=[bdiag, sqk], writes=[pbR])
                rr = RBC.next()
                rstd_from(pbR, 128, nt, 1.0 / 64, rr.t[:, 0:nt], rr)
                if not rope:
                    kb.op(dve, ("tensor_tensor", dict(out=dst_ap, in0=pb1.ap[:, 0:nt], in1=rr.t[:, 0:nt],
                                                         op=ALU.mult)), reads=[pb1, rr], writes=[dst])
                    return
                pb2 = proj(offp, 128)
                t1, t2 = TMP.next(), TMP.next()
                kb.op(dve, ("tensor_tensor", dict(out=t1.t[:], in0=pb1.ap[:, :], in1=TBc, op=ALU.mult)),
                      reads=[pb1, tbl], writes=[t1])
                kb.op(dve, ("tensor_tensor", dict(out=t2.t[:], in0=pb2.ap[:, :], in1=TBs, op=ALU.mult)),
                      reads=[pb2, tbl], writes=[t2])
                kb.op(pool, ("tensor_tensor", dict(out=t1.t[:], in0=t1.t[:], in1=t2.t[:], op=ALU.add)),
                      reads=[t1, t2], writes=[t1])
                kb.op(dve, ("tensor_tensor", dict(out=dst_ap, in0=t1.t[:], in1=rr.t[:], op=ALU.mult)),
                      reads=[t1, rr], writes=[dst])

            kbr = KBR.next()
            normrope(O_KB, O_KBP, 1, kbr.t[:, 0:nt], kbr, not is_ctx)
            kb.dma(stsem("kb", slot), [(P["KbTd"][:, k0:k0 + nt], kbr.t[:, 0:nt], [kbr], [])])

            if not do_q:
                return
            cqT = CQT.next()
            sq3 = SQ2.next()
            for r in range(3):
                pb = proj(O_CQ + r * 128, 128)
                kb.op(dve, ("tensor_copy", dict(out=cqT.t[:, r, :], in_=pb.ap[:, :])),
                      reads=[pb], writes=[cqT])
                kb.op(act, ("activation", dict(out=sq3.t[:, r, :], in_=pb.ap[:, :],
                                                              func=AF.Square)), reads=[pb], writes=[sq3])
            pbR = PSB.next()
            kb.group(pe, [("matmul", dict(out=pbR.ap[:, :], lhsT=ones_bf.t[:], rhs=sq3.t[:, r, :], start=(r == 0), stop=(r == 2)))
                for r in range(3)], reads=[ones_bf, sq3], writes=[pbR])
            rq = RQ.next()
            rstd_from(pbR, 128, 512, 1.0 / 384, rq.t[:], rq)
            gst = GST.next()
            for gi, off in enumerate((O_GA, O_GB)):
                for c in range(4):
                    pb = proj(off + c * 128, 128)
                    kb.op(act, ("activation", dict(
                        out=gst.t[:, gi * 4 + c, :], in_=pb.ap[:, :], func=AF.Silu)), reads=[pb], writes=[gst])
            kb.dma(stsem("g", slot), [(P["GTd"].rearrange("(c p) t -> p c t", p=128)[:, :, k0:k0 + 512],
                                       gst.t[:], [gst], [])])
            qbr = QBR.next()
            for c in range(4):
                normrope(O_QB + c * 128, O_QBP + c * 128, 0, qbr.t[:, c, :], qbr, True)
            kb.dma(stsem("qb", slot), [(P["QbTd"].rearrange("(c p) t -> p c t", p=128)[:, :, k0:k0 + 512],
                                        qbr.t[:], [qbr], [])])
            qnT = QNT.next()
            for c in range(4):
                pb = PSB.next()
                kb.group(pe, [("matmul", dict(out=pb.ap[:, :], lhsT=wuq.t[:, r, c * 128:(c + 1) * 128], rhs=cqT.t[:, r, :],
                    start=(r == 0), stop=(r == 2))) for r in range(3)], reads=[wuq, cqT], writes=[pb])
                kb.op(dve, ("tensor_tensor", dict(
                    out=qnT.t[:, c, :], in0=pb.ap[:, :], in1=rq.t[:], op=ALU.mult)),
                    reads=[pb, rq], writes=[qnT])
            kb.dma(stsem("qn", slot), [(P["QnTd"].rearrange("(c p) t -> p c t", p=128)[:, :, k0:k0 + 512],
                                        qnT.t[:], [qnT], [])])
            qpT = QPT.next()
            for c in range(2):
                pbs = []
                for base in (512, 768):
                    pb = PSB.next()
                    kb.group(pe, [("matmul", dict(out=pb.ap[:, :], lhsT=wuq.t[:, r, base + c * 128:base + (c + 1) * 128], rhs=cqT.t[:, r, :],
                        start=(r == 0), stop=(r == 2))) for r in range(3)], reads=[wuq, cqT], writes=[pb])
                    pbs.append(pb)
                t1, t2 = TMP.next(), TMP.next()
                kb.op(dve, ("tensor_tensor", dict(out=t1.t[:], in0=pbs[0].ap[:, :], in1=TAc,
                                                                       op=ALU.mult)), reads=[pbs[0], tbl], writes=[t1])
                kb.op(dve, ("tensor_tensor", dict(out=t2.t[:], in0=pbs[1].ap[:, :], in1=TAs,
                                                                       op=ALU.mult)), reads=[pbs[1], tbl], writes=[t2])
                kb.op(pool, ("tensor_tensor", dict(out=t1.t[:], in0=t1.t[:], in1=t2.t[:],
                                                                    op=ALU.add)), reads=[t1, t2], writes=[t1])
                kb.op(dve, ("tensor_tensor", dict(out=qpT.t[:, c, :], in0=t1.t[:], in1=rq.t[:],
                                                                 op=ALU.mult)), reads=[t1, rq], writes=[qpT])
            kb.dma(stsem("qp", slot), [(P["QpTd"].rearrange("(c p) t -> p c t", p=128)[:, :, k0:k0 + 512],
                                        qpT.t[:], [qpT], [])])

        for bi in blist:
            do_block(bi)
        kb.flush()


def phase2(P):
    kb, nc, sb, PS = P["kb"], P["nc"], P["sb"], P["PS"]
    pe, act, dve, pool = kb.pe, kb.act, kb.dve, kb.pool
    NB = 3
    with ExitStack() as p2:
        KT = [sb("KT%d" % i, [128, NK], BF16, p2) for i in range(2)]
        VV = [sb("VV%d" % i, [128, NKT, 128], BF16, p2) for i in range(2)]
        QT = [sb("QT%d" % i, [128, SQ], BF16, p2) for i in range(2)]
        GT = [sb("GT%d" % i, [64, SQ], BF16, p2) for i in range(2)]
        hsem = [kb.dsem("hd%d" % i) for i in range(2)]
        PT = [sb("PT%d" % i, [128, 1024], BF16, p2) for i in range(NB)]
        OCP = [sb("ocp%d" % i, [128, 1024], F32, p2) for i in range(2)]
        BC = [sb("bc%d" % i, [64, 1024], F32, p2) for i in range(2)]
        bsem = [kb.dsem("bcs%d" % i) for i in range(2)]
        bsem2 = [kb.dsem("bcl%d" % i) for i in range(2)]
        SUMd = [Buf(), Buf()]
        M1 = sb("m1", [64, 1024], F32, p2)
        MST = [sb("mst%d" % i, [64, 1024], BF16, p2) for i in range(2)]
        msem = [kb.dsem("mst%d" % i) for i in range(2)]
        psS = [PBank(PS[:, i * 1024:(i + 1) * 1024]) for i in range(NB)]
        psO = PBank(PS[:, 3072:4096])
        sums_d = P["sums_d"]

        for sl0 in range(2):
            kb.op(pool, ("memset", dict(ap=KT[sl0].t[64:128, :], constant=0.0)), writes=[KT[sl0]])
            kb.op(dve, ("memset", dict(ap=QT[sl0].t[64:128, :], constant=0.0)), writes=[QT[sl0]])

        def load_head(h16):
            sl = h16 % 2
            kt, vv, qt, gt = KT[sl], VV[sl], QT[sl], GT[sl]
            items = []
            if h16 in (8, 9):
                kb.op(pool, ("memset", dict(ap=kt.t[64:96, :], constant=0.0)), writes=[kt])
                kb.op(dve, ("memset", dict(ap=qt.t[64:96, :], constant=0.0)), writes=[qt])
            if h16 < 8:
                h = h16
                items.append((kt.t[0:64, :], P["KnTd"][h * 64:(h + 1) * 64, :], [], [kt]))
                items.append((kt.t[64:96, :], P["KpTd"][:, :], [], [kt]))
                items.append((vv.t[:], P["Vd"][h], [], [vv]))
                items.append((qt.t[0:64, :], P["QnTd"][h * 64:(h + 1) * 64, :], [], [qt]))
                items.append((qt.t[64:96, :], P["QpTd"][h * 32:(h + 1) * 32, :], [], [qt]))
            else:
                hb = h16 - 8
                kvh = hb // 4
                items.append((kt.t[0:64, :], P["KbTd"][kvh * 64:(kvh + 1) * 64, :], [], [kt]))
                items.append((vv.t[:], P["Vd"][8 + kvh], [], [vv]))
                items.append((qt.t[0:64, :], P["QbTd"][hb * 64:(hb + 1) * 64, :], [], [qt]))
            items.append((gt.t[:], P["GTd"][h16 * 64:(h16 + 1) * 64, :], [], [gt]))
            kb.dma(hsem[sl], items)

        iters = [(h16, sbk, kt) for h16 in range(16) for sbk in range(SQ // 1024) for kt in range(NKT)]
        N = len(iters)

        def emit_S(n):
            h16, sbk, kt = iters[n]
            sl = h16 % 2
            kd = 128
            ps = psS[n % NB]
            kb.group(pe, [("matmul", dict(out=ps.ap[:, hf * 512:(hf + 1) * 512],
                                          lhsT=KT[sl].t[0:kd, kt * 128:(kt + 1) * 128],
                                          rhs=QT[sl].t[0:kd, sbk * 1024 + hf * 512:sbk * 1024 + (hf + 1) * 512],
                                          start=True, stop=True)) for hf in range(2)],
                     reads=[KT[sl], QT[sl]], writes=[ps])

        def emit_exp(n):
            h16, sbk, kt = iters[n]
            sc = A_SCALE if h16 < 8 else B_SCALE
            kb.op(act, ("activation", dict(out=PT[n % NB].t[:], in_=psS[n % NB].ap[:, :], func=AF.Exp, scale=sc)),
                  reads=[psS[n % NB]], writes=[PT[n % NB]])

        def emit_PV(n):
            h16, sbk, kt = iters[n]
            sl = h16 % 2
            kb.group(pe, [("matmul", dict(out=psO.ap[:, hf * 512:(hf + 1) * 512], lhsT=VV[sl].t[:, kt, :],
                                          rhs=PT[n % NB].t[:, hf * 512:(hf + 1) * 512],
                                          start=(kt == 0), stop=(kt == NKT - 1))) for hf in range(2)],
                     reads=[VV[sl], PT[n % NB]], writes=[psO])

        ep = [0]

        def emit_epi1(n):
            e = ep[0]
            ocp = OCP[e % 2]
            kb.op(dve, ("tensor_copy", dict(out=ocp.t[:], in_=psO.ap[:, :])), reads=[psO], writes=[ocp])
            kb.dma(bsem[e % 2], [(sums_d[e % 2:e % 2 + 1, :], ocp.t[64:65, :], [ocp], [SUMd[e % 2]])])
            kb.dma(bsem2[e % 2], [(BC[e % 2].t[:], sums_d[e % 2, :].partition_broadcast(64), [SUMd[e % 2]], [BC[e % 2]])])

        def emit_epi2(n):
            h16, sbk, kt = iters[n]
            sl = h16 % 2
            e = ep[0]
            ep[0] += 1
            ocp, mst, bc = OCP[e % 2], MST[e % 2], BC[e % 2]
            kb.op(dve, ("reciprocal", dict(out=bc.t[:], in_=bc.t[:])), reads=[bc], writes=[bc])
            kb.op(dve, ("tensor_tensor", dict(out=M1.t[:], in0=ocp.t[0:64, :], in1=bc.t[:], op=ALU.mult)),
                  reads=[ocp, bc], writes=[M1])
            kb.op(dve, ("tensor_tensor", dict(out=mst.t[:], in0=M1.t[:],
                                              in1=GT[sl].t[:, sbk * 1024:(sbk + 1) * 1024], op=ALU.mult)),
                  reads=[M1, GT[sl]], writes=[mst])
            kb.dma(msem[e % 2], [(P["MTd"][h16 * 64:(h16 + 1) * 64, sbk * 1024:(sbk + 1) * 1024], mst.t[:],
                                  [mst], [])])
            if sbk == SQ // 1024 - 1 and h16 + 2 < 16:
                load_head(h16 + 2)

        load_head(0)
        load_head(1)
        pending = []
        emit_S(0)
        emit_S(1)
        for n in range(N):
            h16, sbk, kt = iters[n]
            if n + 2 < N:
                emit_S(n + 2)
            emit_exp(n)
            emit_PV(n)
            if kt == NKT - 1:
                emit_epi1(n)
                pending.append((n + 6, n))
            if pending and (pending[0][0] <= n or n == N - 1):
                emit_epi2(pending.pop(0)[1])
        while pending:
            emit_epi2(pending.pop(0)[1])
        kb.flush()


def phase3(P):
    kb, nc, sb, PS = P["kb"], P["nc"], P["sb"], P["PS"]
    ident, epst = P["ident"], P["epst"]
    modc1 = P["modc"][1]
    gate0, gate1 = P["gate_bc"]
    pe, act, dve, pool = kb.pe, kb.act, kb.dve, kb.pool
    with ExitStack() as p3:
        w0o = sb("w0o", [128, 8, D], BF16, p3)
        w1i = sb("w1i", [128, 8, 3 * D], BF16, p3)
        w1o = sb("w1o", [128, 8, D], BF16, p3)
        wsT = sb("wsT", [128, 8, 128], BF16, p3)
        bct = {}
        for nm in ("lng0", "lnb0", "lng1", "lnb1", "vlng", "vlnb"):
            bct[nm] = sb("bc_" + nm, [128, D], F32, p3)
        bsb = sb("bsb", [128, 8, 128], F32, p3)
        sem_c = kb.dsem("p3c")
        kb.dma(sem_c, [(bct[nm].t[:], P[nm + "_d"].partition_broadcast(128), [], [bct[nm]]) for nm in bct] +
               [(bsb.t[:], P["bs_d"].partition_broadcast(128), [], [bsb])])
        p3a = ExitStack()
        wst = [sb("wst3_%d" % i, [128, 3 * D], F32, p3a) for i in range(2)]
        wsem = [kb.dsem("wst3_%d" % i) for i in range(2)]
        k = 0
        for j in range(8):
            st = wst[k % 2]
            kb.dma(wsem[k % 2], [(st.t[:], P["w1in_d"][j * 128:(j + 1) * 128, :], [], [st])])
            kb.op(act if j % 2 == 0 else dve, ("activation" if j % 2 == 0 else "tensor_copy",
                                               dict(out=w1i.t[:, j, :], in_=st.t[:], func=AF.Copy) if j % 2 == 0
                                               else dict(out=w1i.t[:, j, :], in_=st.t[:])),
                  reads=[st], writes=[w1i])
            k += 1
        for (dst, src, gt) in ((w0o, "wout0_d", gate0), (w1o, "wout1_d", gate1)):
            for j2 in range(4):
                st = wst[k % 2]
                kb.dma(wsem[k % 2], [(st.t[:, 0:2048].rearrange("p (a n) -> p a n", a=2),
                                      P[src][j2 * 256:(j2 + 1) * 256, :].rearrange("(a p) n -> p a n", p=128),
                                      [], [st])])
                for a in range(2):
                    kb.op(dve if a == 0 else pool, ("tensor_tensor", dict(
                        out=dst.t[:, j2 * 2 + a, :], in0=st.t[:, a * 1024:(a + 1) * 1024], in1=gt.t[:], op=ALU.mult)),
                        reads=[st, gt], writes=[dst])
                k += 1
        st = wst[k % 2]
        kb.dma(wsem[k % 2], [(st.t[:, 0:1024].rearrange("p (g q) -> p g q", g=8),
                              P["wsT_d"].rearrange("g q p -> q g p"), [], [st])])
        kb.op(dve, ("tensor_copy", dict(out=wsT.t[:], in_=st.t[:, 0:1024].rearrange("p (g q) -> p g q", g=8))),
              reads=[st], writes=[wsT])
        kb.flush()
        p3a.close()

        def rot(name, n, shape, dt):
            return Rot([sb("%s%d" % (name, i), shape, dt, p3) for i in range(n)])

        MTB = rot("mtb", 1, [128, 8, 512], BF16)
        mt_sem = Rot([kb.dsem("mtb%d" % i) for i in range(2)])
        XR = rot("xr", 2, [128, D], F32)
        xr_sem = Rot([kb.dsem("xr%d" % i) for i in range(2)])
        X1 = rot("x1", 1, [128, 4, D], F32)
        XMT = rot("x1mT", 1, [128, 8, 512], BF16)
        UG = rot("ug", 1, [128, 8, 512], F32)
        WK = rot("wk", 3, [128, D], F32)
        TS = rot("tsil", 2, [128, 512], F32)
        STT = rot("stt", 2, [128, 2, 6], F32)
        MV = rot("mv", 2, [128, 2], F32)
        RS = rot("rs", 2, [128, 1], F32)
        VLN = rot("vln", 2, [128, D], BF16)
        ZT = rot("zT", 2, [128, 8, 128], BF16)
        OUT = rot("outt", 2, [128, D], F32)
        out_sem = Rot([kb.dsem("out%d" % i) for i in range(2)])
        PS2 = Rot([PBank(PS[:, 0:1024]), PBank(PS[:, 1024:2048]), PBank(PS[:, 2048:3072])])
        PS1 = Rot([PBank(PS[:, 3072:3584]), PBank(PS[:, 3584:4096])])

        def layer_norm(src, dst_ap, dst, g, b):
            stt, mv, rs = STT.next(), MV.next(), RS.next()
            kb.group(dve, [("bn_stats", dict(out=stt.t[:, hh, :], in_=src.t[:, hh * 512:(hh + 1) * 512]))
                           for hh in range(2)], reads=[src], writes=[stt])
            kb.op(dve, ("bn_aggr", dict(out=mv.t[:], in_=stt.t[:])), reads=[stt], writes=[mv])
            kb.op(act, ("activation", dict(out=rs.t[:], in_=mv.t[:, 1:2], func=AF.Ln, bias=epst.t[:, 0:1])),
                  reads=[mv, epst], writes=[rs])
            kb.op(act, ("activation", dict(out=rs.t[:], in_=rs.t[:], func=AF.Exp, scale=-0.5)),
                  reads=[rs], writes=[rs])
            kb.op(dve, ("tensor_scalar", dict(out=src.t[:], in0=src.t[:], scalar1=mv.t[:, 0:1], scalar2=rs.t[:, 0:1],
                                              op0=ALU.subtract, op1=ALU.mult)), reads=[src, mv, rs], writes=[src])
            kb.op(pool, ("tensor_tensor", dict(out=src.t[:], in0=src.t[:], in1=g.t[:], op=ALU.mult)),
                  reads=[src, g], writes=[src])
            kb.op(pool, ("tensor_tensor", dict(out=dst_ap, in0=src.t[:], in1=b.t[:], op=ALU.add)),
                  reads=[src, b], writes=[dst])

        for bi in range(SQ // 512):
            k0 = bi * 512
            mt, msem = MTB.next(), mt_sem.next()
            kb.dma(msem, [(mt.t[:], P["MTd"].rearrange("(c p) t -> p c t", p=128)[:, :, k0:k0 + 512], [], [mt])])
            x1 = X1.next()
            for tt in range(4):
                xr, xsem = XR.next(), xr_sem.next()
                r0 = k0 + tt * 128
                kb.dma(xsem, [(xr.t[:], P["x_all"][r0:r0 + 128, :], [], [xr])])
                psY = PS2.next()
                kb.group(pe, [("matmul", dict(out=psY.ap[:, cb * 512:(cb + 1) * 512],
                                              lhsT=mt.t[:, c, tt * 128:(tt + 1) * 128],
                                              rhs=w0o.t[:, c, cb * 512:(cb + 1) * 512], start=(c == 0), stop=(c == 7)))
                              for cb in range(2) for c in range(8)], reads=[mt, w0o], writes=[psY])
                r = WK.next()
                kb.op(dve, ("scalar_tensor_tensor", dict(out=r.t[:], in0=xr.t[:], scalar=ALPHA, in1=psY.ap[:, :],
                                                         op0=ALU.mult, op1=ALU.add)), reads=[xr, psY], writes=[r])
                layer_norm(r, x1.t[:, tt, :], x1, bct["lng0"], bct["lnb0"])
            xmT = XMT.next()
            for j in range(8):
                pb = PS1.next()
                kb.group(pe, [("transpose", dict(out=pb.ap[:, tt * 128:(tt + 1) * 128],
                                                 in_=x1.t[:, tt, j * 128:(j + 1) * 128], identity=ident.t[:]))
                              for tt in range(4)], reads=[x1, ident], writes=[pb])
                kb.op(act, ("activation", dict(out=xmT.t[:, j, :], in_=pb.ap[:, :], func=AF.Identity,
                                               scale=modc1.t[:, 8 + j, 0:1], bias=modc1.t[:, j, 0:1])),
                      reads=[pb, modc1], writes=[xmT])
            ug = UG.next()
            for c in range(8):
                pb = PS1.next()
                kb.group(pe, [("matmul", dict(out=pb.ap[:, :], lhsT=w1i.t[:, j, c * 128:(c + 1) * 128],
                                              rhs=xmT.t[:, j, :], start=(j == 0), stop=(j == 7))) for j in range(8)],
                         reads=[w1i, xmT], writes=[pb])
                kb.op(act, ("activation", dict(out=ug.t[:, c, :], in_=pb.ap[:, :], func=AF.Gelu_apprx_tanh)),
                      reads=[pb], writes=[ug])
            for c in range(8):
                pb = PS1.next()
                kb.group(pe, [("matmul", dict(out=pb.ap[:, :], lhsT=w1i.t[:, j, 2048 + c * 128:2048 + (c + 1) * 128],
                                              rhs=xmT.t[:, j, :], start=(j == 0), stop=(j == 7))) for j in range(8)],
                         reads=[w1i, xmT], writes=[pb])
                ts = TS.next()
                kb.op(act, ("activation", dict(out=ts.t[:], in_=pb.ap[:, :], func=AF.Silu)), reads=[pb], writes=[ts])
                kb.op(pool, ("tensor_tensor", dict(out=ug.t[:, c, :], in0=ug.t[:, c, :], in1=ts.t[:], op=ALU.mult)),
                      reads=[ug, ts], writes=[ug])
            for tt in range(4):
                psV = PS2.next()
                kb.group(pe, [("matmul", dict(out=psV.ap[:, cb * 512:(cb + 1) * 512],
                                              lhsT=xmT.t[:, j, tt * 128:(tt + 1) * 128],
                                              rhs=w1i.t[:, j, 1024 + cb * 512:1536 + cb * 512],
                                              start=(j == 0), stop=(j == 7))) for cb in range(2) for j in range(8)],
                         reads=[xmT, w1i], writes=[psV])
                vg = WK.next()
                kb.op(act, ("activation", dict(out=vg.t[:], in_=psV.ap[:, :], func=AF.Gelu_apprx_tanh)),
                      reads=[psV], writes=[vg])
                vln = VLN.next()
                layer_norm(vg, vln.t[:], vln, bct["vlng"], bct["vlnb"])
                psM = PS2.next()
                kb.group(pe, [("matmul", dict(out=psM.ap[:, g * 128:(g + 1) * 128], lhsT=vln.t[:, g * 128:(g + 1) * 128],
                                              rhs=wsT.t[:, g, :], start=True, stop=True)) for g in range(8)],
                         reads=[vln, wsT], writes=[psM])
                tz = WK.next()
                kb.op(dve, ("tensor_tensor", dict(out=tz.t[:].rearrange("p (g q) -> p g q", g=8),
                                                  in0=psM.ap[:, :].rearrange("p (g q) -> p g q", g=8),
                                                  in1=bsb.t[:], op=ALU.add)), reads=[psM, bsb], writes=[tz])
                zT = ZT.next()
                kb.op(dve, ("tensor_tensor", dict(out=zT.t[:], in0=tz.t[:].rearrange("p (g q) -> p g q", g=8),
                                                  in1=ug.t[:, :, tt * 128:(tt + 1) * 128], op=ALU.mult)),
                      reads=[tz, ug], writes=[zT])
                psY1 = PS2.next()
                kb.group(pe, [("matmul", dict(out=psY1.ap[:, cb * 512:(cb + 1) * 512], lhsT=zT.t[:, g, :],
                                              rhs=w1o.t[:, g, cb * 512:(cb + 1) * 512], start=(g == 0), stop=(g == 7)))
                              for cb in range(2) for g in range(8)], reads=[zT, w1o], writes=[psY1])
                r1 = WK.next()
                kb.op(dve, ("scalar_tensor_tensor", dict(out=r1.t[:], in0=x1.t[:, tt, :], scalar=ALPHA,
                                                         in1=psY1.ap[:, :], op0=ALU.mult, op1=ALU.add)),
                      reads=[x1, psY1], writes=[r1])
                ot, osem = OUT.next(), out_sem.next()
                layer_norm(r1, ot.t[:], ot, bct["lng1"], bct["lnb1"])
                r0 = k0 + tt * 128
                kb.dma(osem, [(P["out_d"][r0:r0 + 128, :], ot.t[:], [ot], [])])
        kb.flush()


def _partner(d):
    q = d // 4
    i = np.arange(d)
    r = i % (d // 2)
    partner = np.where(r < q, i + q, i - q)
    sign = np.where(r < q, -1.0, 1.0).astype(np.float32)
    return partner, sign


def _rope_tables(pos):
    out = []
    row = (pos // 64).astype(np.float32)
    col = (pos % 64).astype(np.float32)
    for d, rep in ((64, 2), (32, 4)):
        d_axis = d // 2
        inv = (np.float32(10000.0) ** (-np.arange(0, d_axis, 2, dtype=np.float32) / np.float32(d_axis))).astype(np.float32)
        _, sign = _partner(d)
        i = np.arange(d)
        axis = i // d_axis
        j = i % (d // 4)
        p = np.where(axis[:, None] == 0, row[None, :], col[None, :]).astype(np.float32)
        ang = (p * inv[j][:, None]).astype(np.float32)
        c = np.cos(ang).astype(np.float32)
        s = (np.sin(ang).astype(np.float32) * sign[:, None]).astype(np.float32)
        out.append(np.tile(c, (rep, 1)))
        out.append(np.tile(s, (rep, 1)))
    return np.ascontiguousarray(np.stack(out, 0))


def make_in_maps(inp):
    f = lambda a: np.ascontiguousarray(np.asarray(a, dtype=np.float32))
    x = f(inp["x"]); c = f(inp["c"]); ctx = f(inp["ctx"]); c_ctx = f(inp["c_ctx"])
    e_w_in = f(inp["e_w_in"])[0]
    p64, _ = _partner(64)
    p32, _ = _partner(32)
    h8 = np.arange(8)[:, None]
    cols = np.concatenate([
        384 + np.arange(256), 640 + np.arange(32), 640 + p32,
        1696 + np.arange(128), 1696 + (np.arange(2)[:, None] * 64 + p64[None, :]).reshape(-1),
        1824 + np.arange(128), np.arange(384), 672 + np.arange(512),
        1184 + np.arange(512), 1184 + (h8 * 64 + p64[None, :]).reshape(-1), 1952 + np.arange(512)])
    assert cols.shape[0] == NC0
    w0 = np.ascontiguousarray(e_w_in[:, cols])
    bq = f(inp["e_b_q_norm"])[0]; bk = f(inp["e_b_k_norm"])[0]
    one = lambda n: np.ones(n, np.float32)
    gv0 = np.concatenate([one(256), one(32), one(32), np.tile(bk, 2), np.tile(bk[p64], 2), one(128), one(384),
                          one(512), np.tile(bq, 8), np.tile(bq[p64], 8), one(512)]).astype(np.float32)
    w_uq = f(inp["e_a_w_uq"])[0]
    cu = np.concatenate([(h8 * 96 + np.arange(64)[None, :]).reshape(-1),
                         (h8 * 96 + 64 + np.arange(32)[None, :]).reshape(-1),
                         (h8 * 96 + 64 + p32[None, :]).reshape(-1)])
    wuq = np.ascontiguousarray(w_uq[:, cu])
    w_ukv = f(inp["e_a_w_ukv"])[0]
    ck = np.concatenate([(h8 * 128 + np.arange(64)[None, :]).reshape(-1),
                         (h8 * 128 + 64 + np.arange(64)[None, :]).reshape(-1)])
    wukv = np.ascontiguousarray(w_ukv[:, ck])
    aqc = np.ascontiguousarray(f(inp["e_a_q_norm"])[0].reshape(3, 128).T)
    akvc = np.ascontiguousarray(f(inp["e_a_kv_norm"])[0].reshape(2, 128).T)
    bqk = np.ascontiguousarray(np.stack([np.tile(bq, 2), np.tile(bk, 2)], 1))
    shared = dict(
        ident=np.eye(128, dtype=np.float32),
        wmod0=f(inp["e_w_mod"])[0], wmod1=f(inp["o_w_mod"])[0],
        bmodc0=np.ascontiguousarray(f(inp["e_b_mod"])[0].reshape(24, 128).T),
        bmodc1=np.ascontiguousarray(f(inp["o_b_mod"])[0].reshape(24, 128).T),
        bmodg0=np.ascontiguousarray(f(inp["e_b_mod"])[0][2048:]), bmodg1=np.ascontiguousarray(f(inp["o_b_mod"])[0][2048:]),
        w0=w0, gv0=gv0, wuq=wuq, aqc=aqc, wukv=wukv, akvc=akvc, bqk=bqk,
        wout0=f(inp["e_w_out"])[0], lng0=f(inp["e_ln_g"])[0], lnb0=f(inp["e_ln_b"])[0],
        w1in=f(inp["o_w_in"])[0], vlng=f(inp["o_v_ln_g"])[0], vlnb=f(inp["o_v_ln_b"])[0],
        wsT=np.ascontiguousarray(np.transpose(f(inp["o_w_s"])[0], (0, 2, 1))),
        bs=np.ascontiguousarray(f(inp["o_b_s"])[0].reshape(-1)),
        wout1=f(inp["o_w_out"])[0], lng1=f(inp["o_ln_g"])[0], lnb1=f(inp["o_ln_b"])[0],
    )
    maps = []
    for core in range(8):
        b, half = core // 2, core % 2
        order = np.concatenate([np.arange(half * SQ, (half + 1) * SQ), np.arange((1 - half) * SQ, (2 - half) * SQ)])
        m = dict(shared)
        m["x_all"] = np.ascontiguousarray(x[b][order])
        m["ctx"] = ctx[b]
        m["cc"] = np.ascontiguousarray(np.stack([c[b].reshape(8, 128).T, c_ctx.reshape(8, 128).T], 2))
        m["tabs"] = _rope_tables(order)
        maps.append(m)
    return maps


_CACHE = {}


def kernel(**inputs):
    if "nc" not in _CACHE:
        _CACHE["nc"] = build()
    nc = _CACHE["nc"]
    maps = make_in_maps(inputs)
    res = run_bass_kernel_spmd(nc, maps, core_ids=list(range(8)))
    out = np.empty((4, S, D), np.float32)
    for core in range(8):
        b, half = core // 2, core % 2
        out[b, half * SQ:(half + 1) * SQ] = res.results[core]["out"]
    return out
```

```python
import os
import numpy as np
from contextlib import ExitStack
import concourse.bass as bass
import concourse.mybir as mybir
from concourse.bass_utils import run_bass_kernel_spmd

F32 = mybir.dt.float32
BF16 = mybir.dt.bfloat16
AF = mybir.ActivationFunctionType
ALU = mybir.AluOpType

S = 8192
SQ = 4096
D = 1024
CTX = 256
NK = S + CTX
NKT = NK // 128
EPS = 1e-6
ALPHA = 4.0 ** 0.25
A_SCALE = 96.0 ** -0.5
B_SCALE = 64.0 ** -0.5

O_CKV, O_KR, O_KRP, O_KB, O_KBP, O_VB, O_CQ, O_GA, O_QB, O_QBP, O_GB = (
    0, 256, 288, 320, 448, 576, 704, 1088, 1600, 2112, 2624)
NC0 = 3136


class Sem:
    def __init__(self, nc, es, name):
        self.h = es.enter_context(nc.semaphore(name))
        self.name = name
        self.count = 0


class Buf:
    def __init__(self):
        self.w = {}
        self.r = {}


def _merge(d, ev):
    s, v = ev
    if s.name not in d or d[s.name][1] < v:
        d[s.name] = (s, v)


class T(Buf):
    def __init__(self, t):
        Buf.__init__(self)
        self.t = t


class Eng:
    def __init__(self, name, sem):
        self.name = name
        self.sem = sem
        self.q = []
        self.seen = {}

    def wait(self, ev):
        s, v = ev
        if v <= 0 or self.seen.get(s.name, 0) >= v:
            return
        self.seen[s.name] = v
        self.q.append(("wait", s, v))


class KB:
    def __init__(self, nc, es):
        self.nc = nc
        self.es = es
        self.nsem = 0
        self.pe = Eng("pe", self.sem("pe"))
        self.act = Eng("act", self.sem("act"))
        self.dve = Eng("dve", self.sem("dve"))
        self.pool = Eng("pool", self.sem("pool"))
        self.sp = Eng("sp", None)
        self.dma_sems = []

    def sem(self, name):
        self.nsem += 1
        return Sem(self.nc, self.es, "%s_%d" % (name, self.nsem))

    def dsem(self, name):
        s = self.sem(name)
        self.dma_sems.append(s)
        return s

    def _deps(self, eng, reads, writes):
        for b in reads:
            for ev in list(b.w.values()):
                eng.wait(ev)
            if isinstance(b, PBank):
                for ev in list(b.r.values()):
                    if eng.sem is None or ev[0].name != eng.sem.name:
                        eng.wait(ev)
        for b in writes:
            for ev in list(b.w.values()) + list(b.r.values()):
                eng.wait(ev)

    def op(self, eng, fn, reads=(), writes=()):
        self._deps(eng, reads, writes)
        eng.sem.count += 1
        ev = (eng.sem, eng.sem.count)
        eng.q.append(("op", fn, eng.sem))
        for b in reads:
            _merge(b.r, ev)
        for b in writes:
            _merge(b.w, ev)
        return ev

    def group(self, eng, fns, reads=(), writes=()):
        self._deps(eng, reads, writes)
        for fn in fns[:-1]:
            eng.q.append(("op", fn, None))
        eng.sem.count += 1
        ev = (eng.sem, eng.sem.count)
        eng.q.append(("op", fns[-1], eng.sem))
        for b in reads:
            _merge(b.r, ev)
        for b in writes:
            _merge(b.w, ev)
        return ev

    def dma(self, sem, items, q=None):
        q = q or self.sp
        for (_, _, reads, writes) in items:
            self._deps(q, reads, writes)
        for (o, i, _, _) in items:
            sem.count += 16
            q.q.append(("dma", o, i, sem))
        ev = (sem, sem.count)
        for (_, _, reads, writes) in items:
            for b in reads:
                _merge(b.r, ev)
            for b in writes:
                _merge(b.w, ev)
        return ev

    def flush(self, final_waits=True):
        nc = self.nc
        if final_waits:
            for s in self.dma_sems:
                self.sp.wait((s, s.count))

        def replay(q, e):
            for it in q:
                if it[0] == "wait":
                    e.wait_ge(it[1].h, it[2])
                elif it[0] == "op":
                    ins = getattr(e, it[1][0])(**it[1][1])
                    if it[2] is not None:
                        ins.then_inc(it[2].h, 1)
                else:
                    e.dma_start(out=it[1], in_=it[2]).then_inc(it[3].h, 16)

        with nc.Block() as blk:
            @blk.sync
            def _(e):
                replay(self.sp.q, e)

            @blk.tensor
            def _(e):
                replay(self.pe.q, e)

            @blk.scalar
            def _(e):
                replay(self.act.q, e)

            @blk.vector
            def _(e):
                replay(self.dve.q, e)

            @blk.gpsimd
            def _(e):
                replay(self.pool.q, e)
        for e in (self.sp, self.pe, self.act, self.dve, self.pool):
            e.q = []


class Rot:
    def __init__(self, items):
        self.items = items
        self.i = 0

    def next(self):
        it = self.items[self.i % len(self.items)]
        self.i += 1
        return it


class PBank(Buf):
    def __init__(self, ap):
        Buf.__init__(self)
        self.ap = ap


def build(stop_after=99, dbg=False):
    nc = bass.Bass("TRN2", target_bir_lowering=False)
    es = ExitStack()
    with es:
        def din(name, shape, dt=F32):
            return nc.dram_tensor(name, list(shape), dt, kind="ExternalInput").ap()

        def dscr(name, shape, dt=BF16):
            kind = "ExternalOutput" if dbg else "Internal"
            return nc.dram_tensor(name, list(shape), dt, kind=kind).ap()

        x_all = din("x_all", [S, D])
        ctx = din("ctx", [CTX, D])
        cc_d = din("cc", [128, 8, 2])
        ident_d = din("ident", [128, 128])
        tabs = din("tabs", [4, 128, S])
        wmod_d = [din("wmod0", [D, 3 * D]), din("wmod1", [D, 3 * D])]
        bmodc_d = [din("bmodc0", [128, 24]), din("bmodc1", [128, 24])]
        bmodg_d = [din("bmodg0", [D]), din("bmodg1", [D])]
        w0_d = din("w0", [D, NC0])
        gv0_d = din("gv0", [NC0])
        wuq_d = din("wuq", [384, 1024])
        aqc_d = din("aqc", [128, 3])
        wukv_d = din("wukv", [256, 1024])
        akvc_d = din("akvc", [128, 2])
        bqk_d = din("bqk", [128, 2])
        wout0_d = din("wout0", [D, D])
        lng0_d = din("lng0", [D]); lnb0_d = din("lnb0", [D])
        w1in_d = din("w1in", [D, 3 * D])
        vlng_d = din("vlng", [D]); vlnb_d = din("vlnb", [D])
        wsT_d = din("wsT", [8, 128, 128])
        bs_d = din("bs", [D])
        wout1_d = din("wout1", [D, D])
        lng1_d = din("lng1", [D]); lnb1_d = din("lnb1", [D])
        out_d = nc.dram_tensor("out", [SQ, D], F32, kind="ExternalOutput").ap()

        KnTd = dscr("KnTd", [512, NK])
        KpTd = dscr("KpTd", [32, NK])
        KbTd = dscr("KbTd", [128, NK])
        Vd = dscr("Vd", [10, 128, NKT, 128])
        QnTd = dscr("QnTd", [512, SQ])
        QpTd = dscr("QpTd", [256, SQ])
        QbTd = dscr("QbTd", [512, SQ])
        GTd = dscr("GTd", [1024, SQ])
        MTd = dscr("MTd", [1024, SQ])
        sums_d = nc.dram_tensor("sums_d", [2, 1024], F32, kind="Internal").ap()

        kb = KB(nc, es)

        def sb(name, shape, dt, stack=es):
            return T(stack.enter_context(nc.sbuf_tensor("s_" + name, list(shape), dt)))

        PS = es.enter_context(nc.psum_tensor("ps", [128, 4096], F32))
        banks = [PBank(PS[:, k * 512:(k + 1) * 512]) for k in range(8)]

        ident = sb("ident", [128, 128], F32)
        modc = [sb("modc0", [128, 16, 2], F32), sb("modc1", [128, 16, 2], F32)]
        gate_bc = [sb("gate0", [128, D], F32), sb("gate1", [128, D], F32)]
        epst = sb("epst", [128, 1], F32)
        ones_bf = sb("ones_bf", [128, 128], BF16)
        bdiag = sb("bdiag", [128, 128], BF16)
        sel = sb("sel", [128, 64], F32)

        sem_misc = kb.dsem("misc")

        with ExitStack() as p0:
            wm = sb("wm", [128, 8, 3 * D], F32, p0)
            cc = sb("cc", [128, 8, 2], F32, p0)
            sc = sb("sc", [128, 8, 2], F32, p0)
            screp = sb("screp", [128, 8, 128], F32, p0)
            bmc = [sb("bmc0", [128, 24], F32, p0), sb("bmc1", [128, 24], F32, p0)]
            bmg = [sb("bmg0", [128, D], F32, p0), sb("bmg1", [128, D], F32, p0)]
            sem_wm = kb.dsem("wm")

            kb.dma(sem_misc, [
                (ident.t[:], ident_d[:, :], [], [ident]),
                (cc.t[:], cc_d[:, :, :], [], [cc]),
                (bmc[0].t[:], bmodc_d[0][:, :], [], [bmc[0]]),
                (bmc[1].t[:], bmodc_d[1][:, :], [], [bmc[1]]),
                (bmg[0].t[:], bmodg_d[0].partition_broadcast(128), [], [bmg[0]]),
                (bmg[1].t[:], bmodg_d[1].partition_broadcast(128), [], [bmg[1]]),
            ])
            kb.op(kb.dve, ("memset", dict(ap=epst.t[:], constant=EPS)), writes=[epst])
            kb.op(kb.dve, ("memset", dict(ap=ones_bf.t[:], constant=1.0)), writes=[ones_bf])
            kb.op(kb.dve, ("memset", dict(ap=bdiag.t[:], constant=0.0)), writes=[bdiag])
            kb.op(kb.dve, ("memset", dict(ap=bdiag.t[0:64, 0:64], constant=1.0)), writes=[bdiag])
            kb.op(kb.dve, ("memset", dict(ap=bdiag.t[64:128, 64:128], constant=1.0)), writes=[bdiag])
            kb.op(kb.dve, ("memset", dict(ap=sel.t[:], constant=0.0)), writes=[sel])
            kb.op(kb.dve, ("memset", dict(ap=sel.t[64:65, :], constant=1.0)), writes=[sel])
            kb.op(kb.act, ("activation", dict(out=sc.t[:], in_=cc.t[:], func=AF.Silu)),
                  reads=[cc], writes=[sc])
            for j in range(8):
                kb.op(kb.dve, ("tensor_copy", dict(
                    out=screp.t[:, j, :], in_=sc.t[:, j, 0:1].to_broadcast([128, 128]))),
                    reads=[sc], writes=[screp])
            for L in range(2):
                wv = wmod_d[L].rearrange("(j p) n -> p j n", p=128)
                kb.dma(sem_wm, [(wm.t[:, j, :], wv[:, j, :], [], [wm]) for j in range(8)])
                fns = []
                for k in range(16):
                    for j in range(8):
                        fns.append(("matmul", dict(out=PS[:, 2 * k:2 * k + 2], lhsT=wm.t[:, j, k * 128:(k + 1) * 128],
                            rhs=sc.t[:, j, 0:2], start=(j == 0), stop=(j == 7))))
                kb.group(kb.pe, fns, reads=[wm, sc], writes=[banks[0]])
                kb.op(kb.dve, ("tensor_tensor", dict(
                    out=modc[L].t[:], in0=PS[:, 0:32].rearrange("p (k t) -> p k t", t=2),
                    in1=bmc[L].t[:, 0:16].unsqueeze(2).to_broadcast([128, 16, 2]), op=ALU.add)),
                    reads=[banks[0], bmc[L]], writes=[modc[L]])
                kb.op(kb.dve, ("tensor_scalar", dict(
                    out=modc[L].t[:, 8:16, :], in0=modc[L].t[:, 8:16, :], scalar1=1.0, scalar2=None,
                    op0=ALU.add)), reads=[modc[L]], writes=[modc[L]])
                fns = []
                for blk in range(2):
                    for j in range(8):
                        fns.append(("matmul", dict(out=PS[:, 512 + blk * 512:1024 + blk * 512], lhsT=screp.t[:, j, :],
                            rhs=wm.t[:, j, 2048 + blk * 512:2560 + blk * 512],
                            start=(j == 0), stop=(j == 7))))
                kb.group(kb.pe, fns, reads=[wm, screp], writes=[banks[1], banks[2]])
                kb.op(kb.dve, ("tensor_tensor", dict(
                    out=gate_bc[L].t[:], in0=PS[:, 512:1536], in1=bmg[L].t[:], op=ALU.add)),
                    reads=[banks[1], banks[2], bmg[L]], writes=[gate_bc[L]])
            kb.flush()
        if stop_after == 0:
            return nc

        PH = dict(kb=kb, nc=nc, sb=sb, PS=PS, banks=banks, ident=ident, modc=modc, gate_bc=gate_bc,
                  epst=epst, ones_bf=ones_bf, bdiag=bdiag, sel=sel)
        PH.update(x_all=x_all, ctx=ctx, tabs=tabs, w0_d=w0_d, gv0_d=gv0_d, wuq_d=wuq_d, aqc_d=aqc_d,
                  wukv_d=wukv_d, akvc_d=akvc_d, bqk_d=bqk_d, KnTd=KnTd, KpTd=KpTd, KbTd=KbTd, Vd=Vd,
                  QnTd=QnTd, QpTd=QpTd, QbTd=QbTd, GTd=GTd, MTd=MTd, sums_d=sums_d, wout0_d=wout0_d, lng0_d=lng0_d,
                  lnb0_d=lnb0_d, w1in_d=w1in_d, vlng_d=vlng_d, vlnb_d=vlnb_d, wsT_d=wsT_d, bs_d=bs_d,
                  wout1_d=wout1_d, lng1_d=lng1_d, lnb1_d=lnb1_d, out_d=out_d)
        phase1(PH)
        if stop_after == 1:
            return nc
        phase2(PH)
        if stop_after == 2:
            return nc
        phase3(PH)
        return nc


def phase1(P):
    kb, nc, sb, PS, banks = P["kb"], P["nc"], P["sb"], P["PS"], P["banks"]
    ident, modc, epst, ones_bf, bdiag = P["ident"], P["modc"][0], P["epst"], P["ones_bf"], P["bdiag"]
    pe, act, dve, pool = kb.pe, kb.act, kb.dve, kb.pool
    with ExitStack() as p1:
        W0 = sb("W0", [128, 8, NC0], BF16, p1)
        wuq = sb("wuq", [128, 3, 1024], BF16, p1)
        wukv = sb("wukv", [128, 2, 1024], BF16, p1)
        aqc = sb("aqc", [128, 3], F32, p1)
        akvc = sb("akvc", [128, 2], F32, p1)
        bqk = sb("bqk", [128, 2], F32, p1)
        ginv = sb("ginv", [128, 2], F32, p1)
        p1a = ExitStack()
        gv = sb("gv", [128, NC0], F32, p1a)
        wst = [sb("wst%d" % i, [128, NC0], F32, p1a) for i in range(2)]
        wst_sem = [kb.dsem("wst%d" % i) for i in range(2)]
        sem_c = kb.dsem("p1c")
        kb.dma(sem_c, [
            (gv.t[:], P["gv0_d"].partition_broadcast(128), [], [gv]),
            (aqc.t[:], P["aqc_d"][:, :], [], [aqc]),
            (akvc.t[:], P["akvc_d"][:, :], [], [akvc]),
            (bqk.t[:], P["bqk_d"][:, :], [], [bqk]),
        ])
        kb.op(dve, ("reciprocal", dict(out=ginv.t[:], in_=bqk.t[:])), reads=[bqk], writes=[ginv])
        for j in range(8):
            st = wst[j % 2]
            kb.dma(wst_sem[j % 2], [(st.t[:], P["w0_d"][j * 128:(j + 1) * 128, :], [], [st])])
            eng = dve if j % 2 == 0 else pool
            kb.op(eng, ("tensor_tensor", dict(
                out=W0.t[:, j, :], in0=st.t[:], in1=gv.t[:], op=ALU.mult)), reads=[st, gv], writes=[W0])
        for r in range(3):
            st = wst[r % 2]
            kb.dma(wst_sem[r % 2], [(st.t[:, 0:1024], P["wuq_d"][r * 128:(r + 1) * 128, :], [], [st])])
            kb.op(act, ("activation", dict(
                out=wuq.t[:, r, :], in_=st.t[:, 0:1024], func=AF.Copy, scale=aqc.t[:, r:r + 1])),
                reads=[st, aqc], writes=[wuq])
        for r in range(2):
            st = wst[(r + 1) % 2]
            kb.dma(wst_sem[(r + 1) % 2], [(st.t[:, 0:1024], P["wukv_d"][r * 128:(r + 1) * 128, :], [], [st])])
            kb.op(act, ("activation", dict(
                out=wukv.t[:, r, :], in_=st.t[:, 0:1024], func=AF.Copy, scale=akvc.t[:, r:r + 1])),
                reads=[st, akvc], writes=[wukv])

        kb.flush()
        p1a.close()

        def rot(name, n, shape, dt):
            return Rot([sb("%s%d" % (name, i), shape, dt, p1) for i in range(n)])

        XIN = rot("xin", 2, [128, 4, D], F32)
        TBL = rot("tbl", 2, [128, 4, 512], F32)
        ld_sem = Rot([kb.dsem("ld%d" % i) for i in range(2)])
        XMT = rot("xmT", 2, [128, 8, 512], BF16)
        CKVT = rot("ckvT", 2, [128, 2, 512], BF16)
        SQ2 = rot("sq2", 2, [128, 3, 512], BF16)
        RBC = rot("rbc", 2, [128, 512], F32)
        RCOL = rot("rcol", 2, [128, 4], F32)
        RQ = rot("rq", 1, [128, 512], F32)
        TMP = rot("tmp", 3, [128, 512], F32)
        KNT = rot("knT", 1, [128, 4, 512], BF16)
        KPE = rot("kpe", 2, [32, 512], BF16)
        KBR = rot("kbr", 2, [128, 512], BF16)
        VST = rot("vst", 2, [128, 10, 4, 128], BF16)
        CQT = rot("cqT", 1, [128, 3, 512], BF16)
        GST = rot("gst", 1, [128, 8, 512], BF16)
        QBR = rot("qbr", 1, [128, 4, 512], BF16)
        QNT = rot("qnT", 1, [128, 4, 512], BF16)
        QPT = rot("qpT", 1, [128, 2, 512], BF16)
        st_sems = {}

        def stsem(name, slot):
            key = (name, slot)
            if key not in st_sems:
                st_sems[key] = kb.dsem("st_%s%d" % (name, slot))
            return st_sems[key]

        for v in VST.items:
            kb.op(pool, ("memset", dict(ap=v.t[:], constant=1.0)), writes=[v])
        PSB = Rot(banks)

        nblk = S // 512
        blist = list(range(nblk + 1))
        if os.environ.get("P1_BLOCKS"):
            blist = [int(v) for v in os.environ["P1_BLOCKS"].split(",") if v != "x"]

        loaded = {}

        def issue_load(bi):
            xin, tbl, lsem = XIN.next(), TBL.next(), ld_sem.next()
            if bi == nblk:
                src = P["ctx"].rearrange("(t p) f -> p t f", p=128)
                kb.dma(lsem, [(xin.t[:, 0:2, :], src[:, :, :], [], [xin])])
            else:
                src = P["x_all"].rearrange("(t p) f -> p t f", p=128)
                tsrc = P["tabs"].rearrange("k p t -> p k t")
                kb.dma(lsem, [(xin.t[:], src[:, bi * 4:bi * 4 + 4, :], [], [xin]),
                              (tbl.t[:], tsrc[:, :, bi * 512:bi * 512 + 512], [], [tbl])])
            loaded[bi] = (xin, tbl)

        def do_block(bi):
            is_ctx = (bi == nblk)
            nt = 256 if is_ctx else 512
            ntt = nt // 128
            do_q = (not is_ctx) and bi < SQ // 512
            k0 = bi * 512
            mc = 1 if is_ctx else 0
            slot = bi % 2
            if bi not in loaded:
                issue_load(bi)
            xin, tbl = loaded.pop(bi)
            TBc, TBs, TAc, TAs = (tbl.t[:, i, :] for i in range(4))
            xmT = XMT.next()
            for j in range(8):
                pb = PSB.next()
                kb.group(pe, [("transpose", dict(
                    out=pb.ap[:, tt * 128:(tt + 1) * 128], in_=xin.t[:, tt, j * 128:(j + 1) * 128],
                    identity=ident.t[:])) for tt in range(ntt)], reads=[xin, ident], writes=[pb])
                kb.op(act, ("activation", dict(
                    out=xmT.t[:, j, 0:nt], in_=pb.ap[:, 0:nt], func=AF.Identity,
                    scale=modc.t[:, 8 + j, mc:mc + 1], bias=modc.t[:, j, mc:mc + 1])),
                    reads=[pb, modc], writes=[xmT])

            nxt = blist[blist.index(bi) + 1] if blist.index(bi) + 1 < len(blist) else None
            if nxt is not None:
                issue_load(nxt)

            def proj(off, M):
                pb = PSB.next()
                kb.group(pe, [("matmul", dict(out=pb.ap[0:M, 0:nt], lhsT=W0.t[:, j, off:off + M], rhs=xmT.t[:, j, 0:nt],
                    start=(j == 0), stop=(j == 7))) for j in range(8)], reads=[W0, xmT], writes=[pb])
                return pb

            def rstd_from(pb, npart, ncol, scale, dst_ap, dst):
                kb.op(act, ("activation", dict(out=dst_ap, in_=pb.ap[0:npart, 0:ncol], func=AF.Ln,
                                                  scale=scale, bias=epst.t[0:npart, 0:1])),
                      reads=[pb, epst], writes=[dst])
                kb.op(act, ("activation", dict(out=dst_ap, in_=dst_ap, func=AF.Exp, scale=-0.5)),
                      reads=[dst], writes=[dst])

            ckvT = CKVT.next()
            sq = SQ2.next()
            for r in range(2):
                pb = proj(O_CKV + r * 128, 128)
                kb.op(dve, ("tensor_copy", dict(out=ckvT.t[:, r, 0:nt], in_=pb.ap[:, 0:nt])),
                      reads=[pb], writes=[ckvT])
                kb.op(act, ("activation", dict(out=sq.t[:, r, 0:nt], in_=pb.ap[:, 0:nt],
                                                              func=AF.Square)), reads=[pb], writes=[sq])
            pbR = PSB.next()
            kb.group(pe, [("matmul", dict(out=pbR.ap[:, 0:nt], lhsT=ones_bf.t[:], rhs=sq.t[:, r, 0:nt], start=(r == 0), stop=(r == 1)))
                for r in range(2)], reads=[ones_bf, sq], writes=[pbR])
            rkv = RBC.next()
            rstd_from(pbR, 128, nt, 1.0 / 256, rkv.t[:, 0:nt], rkv)
            pbc = PSB.next()
            fns = []
            for tt in range(ntt):
                for r in range(2):
                    fns.append(("matmul", dict(out=pbc.ap[:, tt:tt + 1], lhsT=sq.t[:, r, tt * 128:(tt + 1) * 128],
                        rhs=ones_bf.t[:, 0:1], start=(r == 0), stop=(r == 1))))
            kb.group(pe, fns, reads=[sq, ones_bf], writes=[pbc])
            rcol = RCOL.next()
            rstd_from(pbc, 128, ntt, 1.0 / 256, rcol.t[:, 0:ntt], rcol)
            knT = KNT.next()
            for c in range(4):
                pb = PSB.next()
                kb.group(pe, [("matmul", dict(out=pb.ap[:, 0:nt], lhsT=wukv.t[:, r, c * 128:(c + 1) * 128], rhs=ckvT.t[:, r, 0:nt],
                    start=(r == 0), stop=(r == 1))) for r in range(2)], reads=[wukv, ckvT], writes=[pb])
                kb.op(dve, ("tensor_tensor", dict(
                    out=knT.t[:, c, 0:nt], in0=pb.ap[:, 0:nt], in1=rkv.t[:, 0:nt], op=ALU.mult)),
                    reads=[pb, rkv], writes=[knT])
            kb.dma(stsem("kn", slot), [(P["KnTd"].rearrange("(c p) t -> p c t", p=128)[:, :, k0:k0 + nt],
                                        knT.t[:, :, 0:nt], [knT], [])])
            vst = VST.next()
            for tt in range(ntt):
                pb = PSB.next()
                kb.group(pe, [("matmul", dict(out=pb.ap[:, 0:512], lhsT=ckvT.t[:, r, tt * 128:(tt + 1) * 128], rhs=wukv.t[:, r, 512:1024],
                    start=(r == 0), stop=(r == 1))) for r in range(2)], reads=[wukv, ckvT], writes=[pb])
                kb.op(dve, ("tensor_scalar", dict(
                    out=vst.t[:, 0:8, tt, 0:64], in0=pb.ap[:, 0:512].rearrange("p (h e) -> p h e", e=64),
                    scalar1=rcol.t[:, tt:tt + 1], scalar2=None, op0=ALU.mult)),
                    reads=[pb, rcol], writes=[vst])
            for tt in range(ntt):
                pb = PSB.next()
                kb.group(pe, [("matmul", dict(out=pb.ap[:, 0:128], lhsT=xmT.t[:, j, tt * 128:(tt + 1) * 128],
                    rhs=W0.t[:, j, O_VB:O_VB + 128], start=(j == 0), stop=(j == 7))) for j in range(8)],
                    reads=[W0, xmT], writes=[pb])
                kb.op(act, ("activation", dict(
                    out=vst.t[:, 8:10, tt, 0:64], in_=pb.ap[:, 0:128].rearrange("p (h e) -> p h e", e=64),
                    func=AF.Copy)), reads=[pb], writes=[vst])
            kt0 = k0 // 128
            kb.dma(stsem("v", slot), [(P["Vd"].rearrange("h p k e -> p h k e")[:, :, kt0:kt0 + ntt, :],
                                       vst.t[:, :, 0:ntt, :], [vst], [])])
            kpe = KPE.next()
            pb1 = proj(O_KR, 32)
            if is_ctx:
                kb.op(dve, ("tensor_copy", dict(out=kpe.t[:, 0:nt], in_=pb1.ap[0:32, 0:nt])),
                      reads=[pb1], writes=[kpe])
            else:
                pb2 = proj(O_KRP, 32)
                t1, t2 = TMP.next(), TMP.next()
                kb.op(dve, ("tensor_tensor", dict(out=t1.t[0:32, :], in0=pb1.ap[0:32, :], in1=TAc[0:32, :],
                                                     op=ALU.mult)), reads=[pb1, tbl], writes=[t1])
                kb.op(dve, ("tensor_tensor", dict(out=t2.t[0:32, :], in0=pb2.ap[0:32, :], in1=TAs[0:32, :],
                                                     op=ALU.mult)), reads=[pb2, tbl], writes=[t2])
                kb.op(pool, ("tensor_tensor", dict(out=kpe.t[:, :], in0=t1.t[0:32, :], in1=t2.t[0:32, :],
                                                      op=ALU.add)), reads=[t1, t2], writes=[kpe])
            kb.dma(stsem("kp", slot), [(P["KpTd"][:, k0:k0 + nt], kpe.t[:, 0:nt], [kpe], [])])

            def normrope(off, offp, gcol, dst_ap, dst, rope):
                pb1 = proj(off, 128)
                sqk = SQ2.next()
                kb.op(act, ("activation", dict(out=sqk.t[:, 0, 0:nt], in_=pb1.ap[:, 0:nt], func=AF.Square,
                                                  scale=ginv.t[:, gcol:gcol + 1])), reads=[pb1, ginv], writes=[sqk])
                pbR = PSB.next()
                kb.group(pe, [("matmul", dict(out=pbR.ap[:, 0:nt], lhsT=bdiag.t[:], rhs=sqk.t[:, 0, 0:nt],
                                                 start=True, stop=True))], reads=[bdiag, sqk], writes=[pbR])
                rr = RBC.next()
                rstd_from(pbR, 128, nt, 1.0 / 64, rr.t[:, 0:nt], rr)
                if not rope:
                    kb.op(dve, ("tensor_tensor", dict(out=dst_ap, in0=pb1.ap[:, 0:nt], in1=rr.t[:, 0:nt],
                                                         op=ALU.mult)), reads=[pb1, rr], writes=[dst])
                    return
                pb2 = proj(offp, 128)
                t1, t2 = TMP.next(), TMP.next()
                kb.op(dve, ("tensor_tensor", dict(out=t1.t[:], in0=pb1.ap[:, :], in1=TBc, op=ALU.mult)),
                      reads=[pb1, tbl], writes=[t1])
                kb.op(dve, ("tensor_tensor", dict(out=t2.t[:], in0=pb2.ap[:, :], in1=TBs, op=ALU.mult)),
                      reads=[pb2, tbl], writes=[t2])
                kb.op(pool, ("tensor_tensor", dict(out=t1.t[:], in0=t1.t[:], in1=t2.t[:], op=ALU.add)),
                      reads=[t1, t2], writes=[t1])
                kb.op(dve, ("tensor_tensor", dict(out=dst_ap, in0=t1.t[:], in1=rr.t[:], op=ALU.mult)),
                      reads=[t1, rr], writes=[dst])

            kbr = KBR.next()
            normrope(O_KB, O_KBP, 1, kbr.t[:, 0:nt], kbr, not is_ctx)
            kb.dma(stsem("kb", slot), [(P["KbTd"][:, k0:k0 + nt], kbr.t[:, 0:nt], [kbr], [])])

            if not do_q:
                return
            cqT = CQT.next()
            sq3 = SQ2.next()
            for r in range(3):
                pb = proj(O_CQ + r * 128, 128)
                kb.op(dve, ("tensor_copy", dict(out=cqT.t[:, r, :], in_=pb.ap[:, :])),
                      reads=[pb], writes=[cqT])
                kb.op(act, ("activation", dict(out=sq3.t[:, r, :], in_=pb.ap[:, :],
                                                              func=AF.Square)), reads=[pb], writes=[sq3])
            pbR = PSB.next()
            kb.group(pe, [("matmul", dict(out=pbR.ap[:, :], lhsT=ones_bf.t[:], rhs=sq3.t[:, r, :], start=(r == 0), stop=(r == 2)))
                for r in range(3)], reads=[ones_bf, sq3], writes=[pbR])
            rq = RQ.next()
            rstd_from(pbR, 128, 512, 1.0 / 384, rq.t[:], rq)
            gst = GST.next()
            for gi, off in enumerate((O_GA, O_GB)):
                for c in range(4):
                    pb = proj(off + c * 128, 128)
                    kb.op(act, ("activation", dict(
                        out=gst.t[:, gi * 4 + c, :], in_=pb.ap[:, :], func=AF.Silu)), reads=[pb], writes=[gst])
            kb.dma(stsem("g", slot), [(P["GTd"].rearrange("(c p) t -> p c t", p=128)[:, :, k0:k0 + 512],
                                       gst.t[:], [gst], [])])
            qbr = QBR.next()
            for c in range(4):
                normrope(O_QB + c * 128, O_QBP + c * 128, 0, qbr.t[:, c, :], qbr, True)
            kb.dma(stsem("qb", slot), [(P["QbTd"].rearrange("(c p) t -> p c t", p=128)[:, :, k0:k0 + 512],
                                        qbr.t[:], [qbr], [])])
            qnT = QNT.next()
            for c in range(4):
                pb = PSB.next()
                kb.group(pe, [("matmul", dict(out=pb.ap[:, :], lhsT=wuq.t[:, r, c * 128:(c + 1) * 128], rhs=cqT.t[:, r, :],
                    start=(r == 0), stop=(r == 2))) for r in range(3)], reads=[wuq, cqT], writes=[pb])
                kb.op(dve, ("tensor_tensor", dict(
                    out=qnT.t[:, c, :], in0=pb.ap[:, :], in1=rq.t[:], op=ALU.mult)),
                    reads=[pb, rq], writes=[qnT])
            kb.dma(stsem("qn", slot), [(P["QnTd"].rearrange("(c p) t -> p c t", p=128)[:, :, k0:k0 + 512],
                                        qnT.t[:], [qnT], [])])
            qpT = QPT.next()
            for c in range(2):
                pbs = []
                for base in (512, 768):
                    pb = PSB.next()
                    kb.group(pe, [("matmul", dict(out=pb.ap[:, :], lhsT=wuq.t[:, r, base + c * 128:base + (c + 1) * 128], rhs=cqT.t[:, r, :],
                        start=(r == 0), stop=(r == 2))) for r in range(3)], reads=[wuq, cqT], writes=[pb])
                    pbs.append(pb)
                t1, t2 = TMP.next(), TMP.next()
                kb.op(dve, ("tensor_tensor", dict(out=t1.t[:], in0=pbs[0].ap[:, :], in1=TAc,
                                                                       op=ALU.mult)), reads=[pbs[0], tbl], writes=[t1])
                kb.op(dve, ("tensor_tensor", dict(out=t2.t[:], in0=pbs[1].ap[:, :], in1=TAs,
                                                                       op=ALU.mult)), reads=[pbs[1], tbl], writes=[t2])
                kb.op(pool, ("tensor_tensor", dict(out=t1.t[:], in0=t1.t[:], in1=t2.t[:],
                                                                    op=ALU.add)), reads=[t1, t2], writes=[t1])
                kb.op(dve, ("tensor_tensor", dict(out=qpT.t[:, c, :], in0=t1.t[:], in1=rq.t[:],
                                                                 op=ALU.mult)), reads=[t1, rq], writes=[qpT])
            kb.dma(stsem("qp", slot), [(P["QpTd"].rearrange("(c p) t -> p c t", p=128)[:, :, k0:k0 + 512],
                                        qpT.t[:], [qpT], [])])

        for bi in blist:
            do_block(bi)
        kb.flush()


def phase2(P):
    kb, nc, sb, PS = P["kb"], P["nc"], P["sb"], P["PS"]
    pe, act, dve, pool = kb.pe, kb.act, kb.dve, kb.pool
    NB = 3
    with ExitStack() as p2:
        KT = [sb("KT%d" % i, [128, NK], BF16, p2) for i in range(2)]
        VV = [sb("VV%d" % i, [128, NKT, 128], BF16, p2) for i in range(2)]
        QT = [sb("QT%d" % i, [128, SQ], BF16, p2) for i in range(2)]
        GT = [sb("GT%d" % i, [64, SQ], BF16, p2) for i in range(2)]
        hsem = [kb.dsem("hd%d" % i) for i in range(2)]
        PT = [sb("PT%d" % i, [128, 1024], BF16, p2) for i in range(NB)]
        OCP = [sb("ocp%d" % i, [128, 1024], F32, p2) for i in range(2)]
        BC = [sb("bc%d" % i, [64, 1024], F32, p2) for i in range(2)]
        bsem = [kb.dsem("bcs%d" % i) for i in range(2)]
        bsem2 = [kb.dsem("bcl%d" % i) for i in range(2)]
        SUMd = [Buf(), Buf()]
        M1 = sb("m1", [64, 1024], F32, p2)
        MST = [sb("mst%d" % i, [64, 1024], BF16, p2) for i in range(2)]
        msem = [kb.dsem("mst%d" % i) for i in range(2)]
        psS = [PBank(PS[:, i * 1024:(i + 1) * 1024]) for i in range(NB)]
        psO = PBank(PS[:, 3072:4096])
        sums_d = P["sums_d"]

        for sl0 in range(2):
            kb.op(pool, ("memset", dict(ap=KT[sl0].t[64:128, :], constant=0.0)), writes=[KT[sl0]])
            kb.op(dve, ("memset", dict(ap=QT[sl0].t[64:128, :], constant=0.0)), writes=[QT[sl0]])

        def load_head(h16):
            sl = h16 % 2
            kt, vv, qt, gt = KT[sl], VV[sl], QT[sl], GT[sl]
            items = []
            if h16 in (8, 9):
                kb.op(pool, ("memset", dict(ap=kt.t[64:96, :], constant=0.0)), writes=[kt])
                kb.op(dve, ("memset", dict(ap=qt.t[64:96, :], constant=0.0)), writes=[qt])
            if h16 < 8:
                h = h16
                items.append((kt.t[0:64, :], P["KnTd"][h * 64:(h + 1) * 64, :], [], [kt]))
                items.append((kt.t[64:96, :], P["KpTd"][:, :], [], [kt]))
                items.append((vv.t[:], P["Vd"][h], [], [vv]))
                items.append((qt.t[0:64, :], P["QnTd"][h * 64:(h + 1) * 64, :], [], [qt]))
                items.append((qt.t[64:96, :], P["QpTd"][h * 32:(h + 1) * 32, :], [], [qt]))
            else:
                hb = h16 - 8
                kvh = hb // 4
                items.append((kt.t[0:64, :], P["KbTd"][kvh * 64:(kvh + 1) * 64, :], [], [kt]))
                items.append((vv.t[:], P["Vd"][8 + kvh], [], [vv]))
                items.append((qt.t[0:64, :], P["QbTd"][hb * 64:(hb + 1) * 64, :], [], [qt]))
            items.append((gt.t[:], P["GTd"][h16 * 64:(h16 + 1) * 64, :], [], [gt]))
            kb.dma(hsem[sl], items)

        iters = [(h16, sbk, kt) for h16 in range(16) for sbk in range(SQ // 1024) for kt in range(NKT)]
        N = len(iters)

        def emit_S(n):
            h16, sbk, kt = iters[n]
            sl = h16 % 2
            kd = 128
            ps = psS[n % NB]
            kb.group(pe, [("matmul", dict(out=ps.ap[:, hf * 512:(hf + 1) * 512],
                                          lhsT=KT[sl].t[0:kd, kt * 128:(kt + 1) * 128],
                                          rhs=QT[sl].t[0:kd, sbk * 1024 + hf * 512:sbk * 1024 + (hf + 1) * 512],
                                          start=True, stop=True)) for hf in range(2)],
                     reads=[KT[sl], QT[sl]], writes=[ps])

        def emit_exp(n):
            h16, sbk, kt = iters[n]
            sc = A_SCALE if h16 < 8 else B_SCALE
            kb.op(act, ("activation", dict(out=PT[n % NB].t[:], in_=psS[n % NB].ap[:, :], func=AF.Exp, scale=sc)),
                  reads=[psS[n % NB]], writes=[PT[n % NB]])

        def emit_PV(n):
            h16, sbk, kt = iters[n]
            sl = h16 % 2
            kb.group(pe, [("matmul", dict(out=psO.ap[:, hf * 512:(hf + 1) * 512], lhsT=VV[sl].t[:, kt, :],
                                          rhs=PT[n % NB].t[:, hf * 512:(hf + 1) * 512],
                                          start=(kt == 0), stop=(kt == NKT - 1))) for hf in range(2)],
                     reads=[VV[sl], PT[n % NB]], writes=[psO])

        ep = [0]

        def emit_epi1(n):
            e = ep[0]
            ocp = OCP[e % 2]
            kb.op(dve, ("tensor_copy", dict(out=ocp.t[:], in_=psO.ap[:, :])), reads=[psO], writes=[ocp])
            kb.dma(bsem[e % 2], [(sums_d[e % 2:e % 2 + 1, :], ocp.t[64:65, :], [ocp], [SUMd[e % 2]])])
            kb.dma(bsem2[e % 2], [(BC[e % 2].t[:], sums_d[e % 2, :].partition_broadcast(64), [SUMd[e % 2]], [BC[e % 2]])])

        def emit_epi2(n):
            h16, sbk, kt = iters[n]
            sl = h16 % 2
            e = ep[0]
            ep[0] += 1
            ocp, mst, bc = OCP[e % 2], MST[e % 2], BC[e % 2]
            kb.op(dve, ("reciprocal", dict(out=bc.t[:], in_=bc.t[:])), reads=[bc], writes=[bc])
            kb.op(dve, ("tensor_tensor", dict(out=M1.t[:], in0=ocp.t[0:64, :], in1=bc.t[:], op=ALU.mult)),
                  reads=[ocp, bc], writes=[M1])
            kb.op(dve, ("tensor_tensor", dict(out=mst.t[:], in0=M1.t[:],
                                              in1=GT[sl].t[:, sbk * 1024:(sbk + 1) * 1024], op=ALU.mult)),
                  reads=[M1, GT[sl]], writes=[mst])
            kb.dma(msem[e % 2], [(P["MTd"][h16 * 64:(h16 + 1) * 64, sbk * 1024:(sbk + 1) * 1024], mst.t[:],
                                  [mst], [])])
            if sbk == SQ // 1024 - 1 and h16 + 2 < 16:
                load_head(h16 + 2)

        load_head(0)
        load_head(1)
        pending = []
        emit_S(0)
        emit_S(1)
        for n in range(N):
            h16, sbk, kt = iters[n]
            if n + 2 < N:
                emit_S(n + 2)
            emit_exp(n)
            emit_PV(n)
            if kt == NKT - 1:
                emit_epi1(n)
                pending.append((n + 6, n))
            if pending and (pending[0][0] <= n or n == N - 1):
                emit_epi2(pending.pop(0)[1])
        while pending:
            emit_epi2(pending.pop(0)[1])
        kb.flush()


def phase3(P):
    kb, nc, sb, PS = P["kb"], P["nc"], P["sb"], P["PS"]
    ident, epst = P["ident"], P["epst"]
    modc1 = P["modc"][1]
    gate0, gate1 = P["gate_bc"]
    pe, act, dve, pool = kb.pe, kb.act, kb.dve, kb.pool
    with ExitStack() as p3:
        w0o = sb("w0o", [128, 8, D], BF16, p3)
        w1i = sb("w1i", [128, 8, 3 * D], BF16, p3)
        w1o = sb("w1o", [128, 8, D], BF16, p3)
        wsT = sb("wsT", [128, 8, 128], BF16, p3)
        bct = {}
        for nm in ("lng0", "lnb0", "lng1", "lnb1", "vlng", "vlnb"):
            bct[nm] = sb("bc_" + nm, [128, D], F32, p3)
        bsb = sb("bsb", [128, 8, 128], F32, p3)
        sem_c = kb.dsem("p3c")
        kb.dma(sem_c, [(bct[nm].t[:], P[nm + "_d"].partition_broadcast(128), [], [bct[nm]]) for nm in bct] +
               [(bsb.t[:], P["bs_d"].partition_broadcast(128), [], [bsb])])
        p3a = ExitStack()
        wst = [sb("wst3_%d" % i, [128, 3 * D], F32, p3a) for i in range(2)]
        wsem = [kb.dsem("wst3_%d" % i) for i in range(2)]
        k = 0
        for j in range(8):
            st = wst[k % 2]
            kb.dma(wsem[k % 2], [(st.t[:], P["w1in_d"][j * 128:(j + 1) * 128, :], [], [st])])
            kb.op(act if j % 2 == 0 else dve, ("activation" if j % 2 == 0 else "tensor_copy",
                                               dict(out=w1i.t[:, j, :], in_=st.t[:], func=AF.Copy) if j % 2 == 0
                                               else dict(out=w1i.t[:, j, :], in_=st.t[:])),
                  reads=[st], writes=[w1i])
            k += 1
        for (dst, src, gt) in ((w0o, "wout0_d", gate0), (w1o, "wout1_d", gate1)):
            for j2 in range(4):
                st = wst[k % 2]
                kb.dma(wsem[k % 2], [(st.t[:, 0:2048].rearrange("p (a n) -> p a n", a=2),
                                      P[src][j2 * 256:(j2 + 1) * 256, :].rearrange("(a p) n -> p a n", p=128),
                                      [], [st])])
                for a in range(2):
                    kb.op(dve if a == 0 else pool, ("tensor_tensor", dict(
                        out=dst.t[:, j2 * 2 + a, :], in0=st.t[:, a * 1024:(a + 1) * 1024], in1=gt.t[:], op=ALU.mult)),
                        reads=[st, gt], writes=[dst])
                k += 1
        st = wst[k % 2]
        kb.dma(wsem[k % 2], [(st.t[:, 0:1024].rearrange("p (g q) -> p g q", g=8),
                              P["wsT_d"].rearrange("g q p -> q g p"), [], [st])])
        kb.op(dve, ("tensor_copy", dict(out=wsT.t[:], in_=st.t[:, 0:1024].rearrange("p (g q) -> p g q", g=8))),
              reads=[st], writes=[wsT])
        kb.flush()
        p3a.close()

        def rot(name, n, shape, dt):
            return Rot([sb("%s%d" % (name, i), shape, dt, p3) for i in range(n)])

        MTB = rot("mtb", 1, [128, 8, 512], BF16)
        mt_sem = Rot([kb.dsem("mtb%d" % i) for i in range(2)])
        XR = rot("xr", 2, [128, D], F32)
        xr_sem = Rot([kb.dsem("xr%d" % i) for i in range(2)])
        X1 = rot("x1", 1, [128, 4, D], F32)
        XMT = rot("x1mT", 1, [128, 8, 512], BF16)
        UG = rot("ug", 1, [128, 8, 512], F32)
        WK = rot("wk", 3, [128, D], F32)
        TS = rot("tsil", 2, [128, 512], F32)
        STT = rot("stt", 2, [128, 2, 6], F32)
        MV = rot("mv", 2, [128, 2], F32)
        RS = rot("rs", 2, [128, 1], F32)
        VLN = rot("vln", 2, [128, D], BF16)
        ZT = rot("zT", 2, [128, 8, 128], BF16)
        OUT = rot("outt", 2, [128, D], F32)
        out_sem = Rot([kb.dsem("out%d" % i) for i in range(2)])
        PS2 = Rot([PBank(PS[:, 0:1024]), PBank(PS[:, 1024:2048]), PBank(PS[:, 2048:3072])])
        PS1 = Rot([PBank(PS[:, 3072:3584]), PBank(PS[:, 3584:4096])])

        def layer_norm(src, dst_ap, dst, g, b):
            stt, mv, rs = STT.next(), MV.next(), RS.next()
            kb.group(dve, [("bn_stats", dict(out=stt.t[:, hh, :], in_=src.t[:, hh * 512:(hh + 1) * 512]))
                           for hh in range(2)], reads=[src], writes=[stt])
            kb.op(dve, ("bn_aggr", dict(out=mv.t[:], in_=stt.t[:])), reads=[stt], writes=[mv])
            kb.op(act, ("activation", dict(out=rs.t[:], in_=mv.t[:, 1:2], func=AF.Ln, bias=epst.t[:, 0:1])),
                  reads=[mv, epst], writes=[rs])
            kb.op(act, ("activation", dict(out=rs.t[:], in_=rs.t[:], func=AF.Exp, scale=-0.5)),
                  reads=[rs], writes=[rs])
            kb.op(dve, ("tensor_scalar", dict(out=src.t[:], in0=src.t[:], scalar1=mv.t[:, 0:1], scalar2=rs.t[:, 0:1],
                                              op0=ALU.subtract, op1=ALU.mult)), reads=[src, mv, rs], writes=[src])
            kb.op(pool, ("tensor_tensor", dict(out=src.t[:], in0=src.t[:], in1=g.t[:], op=ALU.mult)),
                  reads=[src, g], writes=[src])
            kb.op(pool, ("tensor_tensor", dict(out=dst_ap, in0=src.t[:], in1=b.t[:], op=ALU.add)),
                  reads=[src, b], writes=[dst])

        for bi in range(SQ // 512):
            k0 = bi * 512
            mt, msem = MTB.next(), mt_sem.next()
            kb.dma(msem, [(mt.t[:], P["MTd"].rearrange("(c p) t -> p c t", p=128)[:, :, k0:k0 + 512], [], [mt])])
            x1 = X1.next()
            for tt in range(4):
                xr, xsem = XR.next(), xr_sem.next()
                r0 = k0 + tt * 128
                kb.dma(xsem, [(xr.t[:], P["x_all"][r0:r0 + 128, :], [], [xr])])
                psY = PS2.next()
                kb.group(pe, [("matmul", dict(out=psY.ap[:, cb * 512:(cb + 1) * 512],
                                              lhsT=mt.t[:, c, tt * 128:(tt + 1) * 128],
                                              rhs=w0o.t[:, c, cb * 512:(cb + 1) * 512], start=(c == 0), stop=(c == 7)))
                              for cb in range(2) for c in range(8)], reads=[mt, w0o], writes=[psY])
                r = WK.next()
                kb.op(dve, ("scalar_tensor_tensor", dict(out=r.t[:], in0=xr.t[:], scalar=ALPHA, in1=psY.ap[:, :],
                                                         op0=ALU.mult, op1=ALU.add)), reads=[xr, psY], writes=[r])
                layer_norm(r, x1.t[:, tt, :], x1, bct["lng0"], bct["lnb0"])
            xmT = XMT.next()
            for j in range(8):
                pb = PS1.next()
                kb.group(pe, [("transpose", dict(out=pb.ap[:, tt * 128:(tt + 1) * 128],
                                                 in_=x1.t[:, tt, j * 128:(j + 1) * 128], identity=ident.t[:]))
                              for tt in range(4)], reads=[x1, ident], writes=[pb])
                kb.op(act, ("activation", dict(out=xmT.t[:, j, :], in_=pb.ap[:, :], func=AF.Identity,
                                               scale=modc1.t[:, 8 + j, 0:1], bias=modc1.t[:, j, 0:1])),
                      reads=[pb, modc1], writes=[xmT])
            ug = UG.next()
            for c in range(8):
                pb = PS1.next()
                kb.group(pe, [("matmul", dict(out=pb.ap[:, :], lhsT=w1i.t[:, j, c * 128:(c + 1) * 128],
                                              rhs=xmT.t[:, j, :], start=(j == 0), stop=(j == 7))) for j in range(8)],
                         reads=[w1i, xmT], writes=[pb])
                kb.op(act, ("activation", dict(out=ug.t[:, c, :], in_=pb.ap[:, :], func=AF.Gelu_apprx_tanh)),
                      reads=[pb], writes=[ug])
            for c in range(8):
                pb = PS1.next()
                kb.group(pe, [("matmul", dict(out=pb.ap[:, :], lhsT=w1i.t[:, j, 2048 + c * 128:2048 + (c + 1) * 128],
                                              rhs=xmT.t[:, j, :], start=(j == 0), stop=(j == 7))) for j in range(8)],
                         reads=[w1i, xmT], writes=[pb])
                ts = TS.next()
                kb.op(act, ("activation", dict(out=ts.t[:], in_=pb.ap[:, :], func=AF.Silu)), reads=[pb], writes=[ts])
                kb.op(pool, ("tensor_tensor", dict(out=ug.t[:, c, :], in0=ug.t[:, c, :], in1=ts.t[:], op=ALU.mult)),
                      reads=[ug, ts], writes=[ug])
            vln_of, zT_of = {}, {}

            def s1(tt):
                psV = PS2.next()
                kb.group(pe, [("matmul", dict(out=psV.ap[:, cb * 512:(cb + 1) * 512],
                                              lhsT=xmT.t[:, j, tt * 128:(tt + 1) * 128],
                                              rhs=w1i.t[:, j, 1024 + cb * 512:1536 + cb * 512],
                                              start=(j == 0), stop=(j == 7))) for cb in range(2) for j in range(8)],
                         reads=[xmT, w1i], writes=[psV])
                vg = WK.next()
                kb.op(act, ("activation", dict(out=vg.t[:], in_=psV.ap[:, :], func=AF.Gelu_apprx_tanh)),
                      reads=[psV], writes=[vg])
                vln = VLN.next()
                layer_norm(vg, vln.t[:], vln, bct["vlng"], bct["vlnb"])
                vln_of[tt] = vln

            def s2(tt):
                vln = vln_of[tt]
                psM = PS2.next()
                kb.group(pe, [("matmul", dict(out=psM.ap[:, g * 128:(g + 1) * 128], lhsT=vln.t[:, g * 128:(g + 1) * 128],
                                              rhs=wsT.t[:, g, :], start=True, stop=True)) for g in range(8)],
                         reads=[vln, wsT], writes=[psM])
                tz = WK.next()
                kb.op(dve, ("tensor_tensor", dict(out=tz.t[:].rearrange("p (g q) -> p g q", g=8),
                                                  in0=psM.ap[:, :].rearrange("p (g q) -> p g q", g=8),
                                                  in1=bsb.t[:], op=ALU.add)), reads=[psM, bsb], writes=[tz])
                zT = ZT.next()
                kb.op(dve, ("tensor_tensor", dict(out=zT.t[:], in0=tz.t[:].rearrange("p (g q) -> p g q", g=8),
                                                  in1=ug.t[:, :, tt * 128:(tt + 1) * 128], op=ALU.mult)),
                      reads=[tz, ug], writes=[zT])
                zT_of[tt] = zT

            def s3(tt):
                zT = zT_of[tt]
                psY1 = PS2.next()
                kb.group(pe, [("matmul", dict(out=psY1.ap[:, cb * 512:(cb + 1) * 512], lhsT=zT.t[:, g, :],
                                              rhs=w1o.t[:, g, cb * 512:(cb + 1) * 512], start=(g == 0), stop=(g == 7)))
                              for cb in range(2) for g in range(8)], reads=[zT, w1o], writes=[psY1])
                r1 = WK.next()
                kb.op(dve, ("scalar_tensor_tensor", dict(out=r1.t[:], in0=x1.t[:, tt, :], scalar=ALPHA,
                                                         in1=psY1.ap[:, :], op0=ALU.mult, op1=ALU.add)),
                      reads=[x1, psY1], writes=[r1])
                ot, osem = OUT.next(), out_sem.next()
                layer_norm(r1, ot.t[:], ot, bct["lng1"], bct["lnb1"])
                r0 = k0 + tt * 128
                kb.dma(osem, [(P["out_d"][r0:r0 + 128, :], ot.t[:], [ot], [])])

            for st, tt in ((s1, 0), (s1, 1), (s2, 0), (s1, 2), (s3, 0), (s2, 1), (s1, 3), (s3, 1), (s2, 2),
                           (s3, 2), (s2, 3), (s3, 3)):
                st(tt)
        kb.flush()


def _partner(d):
    q = d // 4
    i = np.arange(d)
    r = i % (d // 2)
    partner = np.where(r < q, i + q, i - q)
    sign = np.where(r < q, -1.0, 1.0).astype(np.float32)
    return partner, sign


def _rope_tables(pos):
    out = []
    row = (pos // 64).astype(np.float32)
    col = (pos % 64).astype(np.float32)
    for d, rep in ((64, 2), (32, 4)):
        d_axis = d // 2
        inv = (np.float32(10000.0) ** (-np.arange(0, d_axis, 2, dtype=np.float32) / np.float32(d_axis))).astype(np.float32)
        _, sign = _partner(d)
        i = np.arange(d)
        axis = i // d_axis
        j = i % (d // 4)
        p = np.where(axis[:, None] == 0, row[None, :], col[None, :]).astype(np.float32)
        ang = (p * inv[j][:, None]).astype(np.float32)
        c = np.cos(ang).astype(np.float32)
        s = (np.sin(ang).astype(np.float32) * sign[:, None]).astype(np.float32)
        out.append(np.tile(c, (rep, 1)))
        out.append(np.tile(s, (rep, 1)))
    return np.ascontiguousarray(np.stack(out, 0))


def make_in_maps(inp):
    f = lambda a: np.ascontiguousarray(np.asarray(a, dtype=np.float32))
    x = f(inp["x"]); c = f(inp["c"]); ctx = f(inp["ctx"]); c_ctx = f(inp["c_ctx"])
    e_w_in = f(inp["e_w_in"])[0]
    p64, _ = _partner(64)
    p32, _ = _partner(32)
    h8 = np.arange(8)[:, None]
    cols = np.concatenate([
        384 + np.arange(256), 640 + np.arange(32), 640 + p32,
        1696 + np.arange(128), 1696 + (np.arange(2)[:, None] * 64 + p64[None, :]).reshape(-1),
        1824 + np.arange(128), np.arange(384), 672 + np.arange(512),
        1184 + np.arange(512), 1184 + (h8 * 64 + p64[None, :]).reshape(-1), 1952 + np.arange(512)])
    assert cols.shape[0] == NC0
    w0 = np.ascontiguousarray(e_w_in[:, cols])
    bq = f(inp["e_b_q_norm"])[0]; bk = f(inp["e_b_k_norm"])[0]
    one = lambda n: np.ones(n, np.float32)
    gv0 = np.concatenate([one(256), one(32), one(32), np.tile(bk, 2), np.tile(bk[p64], 2), one(128), one(384),
                          one(512), np.tile(bq, 8), np.tile(bq[p64], 8), one(512)]).astype(np.float32)
    w_uq = f(inp["e_a_w_uq"])[0]
    cu = np.concatenate([(h8 * 96 + np.arange(64)[None, :]).reshape(-1),
                         (h8 * 96 + 64 + np.arange(32)[None, :]).reshape(-1),
                         (h8 * 96 + 64 + p32[None, :]).reshape(-1)])
    wuq = np.ascontiguousarray(w_uq[:, cu])
    w_ukv = f(inp["e_a_w_ukv"])[0]
    ck = np.concatenate([(h8 * 128 + np.arange(64)[None, :]).reshape(-1),
                         (h8 * 128 + 64 + np.arange(64)[None, :]).reshape(-1)])
    wukv = np.ascontiguousarray(w_ukv[:, ck])
    aqc = np.ascontiguousarray(f(inp["e_a_q_norm"])[0].reshape(3, 128).T)
    akvc = np.ascontiguousarray(f(inp["e_a_kv_norm"])[0].reshape(2, 128).T)
    bqk = np.ascontiguousarray(np.stack([np.tile(bq, 2), np.tile(bk, 2)], 1))
    shared = dict(
        ident=np.eye(128, dtype=np.float32),
        wmod0=f(inp["e_w_mod"])[0], wmod1=f(inp["o_w_mod"])[0],
        bmodc0=np.ascontiguousarray(f(inp["e_b_mod"])[0].reshape(24, 128).T),
        bmodc1=np.ascontiguousarray(f(inp["o_b_mod"])[0].reshape(24, 128).T),
        bmodg0=np.ascontiguousarray(f(inp["e_b_mod"])[0][2048:]), bmodg1=np.ascontiguousarray(f(inp["o_b_mod"])[0][2048:]),
        w0=w0, gv0=gv0, wuq=wuq, aqc=aqc, wukv=wukv, akvc=akvc, bqk=bqk,
        wout0=f(inp["e_w_out"])[0], lng0=f(inp["e_ln_g"])[0], lnb0=f(inp["e_ln_b"])[0],
        w1in=f(inp["o_w_in"])[0], vlng=f(inp["o_v_ln_g"])[0], vlnb=f(inp["o_v_ln_b"])[0],
        wsT=np.ascontiguousarray(np.transpose(f(inp["o_w_s"])[0], (0, 2, 1))),
        bs=np.ascontiguousarray(f(inp["o_b_s"])[0].reshape(-1)),
        wout1=f(inp["o_w_out"])[0], lng1=f(inp["o_ln_g"])[0], lnb1=f(inp["o_ln_b"])[0],
    )
    maps = []
    for core in range(8):
        b, half = core // 2, core % 2
        order = np.concatenate([np.arange(half * SQ, (half + 1) * SQ), np.arange((1 - half) * SQ, (2 - half) * SQ)])
        m = dict(shared)
        m["x_all"] = np.ascontiguousarray(x[b][order])
        m["ctx"] = ctx[b]
        m["cc"] = np.ascontiguousarray(np.stack([c[b].reshape(8, 128).T, c_ctx.reshape(8, 128).T], 2))
        m["tabs"] = _rope_tables(order)
        maps.append(m)
    return maps


_CACHE = {}


def kernel(**inputs):
    if "nc" not in _CACHE:
        _CACHE["nc"] = build()
    nc = _CACHE["nc"]
    maps = make_in_maps(inputs)
    res = run_bass_kernel_spmd(nc, maps, core_ids=list(range(8)))
    out = np.empty((4, S, D), np.float32)
    for core in range(8):
        b, half = core // 2, core % 2
        out[b, half * SQ:(half + 1) * SQ] = res.results[core]["out"]
    return out
```
